# Optimizing a Trainium2 kernel written in Bass

```python
import jax, jax.numpy as jnp
from jax import lax
import numpy as np

D_MODEL = 1024
BATCH = 16
SEQ = 4096
DEPTH = 2
DEC_BATCH = 16
DEC_SEQ = 32
PAST_LEN = 1024

CHUNK = 64
HEAD_DIM = 64
H_MLSTM = 4
H_FOX = 8
G_GMLP = 4
D_MLSTM = H_MLSTM * HEAD_DIM
D_FOX = H_FOX * HEAD_DIM
D_GMLP = G_GMLP * HEAD_DIM
D_MIX = D_MLSTM + D_FOX + D_GMLP
MLP_CHUNK = 128
Q_BLOCK = 128
D_FF = ((8 * D_MODEL + 3 * 256 - 1) // (3 * 256)) * 256
IN_SIZES = (D_MLSTM, D_MLSTM, D_MLSTM, D_MLSTM, H_MLSTM, H_MLSTM, D_FOX, D_FOX, D_FOX, H_FOX, D_GMLP, D_GMLP)
N_IN = 4 * D_MLSTM + 2 * H_MLSTM + 3 * D_FOX + H_FOX + 2 * D_GMLP
ALPHA = (2 * DEPTH) ** 0.25
BETA = (8 * DEPTH) ** -0.25
LN_EPS = 1e-5
HN_EPS = 1e-6
FOX_SCALE = HEAD_DIM ** -0.5

kernel_name = "hybrid_mlstm_fox_gmlp_stream_step"


def _layer_norm(x, g, b):
    xf = x.astype(jnp.float32)
    mu = xf.mean(-1, keepdims=True)
    var = jnp.square(xf - mu).mean(-1, keepdims=True)
    return ((xf - mu) * lax.rsqrt(var + LN_EPS) * g + b).astype(x.dtype)


def _head_norm(h):
    mu = h.mean(-1, keepdims=True)
    var = jnp.square(h - mu).mean(-1, keepdims=True)
    return (h - mu) * lax.rsqrt(var + HN_EPS)


def _mlstm_chunk(carry, inp):
    C0, n0, m0 = carry
    q, k, v, li, lf = inp
    L = q.shape[1]
    bT = jnp.cumsum(lf, axis=1).transpose(0, 2, 1)
    liT = li.transpose(0, 2, 1)
    causal = jnp.tril(jnp.ones((L, L), bool))
    dmat = jnp.where(causal, bT[..., :, None] - bT[..., None, :] + liT[..., None, :], -jnp.inf)
    inter = m0[..., None] + bT
    m = jnp.maximum(inter, dmat.max(-1))
    w_intra = jnp.exp(dmat - m[..., None])
    w_inter = jnp.exp(inter - m)
    s = jnp.einsum('bthd,bshd->bhts', q, k) * w_intra
    num = (jnp.einsum('bhts,bshd->bthd', s, v)
           + jnp.einsum('bthk,bhkv->bthv', q, C0) * w_inter.transpose(0, 2, 1)[..., None])
    den = s.sum(-1) + jnp.einsum('bthk,bhk->bht', q, n0) * w_inter
    den = jnp.maximum(jnp.abs(den), jnp.exp(-m))
    h = num / den.transpose(0, 2, 1)[..., None]
    m_end = m[..., -1]
    w_end = jnp.exp(dmat[..., -1, :] - m_end[..., None])
    d0 = jnp.exp(inter[..., -1] - m_end)
    C1 = d0[..., None, None] * C0 + jnp.einsum('bhs,bshk,bshv->bhkv', w_end, k, v)
    n1 = d0[..., None] * n0 + jnp.einsum('bhs,bshk->bhk', w_end, k)
    return (C1, n1, m_end), h


def _mlstm(q, k, v, li, lf, state0, blk):
    B, S = q.shape[:2]
    nc = S // blk

    def to_blocks(a):
        return a.reshape(B, nc, blk, *a.shape[2:]).swapaxes(0, 1)

    final, h = lax.scan(_mlstm_chunk, state0,
                        (to_blocks(q), to_blocks(k), to_blocks(v), to_blocks(li), to_blocks(lf)))
    return h.swapaxes(0, 1).reshape(B, S, H_MLSTM, HEAD_DIM), final


def _fox_prompt(q, k, v, logf):
    B, S = q.shape[:2]
    nb = S // Q_BLOCK
    Ft = jnp.cumsum(logf, axis=1).transpose(0, 2, 1)
    qb = q.reshape(B, nb, Q_BLOCK, H_FOX, HEAD_DIM).swapaxes(0, 1)
    Fb = Ft.reshape(B, H_FOX, nb, Q_BLOCK).transpose(2, 0, 1, 3)
    pos_k = jnp.arange(S)

    def block(args):
        i, qi, Fi = args
        pos_q = i * Q_BLOCK + jnp.arange(Q_BLOCK)
        logits = (jnp.einsum('bthd,bshd->bhts', qi, k) * FOX_SCALE
                  + Fi[..., None] - Ft[:, :, None, :])
        logits = jnp.where(pos_k[None, :] <= pos_q[:, None], logits, -jnp.inf)
        p = jax.nn.softmax(logits, axis=-1)
        return jnp.einsum('bhts,bshd->bthd', p, v)

    out = lax.map(block, (jnp.arange(nb), qb, Fb))
    return out.swapaxes(0, 1).reshape(B, S, H_FOX, HEAD_DIM)


def _fox_sample(q, k, v, logf, k_c, v_c, logf_c):
    P, T = k_c.shape[1], q.shape[1]
    k_all = jnp.concatenate([k_c, k], axis=1)
    v_all = jnp.concatenate([v_c, v], axis=1)
    Ft = jnp.cumsum(jnp.concatenate([logf_c, logf], axis=1), axis=1).transpose(0, 2, 1)
    logits = (jnp.einsum('bthd,bshd->bhts', q, k_all) * FOX_SCALE
              + Ft[:, :, P:, None] - Ft[:, :, None, :])
    mask = jnp.arange(P + T)[None, :] <= (P + jnp.arange(T))[:, None]
    p = jax.nn.softmax(jnp.where(mask, logits, -jnp.inf), axis=-1)
    return jnp.einsum('bhts,bshd->bthd', p, v_all)


def _spatial_gate(u, vn, ws, bs):
    B, S, _ = u.shape
    L = min(S, MLP_CHUNK)
    w = jnp.where(jnp.tril(jnp.ones((L, L), bool)), ws[:, :L, :L], 0.0)
    vc = vn.reshape(B, S // L, L, G_GMLP, HEAD_DIM)
    z = jnp.einsum('gts,bnsgc->bntgc', w, vc) + bs[:, :L].T[None, None, :, :, None]
    return u * z.reshape(B, S, D_GMLP).astype(u.dtype)


def _layer(x, c, p, cache):
    (w_ada, b_ada, w_in, b_mlstm_i, b_mlstm_f, mlstm_norm_g, b_fox_f, gmlp_ln_g, gmlp_ln_b,
     gmlp_ws, gmlp_bs, w_o, ln1_g, ln1_b, w_gate, w_up, w_down, ln2_g, ln2_b) = p
    B, S, _ = x.shape
    dt = x.dtype
    f32 = jnp.float32
    mod = jax.nn.silu(c) @ w_ada + b_ada
    sh1, sc1, g1, sh2, sc2, g2 = jnp.split(mod[:, None, :], 6, axis=-1)
    h = x * (1 + sc1) + sh1
    split_at = np.cumsum(IN_SIZES)[:-1].tolist()
    mq, mk, mv, mo, mi, mf, fq, fk, fv, ff, gu, gv = jnp.split(h @ w_in, split_at, axis=-1)

    def heads(a, n):
        return a.reshape(B, S, n, HEAD_DIM).astype(f32)

    li = mi.astype(f32) + b_mlstm_i
    lf = jax.nn.log_sigmoid(mf.astype(f32) + b_mlstm_f)
    if cache is None:
        st0 = (jnp.zeros((B, H_MLSTM, HEAD_DIM, HEAD_DIM), f32),
               jnp.zeros((B, H_MLSTM, HEAD_DIM), f32),
               jnp.full((B, H_MLSTM), -jnp.inf, f32))
        blk = CHUNK
    else:
        st0 = (cache[3].astype(f32), cache[4].astype(f32), cache[5].astype(f32))
        blk = S
    hm, (C1, n1, m1) = _mlstm(heads(mq, H_MLSTM), heads(mk, H_MLSTM) * HEAD_DIM ** -0.5,
                              heads(mv, H_MLSTM), li, lf, st0, blk)
    hm = (_head_norm(hm) * mlstm_norm_g.reshape(H_MLSTM, HEAD_DIM)
          * jax.nn.sigmoid(heads(mo, H_MLSTM)))
    fk_h, fv_h = heads(fk, H_FOX), heads(fv, H_FOX)
    logf = jax.nn.log_sigmoid(ff.astype(f32) + b_fox_f)
    if cache is None:
        hf = _fox_prompt(heads(fq, H_FOX), fk_h, fv_h, logf)
    else:
        hf = _fox_sample(heads(fq, H_FOX), fk_h, fv_h, logf,
                         cache[0].astype(f32), cache[1].astype(f32), cache[2].astype(f32))
    vn = _layer_norm(gv, gmlp_ln_g, gmlp_ln_b)
    hg = _spatial_gate(gu, vn, gmlp_ws, gmlp_bs)
    mix = jnp.concatenate([hm.reshape(B, S, D_MLSTM).astype(dt),
                           hf.reshape(B, S, D_FOX).astype(dt), hg], axis=-1) @ w_o
    x = _layer_norm(ALPHA * x + (1 + g1) * mix, ln1_g, ln1_b)
    h = x * (1 + sc2) + sh2
    y = (jax.nn.silu(h @ w_gate) * (h @ w_up)) @ w_down
    x = _layer_norm(ALPHA * x + (1 + g2) * y, ln2_g, ln2_b)
    new = (fk_h.astype(dt), fv_h.astype(dt), logf.astype(dt),
           C1.astype(dt), n1.astype(dt), m1.astype(dt), vn)
    return x, new


def setup_inputs(seed: int = 0) -> dict:
    key = jax.random.key(seed)
    ks = jax.random.split(key, 40)
    cnt = [0]

    def nk():
        cnt[0] += 1
        return ks[cnt[0] - 1]

    def nrm(shape, s=1.0):
        return s * jax.random.normal(nk(), shape, jnp.float32)

    def uni(shape, lo, hi):
        return jax.random.uniform(nk(), shape, jnp.float32, lo, hi)

    return {
        'x_prompt': nrm((BATCH, SEQ, D_MODEL)),
        'x_sample': nrm((DEC_BATCH, DEC_SEQ, D_MODEL)),
        'cache_fox_k': nrm((DEPTH, DEC_BATCH, PAST_LEN, H_FOX, HEAD_DIM)),
        'cache_fox_v': nrm((DEPTH, DEC_BATCH, PAST_LEN, H_FOX, HEAD_DIM)),
        'cache_fox_logf': jax.nn.log_sigmoid(3.5 + nrm((DEPTH, DEC_BATCH, PAST_LEN, H_FOX))),
        'state_mlstm_C': nrm((DEPTH, DEC_BATCH, H_MLSTM, HEAD_DIM, HEAD_DIM), 0.5),
        'state_mlstm_n': nrm((DEPTH, DEC_BATCH, H_MLSTM, HEAD_DIM), 0.5),
        'state_mlstm_m': nrm((DEPTH, DEC_BATCH, H_MLSTM)),
        'c_prompt': nrm((BATCH, D_MODEL)),
        'c_sample': nrm((DEC_BATCH, D_MODEL)),
        'w_ada': nrm((DEPTH, D_MODEL, 6 * D_MODEL), 0.5 * D_MODEL ** -0.5),
        'b_ada': nrm((DEPTH, 6 * D_MODEL), 0.05),
        'w_in': nrm((DEPTH, D_MODEL, N_IN), D_MODEL ** -0.5),
        'b_mlstm_i': nrm((DEPTH, H_MLSTM), 0.1),
        'b_mlstm_f': uni((DEPTH, H_MLSTM), 3.0, 6.0),
        'mlstm_norm_g': 1.0 + nrm((DEPTH, D_MLSTM), 0.05),
        'b_fox_f': uni((DEPTH, H_FOX), 2.0, 5.0),
        'gmlp_ln_g': 1.0 + nrm((DEPTH, D_GMLP), 0.05),
        'gmlp_ln_b': nrm((DEPTH, D_GMLP), 0.05),
        'gmlp_ws': nrm((DEPTH, G_GMLP, MLP_CHUNK, MLP_CHUNK), MLP_CHUNK ** -0.5),
        'gmlp_bs': 1.0 + nrm((DEPTH, G_GMLP, MLP_CHUNK), 0.1),
        'w_o': nrm((DEPTH, D_MIX, D_MODEL), BETA * D_MIX ** -0.5),
        'ln1_g': 1.0 + nrm((DEPTH, D_MODEL), 0.05),
        'ln1_b': nrm((DEPTH, D_MODEL), 0.05),
        'w_gate': nrm((DEPTH, D_MODEL, D_FF), D_MODEL ** -0.5),
        'w_up': nrm((DEPTH, D_MODEL, D_FF), D_MODEL ** -0.5),
        'w_down': nrm((DEPTH, D_FF, D_MODEL), BETA * D_FF ** -0.5),
        'ln2_g': 1.0 + nrm((DEPTH, D_MODEL), 0.05),
        'ln2_b': nrm((DEPTH, D_MODEL), 0.05),
    }


def reference(x_prompt, x_sample, cache_fox_k, cache_fox_v, cache_fox_logf, state_mlstm_C, state_mlstm_n,
              state_mlstm_m, c_prompt, c_sample, w_ada, b_ada, w_in, b_mlstm_i, b_mlstm_f, mlstm_norm_g,
              b_fox_f, gmlp_ln_g, gmlp_ln_b, gmlp_ws, gmlp_bs, w_o, ln1_g, ln1_b, w_gate, w_up, w_down,
              ln2_g, ln2_b):
    xp, xs = x_prompt, x_sample
    new_p, new_s = [], []
    for l in range(DEPTH):
        p = (w_ada[l], b_ada[l], w_in[l], b_mlstm_i[l], b_mlstm_f[l], mlstm_norm_g[l], b_fox_f[l],
             gmlp_ln_g[l], gmlp_ln_b[l], gmlp_ws[l], gmlp_bs[l], w_o[l], ln1_g[l], ln1_b[l],
             w_gate[l], w_up[l], w_down[l], ln2_g[l], ln2_b[l])
        xp, st_p = _layer(xp, c_prompt, p, None)
        xs, st_s = _layer(xs, c_sample, p, (cache_fox_k[l], cache_fox_v[l], cache_fox_logf[l],
                                             state_mlstm_C[l], state_mlstm_n[l], state_mlstm_m[l]))
        new_p.append(st_p)
        new_s.append(st_s)

    def stk(lst, i):
        return jnp.stack([e[i] for e in lst], axis=0)

    return (xp, xs,
            stk(new_p, 0), stk(new_p, 1), stk(new_p, 2), stk(new_p, 3), stk(new_p, 4), stk(new_p, 5),
            stk(new_s, 0), stk(new_s, 1), stk(new_s, 2), stk(new_s, 3), stk(new_s, 4), stk(new_s, 5),
            stk(new_s, 6))
```

```python
import numpy as np
import ml_dtypes
from contextlib import ExitStack
import concourse.bass as bass
import concourse.mybir as mybir
from concourse.bass_utils import run_bass_kernel_spmd

F32 = mybir.dt.float32
BF16 = mybir.dt.bfloat16
AF = mybir.ActivationFunctionType
ALU = mybir.AluOpType
AX = mybir.AxisListType

NL = 2
D = 1024
KC = 8
DFF = 2816
FC = 22
PAST = 1024
DEC = 32
ALPHA = (2 * NL) ** 0.25
EPS_LN = 1e-5 / (ALPHA * ALPHA)
EPS_G = 1e-5
EPS_HN = 1e-6
NEG = -30000.0

W1_OFF = 0
W2_OFF = W1_OFF + 15 * 1024
W2_SIZES = [4096, 2048, 4096, 4096]
WO_OFF = W2_OFF + sum(W2_SIZES)
GU_OFF = WO_OFF + 8 * 1536
WD_OFF = GU_OFF + 11 * 4096
FTOT = WD_OFF + 8 * 2816
CONV = 4096
FTOT_PAD = ((FTOT + CONV - 1) // CONV) * CONV


def weight_pieces():
    pcs = []
    for i in range(0, 15, 4):
        n = min(4, 15 - i)
        pcs.append((0, W1_OFF + i * 1024, n * 1024))
        if i == 0:
            pcs.append((1, 0, 4096))
    w2o = [W2_OFF + sum(W2_SIZES[:i]) for i in range(4)]
    for i in (1, 2, 3, 0):
        pcs.append((0, w2o[i], W2_SIZES[i]))
    for i in range(0, 8, 2):
        pcs.append((0, WO_OFF + i * 1536, 2 * 1536))
    for i in range(11):
        pcs.append((0, GU_OFF + i * 4096, 4096))
    for i in range(8):
        pcs.append((0, WD_OFF + i * 2816, 2816))
    return pcs


PI_W1 = [0, 2, 3, 4]
PI_W1LO = 1
PI_W2 = 5
PI_WO = 9
PI_GU = 13
PI_WD = 24


C_ID = 0
C_M01 = 128
C_SEL = 256
C_VAL = 272
C_MNEG = 400
NCON = C_MNEG + 4 * 512

LP_BADA = 0
LP_LN = 48
LP_BFF = 80
LP_BI = 81
LP_BF = 82
LP_BS = 83
LP_GM = 87
LP_GG = 343
LP_GB = 599
LP_WS = 855
NLP = LP_WS + 512


class Sem:
    def __init__(self, nc, es, name):
        self.sem = es.enter_context(nc.semaphore(name))
        self.n = 0


class Eng(Sem):
    def __init__(self, nc, es, name, h):
        super().__init__(nc, es, "e_" + name)
        self.h = h
        self.name = name
        self.seen = {}


class Buf:
    __slots__ = ("name", "w", "r", "dsem", "excl")

    def __init__(self, name):
        self.name = name
        self.w = None
        self.r = {}
        self.dsem = None
        self.excl = False


class KB:
    def __init__(self, nc, es):
        self.nc = nc
        self.es = es
        self.PE = Eng(nc, es, "pe", nc.tensor)
        self.ACT = Eng(nc, es, "act", nc.scalar)
        self.DVE = Eng(nc, es, "dve", nc.vector)
        self.POOL = Eng(nc, es, "pool", nc.gpsimd)
        self.SP = Eng(nc, es, "sp", nc.sync)
        self.engs = [self.PE, self.ACT, self.DVE, self.POOL, self.SP]
        self.dsems = []
        self.nsem = 5

    def _wait(self, eng, dep):
        so, v = dep
        if eng.seen.get(id(so), 0) >= v:
            return
        eng.h.wait_ge(so.sem, v)
        eng.seen[id(so)] = v

    def _deps(self, eng, r, w):
        for b in r:
            if b.w is not None:
                if not (b.w[0] is eng and eng is self.PE):
                    self._wait(eng, b.w)
            if b.excl:
                for d in b.r.values():
                    if d[0] is not eng:
                        self._wait(eng, d)
        for b in w:
            if b.w is not None:
                if not (b.w[0] is eng and eng is self.PE):
                    self._wait(eng, b.w)
            for d in b.r.values():
                if not (d[0] is eng and eng is self.PE):
                    self._wait(eng, d)

    def op(self, eng, ins_fn, r=(), w=()):
        self._deps(eng, r, w)
        ins = ins_fn(eng.h)
        eng.n += 1
        ins.then_inc(eng.sem, 1)
        tok = (eng, eng.n)
        eng.seen[id(eng)] = eng.n if eng is self.PE else eng.seen.get(id(eng), 0)
        for b in w:
            b.w = tok
            b.r = {}
        for b in r:
            b.r[id(eng)] = tok

    def dma(self, eng, out_ap, in_ap, r=(), w=(), prim=None):
        self._deps(eng, r, w)
        if prim.dsem is None:
            prim.dsem = Sem(self.nc, self.es, "d%d" % self.nsem)
            self.nsem += 1
            self.dsems.append(prim.dsem)
        so = prim.dsem
        ins = eng.h.dma_start(out=out_ap, in_=in_ap)
        so.n += 16
        ins.then_inc(so.sem, 16)
        tok = (so, so.n)
        for b in w:
            b.w = tok
            b.r = {}
        for b in r:
            b.r[id(so)] = tok

    def barrier(self):
        for e in self.engs:
            for o in self.engs:
                if o is not e and o.n > 0:
                    self._wait(e, (o, o.n))
            for so in self.dsems:
                if so.n > 0:
                    self._wait(e, (so, so.n))

    def finish(self):
        e = self.SP
        for so in self.dsems:
            if so.n > 0:
                self._wait(e, (so, so.n))
        for o in self.engs:
            if o is not e and o.n > 0:
                self._wait(e, (o, o.n))


import os
_STOP = float(os.environ.get("KSTOP", "999"))


class StopEmit(Exception):
    pass


def stop_at(n):
    if _STOP <= n:
        raise StopEmit()


class Pool:
    def __init__(self, cells):
        self.free = list(cells)

    def get(self):
        assert self.free, "pool exhausted"
        return self.free.pop(0)

    def put(self, c):
        self.free.append(c)


class T:
    def __init__(self, ap_tensor, name):
        self.t = ap_tensor
        self.b = Buf(name)


def build(S, NPS, NSS):
    NSEQ = NPS + NSS
    SC = max(S, PAST + 128)
    NB = SC // 128
    nc = bass.Bass("TRN2", target_bir_lowering=False)

    def din(name, shape, dt=F32):
        return nc.dram_tensor(name, list(shape), dt, kind="ExternalInput").ap()

    def dout(name, shape, dt=F32):
        return nc.dram_tensor(name, list(shape), dt, kind="ExternalOutput").ap()

    def dscr(name, shape, dt=F32):
        return nc.dram_tensor(name, list(shape), dt, kind="Internal").ap()

    xp = din("xp", [NPS, 8, 128, S])
    xs = din("xs", [NSS, 8, 128, 128])
    cT = din("cT", [128, 8 * NSEQ])
    ck = din("ck", [NL, NSS, 4, 128, PAST])
    cv = din("cv", [NL, NSS, PAST, 512])
    clf = din("clf", [NL, NSS, 128, PAST])
    sC = din("sC", [NL, NSS, 2, 128, 65])
    sM = din("sM", [NL, NSS, 128, 1])
    consts = din("consts", [128, NCON])
    lpar = din("lpar", [NL, 128, NLP])
    wall = din("wall", [NL, 128, FTOT_PAD])
    wada = din("wada", [NL, 12, 128, 4096])

    yp = dout("yp", [NPS, 8, 128, S])
    ys = dout("ys", [NSS, 8, 128, 128])
    pk = dout("pk", [NL, NPS, 4, 128, S])
    pv = dout("pv", [NL, NPS, S, 512])
    plf = dout("plf", [NL, NPS, 128, S])
    pC = dout("pC", [NL, NPS, 2, 128, 65])
    pM = dout("pM", [NL, NPS, 128, 1])
    sk = dout("sk", [NL, NSS, 4, 128, 128])
    sv = dout("sv", [NL, NSS, 128, 512])
    slf = dout("slf", [NL, NSS, 128, 128])
    sCo = dout("sCo", [NL, NSS, 2, 128, 65])
    sMo = dout("sMo", [NL, NSS, 128, 1])
    sgv = dout("sgv", [NL, NSS, 128, 256])

    wsc = dscr("wsc", [NL, 128, FTOT_PAD], BF16)
    wlo = dscr("wlo", [NL, 128, CONV], BF16)
    xmp = dscr("xmp", [NPS, 8, 128, S])
    xms = dscr("xms", [NSS, 8, 128, 128])

    es = ExitStack()
    K = KB(nc, es)
    PE, ACT, DVE, POOL, SP = K.PE, K.ACT, K.DVE, K.POOL, K.SP
    STQ = SP if os.environ.get('KSTQ', 'sp') == 'sp' else POOL
    cnt = [0]

    def sb(shape, dt=F32, name=None, stack=None):
        cnt[0] += 1
        nm = "%s_%d" % (name or "t", cnt[0])
        t = (stack or es).enter_context(nc.sbuf_tensor(nm, list(shape), dt))
        return T(t, nm)

    identF = sb([128, 128], F32, "identF")
    identB = sb([128, 128], BF16, "identB")
    mask01 = sb([128, 128], F32, "mask01")
    validF = sb([128, 128], F32, "validF")
    selc = sb([128, 16], F32, "selc")
    mnegB = sb([128, 4 * 512], BF16, "mnegB")
    onesM = sb([128, 128], BF16, "onesM")
    onesF = sb([128, 128], F32, "onesF")
    ones5 = sb([128, 512], F32, "ones5")
    lp = [sb([128, LP_WS], F32, "lp%d" % l) for l in range(NL)]
    wsTb = [sb([128, 512], BF16, "wsTb%d" % l) for l in range(NL)]
    modT = [sb([128, 48, NSEQ], F32, "modT%d" % l) for l in range(NL)]
    dsc1 = [sb([128, 8, NSEQ], F32, "dsc1") for l in range(NL)]
    dg1 = [sb([128, 8, NSEQ], F32, "dg1") for l in range(NL)]
    dA2 = [sb([128, 8, NSEQ], F32, "dA2") for l in range(NL)]
    dB2 = [sb([128, 8, NSEQ], F32, "dB2") for l in range(NL)]
    dg2 = [sb([128, 8, NSEQ], F32, "dg2") for l in range(NL)]
    negb = [sb([128, 2], F32, "negb") for l in range(NL)]

    psum = []
    for i in range(8):
        t = es.enter_context(nc.psum_tensor("ps%d" % i, [128, 512], F32))
        psum.append(T(t, "ps%d" % i))
        psum[-1].b.excl = True
    ps_acc = Pool(psum[0:2])
    ps_st = Pool(psum[2:5])
    ps_gen = Pool(psum[5:8])

    def rot(pool):
        c = pool.get()
        pool.put(c)
        return c

    with ExitStack() as s0:
        cf = sb([128, NCON], F32, "cf", s0)
        K.dma(SP, cf.t[:, :], consts[:, :], w=[cf.b], prim=cf.b)
        for l in range(NL):
            K.dma(SP, lp[l].t[:, :], lpar[l, :, 0:LP_WS], w=[lp[l].b], prim=lp[l].b)
        wsF = [sb([128, 512], F32, "wsF", s0) for l in range(NL)]
        for l in range(NL):
            K.dma(SP, wsF[l].t[:, :], lpar[l, :, LP_WS:LP_WS + 512], w=[wsF[l].b], prim=wsF[l].b)
        ctile = sb([128, 8 * NSEQ], F32, "ctile", s0)
        K.dma(SP, ctile.t[:, :], cT[:, :], w=[ctile.b], prim=ctile.b)

        K.op(DVE, lambda e: e.tensor_copy(out=identF.t[:, :], in_=cf.t[:, C_ID:C_ID + 128]), r=[cf.b], w=[identF.b])
        K.op(DVE, lambda e: e.tensor_copy(out=identB.t[:, :], in_=cf.t[:, C_ID:C_ID + 128]), r=[cf.b], w=[identB.b])
        K.op(DVE, lambda e: e.tensor_copy(out=mask01.t[:, :], in_=cf.t[:, C_M01:C_M01 + 128]), r=[cf.b], w=[mask01.b])
        K.op(DVE, lambda e: e.tensor_copy(out=validF.t[:, :], in_=cf.t[:, C_VAL:C_VAL + 128]), r=[cf.b], w=[validF.b])
        K.op(DVE, lambda e: e.tensor_copy(out=selc.t[:, :], in_=cf.t[:, C_SEL:C_SEL + 16]), r=[cf.b], w=[selc.b])
        K.op(DVE, lambda e: e.tensor_copy(out=mnegB.t[:, :], in_=cf.t[:, C_MNEG:C_MNEG + 2048]), r=[cf.b], w=[mnegB.b])
        K.op(POOL, lambda e: e.memset(onesM.t[:, :], 1.0 / 1024.0), w=[onesM.b])
        K.op(POOL, lambda e: e.memset(onesF.t[:, :], 1.0), w=[onesF.b])
        K.op(POOL, lambda e: e.memset(ones5.t[:, :], 1.0), w=[ones5.b])
        for l in range(NL):
            K.op(DVE, lambda e, l=l: e.tensor_tensor(
                out=wsTb[l].t[:, :].rearrange("p (g t) -> p g t", g=4),
                in0=wsF[l].t[:, :].rearrange("p (g t) -> p g t", g=4),
                in1=cf.t[:, C_M01:C_M01 + 128].rearrange("p (o t) -> p o t", o=1).to_broadcast([128, 4, 128]),
                op=ALU.mult), r=[wsF[l].b, cf.b], w=[wsTb[l].b])
            K.op(DVE, lambda e, l=l: e.tensor_scalar(out=negb[l].t[:, 0:1], in0=lp[l].t[:, LP_BFF:LP_BFF + 1],
                                                     scalar1=-1.0, scalar2=None, op0=ALU.mult),
                 r=[lp[l].b], w=[negb[l].b])
            K.op(DVE, lambda e, l=l: e.tensor_scalar(out=negb[l].t[:, 1:2], in0=lp[l].t[:, LP_BF:LP_BF + 1],
                                                     scalar1=-1.0, scalar2=None, op0=ALU.mult),
                 r=[lp[l].b], w=[negb[l].b])

        scT = sb([128, 8 * NSEQ], F32, "scT", s0)
        K.op(ACT, lambda e: e.activation(out=scT.t[:, :], in_=ctile.t[:, :], func=AF.Silu), r=[ctile.b], w=[scT.b])
        wa = [sb([128, 4096], F32, "wa", s0) for i in range(2)]
        for l in range(NL):
            pm = rot(ps_gen)
            for j in range(12):
                wt = wa[(l * 12 + j) % 2]
                K.dma(SP, wt.t[:, :], wada[l, j, :, :], w=[wt.b], prim=wt.b)
                for q in range(4):
                    oc = 4 * j + q
                    for kc in range(8):
                        K.op(PE, lambda e, wt=wt, q=q, kc=kc, oc=oc, pm=pm: e.matmul(
                            pm.t[:, oc * NSEQ:(oc + 1) * NSEQ],
                            lhsT=wt.t[:, kc * 512 + q * 128: kc * 512 + (q + 1) * 128],
                            rhs=scT.t[:, kc * NSEQ:(kc + 1) * NSEQ],
                            start=(kc == 0), stop=(kc == 7)), r=[wt.b, scT.b], w=[pm.b])
            K.op(DVE, lambda e, l=l, pm=pm: e.tensor_tensor(
                out=modT[l].t[:, :, :],
                in0=pm.t[:, 0:48 * NSEQ].rearrange("p (c s) -> p c s", s=NSEQ),
                in1=lp[l].t[:, LP_BADA:LP_BADA + 48].rearrange("p (c o) -> p c o", o=1).to_broadcast([128, 48, NSEQ]),
                op=ALU.add), r=[pm.b, lp[l].b], w=[modT[l].b])
            m = modT[l].t
            lng = lambda off: lp[l].t[:, LP_LN + off:LP_LN + off + 8].rearrange("p (c o) -> p c o", o=1).to_broadcast([128, 8, NSEQ])
            K.op(DVE, lambda e, l=l, m=m: e.tensor_scalar(out=dsc1[l].t[:, :, :], in0=m[:, 8:16, :], scalar1=1.0, scalar2=None, op0=ALU.add),
                 r=[modT[l].b], w=[dsc1[l].b])
            K.op(DVE, lambda e, l=l, m=m: e.tensor_scalar(out=dg1[l].t[:, :, :], in0=m[:, 16:24, :], scalar1=1.0, scalar2=1.0 / ALPHA, op0=ALU.add, op1=ALU.mult),
                 r=[modT[l].b], w=[dg1[l].b])
            K.op(DVE, lambda e, l=l, m=m: e.tensor_scalar(out=dg2[l].t[:, :, :], in0=m[:, 40:48, :], scalar1=1.0, scalar2=1.0 / ALPHA, op0=ALU.add, op1=ALU.mult),
                 r=[modT[l].b], w=[dg2[l].b])
            K.op(DVE, lambda e, l=l, m=m: e.tensor_scalar(out=dA2[l].t[:, :, :], in0=m[:, 32:40, :], scalar1=1.0, scalar2=None, op0=ALU.add),
                 r=[modT[l].b], w=[dA2[l].b])
            K.op(DVE, lambda e, l=l: e.tensor_tensor(out=dB2[l].t[:, :, :], in0=dA2[l].t[:, :, :], in1=lng(8), op=ALU.mult),
                 r=[dA2[l].b, lp[l].b], w=[dB2[l].b])
            K.op(DVE, lambda e, l=l, m=m: e.tensor_tensor(out=dB2[l].t[:, :, :], in0=dB2[l].t[:, :, :], in1=m[:, 24:32, :], op=ALU.add),
                 r=[dB2[l].b, modT[l].b], w=[dB2[l].b])
            K.op(DVE, lambda e, l=l: e.tensor_tensor(out=dA2[l].t[:, :, :], in0=dA2[l].t[:, :, :], in1=lng(0), op=ALU.mult),
                 r=[dA2[l].b, lp[l].b], w=[dA2[l].b])

        cin = [sb([128, CONV], F32, "cin", s0) for i in range(4)]
        cout = [sb([128, CONV], BF16, "cout", s0) for i in range(4)]
        clo = sb([128, CONV], BF16, "clo", s0)
        wscb = [Buf("wsc%d" % l) for l in range(NL)]
        nconv = FTOT_PAD // CONV
        ci = 0
        cengs = [ACT, DVE, POOL]
        for l in range(NL):
            for j in range(nconv):
                a = cin[ci % 4]
                o = cout[ci % 4]
                K.dma(SP, a.t[:, :], wall[l, :, j * CONV:(j + 1) * CONV], w=[a.b], prim=a.b)
                eng = ACT if ci % 2 == 0 else DVE
                if eng is ACT:
                    K.op(ACT, lambda e, a=a, o=o: e.activation(out=o.t[:, :], in_=a.t[:, :], func=AF.Copy), r=[a.b], w=[o.b])
                else:
                    K.op(eng, lambda e, a=a, o=o: e.tensor_copy(out=o.t[:, :], in_=a.t[:, :]), r=[a.b], w=[o.b])
                K.dma(ACT, wsc[l, :, j * CONV:(j + 1) * CONV], o.t[:, :], r=[o.b], w=[], prim=o.b)
                if j == 0:
                    K.op(DVE, lambda e, a=a, o=o: e.tensor_tensor(out=clo.t[:, :], in0=a.t[:, :], in1=o.t[:, :], op=ALU.subtract), r=[a.b, o.b], w=[clo.b])
                    K.dma(ACT, wlo[l, :, :], clo.t[:, :], r=[clo.b], w=[], prim=clo.b)
                wscb[l].w = o.b.r[id(o.b.dsem)]
                ci += 1
        K.barrier()

    if _STOP <= 0:
        K.finish()
        es.close()
        return nc
    kTc = [sb([128, SC], BF16, "kTc%d" % i) for i in range(4)]
    KBt = sb([128, SC], BF16, "KBt")
    Vc = sb([128, NB, 8, 65], BF16, "Vc")
    kTb = [[Buf("kT%d_%d" % (i, j)) for j in range(NB)] for i in range(4)]
    KBb = [Buf("KB%d" % j) for j in range(NB)]
    Vb = [Buf("V%d" % j) for j in range(NB)]
    Ctil = [sb([128, 65], F32, "Ctil") for i in range(2)]
    Cp = [sb([128, 65], F32, "Cp") for i in range(2)]
    Cb = [sb([128, 65], BF16, "Cb") for i in range(2)]
    carF = sb([128, 1], F32, "carF")
    carB = sb([128, 1], F32, "carB")
    carG = sb([128, 1], F32, "carG")
    xt = sb([128, 8, 512], F32, "xt")
    xb = [Buf("x%d" % i) for i in range(8)]
    arena = sb([128, 22, 512], BF16, "arena")
    ab = [Buf("ar%d" % i) for i in range(22)]
    h8 = sb([128, 8, 512], BF16, "h8")
    h8b = [Buf("h8_%d" % i) for i in range(8)]
    mix4 = sb([128, 4, 512], BF16, "mix4")
    mix4b = [Buf("mx%d" % i) for i in range(4)]
    fcells = Pool([sb([128, 512], F32, "fc") for i in range(6)])
    kTml = [sb([128, 512], BF16, "kTml") for i in range(2)]
    bcells = Pool([sb([128, 512], BF16, "bc") for i in range(6)])
    wslot = [sb([128, 4096], BF16, "wslot") for i in range(3)]
    E1t = sb([128, 128], F32, "E1t")
    CLt = sb([128, 128], F32, "CLt")
    lamc = sb([128, 1], F32, "lamc")
    Dl = sb([128, 4], F32, "Dl")
    lamB = sb([128, 4], F32, "lamB")
    tokS = [sb([128, 8], F32, "tokS") for i in range(4)]
    ktok = [sb([128, 256], BF16, "ktok") for i in range(4)]
    vaug = [sb([128, 4, 65], BF16, "vaug") for i in range(4)]
    sgt = [sb([128, 256], BF16, "sgt") for i in range(4)]
    ut = [sb([128, 256], BF16, "ut") for i in range(4)]
    vnt = sb([128, 256], F32, "vnt")
    vnb = [sb([128, 256], BF16, "vnb") for i in range(4)]
    hh = sb([128, 256], F32, "hh")
    sq = sb([128, 256], F32, "sq")
    hmb = sb([128, 256], BF16, "hmb")
    hgb = sb([128, 256], BF16, "hgb")
    st6 = sb([128, 6], F32, "st6")
    st2 = sb([128, 2], F32, "st2")
    sm4 = [sb([128, 4], F32, "sm4_%d" % i) for i in range(8)]
    mout = sb([128, 1], F32, "mout")
    gprev = sb([128, 1], F32, "gprev")
    mhalf = sb([128, 4], F32, "mhalf")
    amt = sb([128, 128], F32, "amt")

    for tt in (E1t, CLt):
        K.op(POOL, lambda e, tt=tt: e.memset(tt.t[:, :], 0.0), w=[tt.b])
    K.op(POOL, lambda e: e.memset(lamc.t[:, :], 0.0), w=[lamc.b])
    K.op(POOL, lambda e: e.memset(mhalf.t[:, :], -0.5), w=[mhalf.b])
    K.op(POOL, lambda e: e.memset(Vc.t[:, :, :, :], 0.0), w=Vb)
    K.op(POOL, lambda e: e.memset(Vc.t[:, :, :, 64:65], 1.0), w=Vb)
    for i in range(4):
        K.op(POOL, lambda e, i=i: e.memset(kTc[i].t[:, :], 0.0), w=kTb[i])
    K.op(POOL, lambda e: e.memset(KBt.t[:, :], 0.0), w=KBb)
    K.op(POOL, lambda e: e.memset(h8.t[:, :, :], 0.0), w=h8b)

    if _STOP <= 1:
        K.finish()
        es.close()
        return nc
    pieces = weight_pieces()
    NPC = len(pieces)
    passes = []
    seqs = []
    for s in range(NPS):
        seqs.append(dict(kind="p", idx=s, mi=s, T=S, TB=512, t0=0))
    for s in range(NSS):
        seqs.append(dict(kind="s", idx=s, mi=NPS + s, T=128, TB=128, t0=PAST))
    for l in range(NL):
        for sq_ in seqs:
            for blk in range(sq_["T"] // sq_["TB"]):
                passes.append(l)
    wstate = dict(issued=0)

    def wissue(upto):
        while wstate["issued"] <= upto and wstate["issued"] < len(passes) * NPC:
            g = wstate["issued"]
            l = passes[g // NPC]
            src, off, sz = pieces[g % NPC]
            slot = wslot[g % 3]
            K.dma(SP, slot.t[:, 0:sz], (wsc if src == 0 else wlo)[l, :, off:off + sz], w=[slot.b], prim=slot.b)
            wstate["issued"] += 1

    def wget(g, ahead=2):
        wissue(g + ahead)
        return wslot[g % 3]

    gpass = [0]

    def ev(i):
        return ACT if (i % 2 == 0) else DVE

    def copy_on(eng, out_ap, in_ap, r, w, scale=None):
        if eng is ACT:
            if scale is None:
                K.op(ACT, lambda e: e.activation(out=out_ap, in_=in_ap, func=AF.Copy), r=r, w=w)
            else:
                K.op(ACT, lambda e: e.activation(out=out_ap, in_=in_ap, func=AF.Copy, scale=scale), r=r, w=w)
        else:
            if scale is None:
                K.op(eng, lambda e: e.tensor_copy(out=out_ap, in_=in_ap), r=r, w=w)
            else:
                K.op(eng, lambda e: e.tensor_scalar(out=out_ap, in0=in_ap, scalar1=scale, scalar2=None, op0=ALU.mult), r=r, w=w)

    def ln_chunk_stats(c, TB, pmean, pmsq):
        rb = bcells.get()
        rs = bcells.get()
        K.op(ACT, lambda e: e.activation(out=rb.t[:, 0:TB], in_=xt.t[:, c, 0:TB], func=AF.Copy), r=[xb[c]], w=[rb.b])
        K.op(ACT, lambda e: e.activation(out=rs.t[:, 0:TB], in_=xt.t[:, c, 0:TB], func=AF.Square), r=[xb[c]], w=[rs.b])
        K.op(PE, lambda e: e.matmul(pmean.t[:, 0:TB], lhsT=onesM.t[:, :], rhs=rb.t[:, 0:TB], start=(c == 0), stop=(c == 7)),
             r=[onesM.b, rb.b], w=[pmean.b])
        K.op(PE, lambda e: e.matmul(pmsq.t[:, 0:TB], lhsT=onesM.t[:, :], rhs=rs.t[:, 0:TB], start=(c == 0), stop=(c == 7)),
             r=[onesM.b, rs.b], w=[pmsq.b])
        bcells.put(rb)
        bcells.put(rs)

    def ln_finish(l, si, TB, g_col, b_col, second, pmean, pmsq, after_chunk=None):
        mean = fcells.get()
        var = fcells.get()
        rstd = fcells.get()
        if second:
            K.op(DVE, lambda e: e.tensor_copy(out=mean.t[:, 0:TB], in_=pmean.t[:, 0:TB]), r=[pmean.b], w=[mean.b])
        else:
            K.op(ACT, lambda e: e.activation(out=mean.t[:, 0:TB], in_=pmean.t[:, 0:TB], func=AF.Copy), r=[pmean.b], w=[mean.b])
        K.op(DVE, lambda e: e.tensor_tensor(out=var.t[:, 0:TB], in0=mean.t[:, 0:TB], in1=mean.t[:, 0:TB], op=ALU.mult), r=[mean.b], w=[var.b])
        K.op(DVE, lambda e: e.tensor_tensor(out=var.t[:, 0:TB], in0=pmsq.t[:, 0:TB], in1=var.t[:, 0:TB], op=ALU.subtract), r=[pmsq.b, var.b], w=[var.b])
        K.op(DVE, lambda e: e.tensor_scalar(out=var.t[:, 0:TB], in0=var.t[:, 0:TB], scalar1=0.0, scalar2=EPS_LN, op0=ALU.max, op1=ALU.add), r=[var.b], w=[var.b])
        K.op(ACT, lambda e: e.activation(out=rstd.t[:, 0:TB], in_=var.t[:, 0:TB], func=AF.Ln), r=[var.b], w=[rstd.b])
        K.op(ACT, lambda e: e.activation(out=rstd.t[:, 0:TB], in_=rstd.t[:, 0:TB], func=AF.Exp, scale=-0.5), r=[rstd.b], w=[rstd.b])
        K.op(DVE, lambda e: e.scalar_tensor_tensor(out=var.t[:, 0:TB], in0=mean.t[:, 0:TB], scalar=-1.0, in1=rstd.t[:, 0:TB], op0=ALU.mult, op1=ALU.mult),
             r=[mean.b, rstd.b], w=[var.b])
        ps_acc.put(pmean)
        ps_acc.put(pmsq)
        for c in range(8):
            eng = POOL if c in (3, 7) else DVE
            K.op(eng, lambda e, c=c: e.tensor_tensor(out=xt.t[:, c, 0:TB], in0=xt.t[:, c, 0:TB], in1=rstd.t[:, 0:TB], op=ALU.mult), r=[xb[c], rstd.b], w=[xb[c]])
            K.op(eng, lambda e, c=c: e.tensor_tensor(out=xt.t[:, c, 0:TB], in0=xt.t[:, c, 0:TB], in1=var.t[:, 0:TB], op=ALU.add), r=[xb[c], var.b], w=[xb[c]])
            if not second:
                K.op(DVE, lambda e, c=c: e.tensor_scalar(out=h8.t[:, c, 0:TB], in0=xt.t[:, c, 0:TB],
                                                         scalar1=dA2[l].t[:, c, si:si + 1], scalar2=dB2[l].t[:, c, si:si + 1],
                                                         op0=ALU.mult, op1=ALU.add), r=[xb[c], dA2[l].b, dB2[l].b], w=[h8b[c]])
            if second:
                K.op(DVE, lambda e, c=c: e.tensor_scalar(out=xt.t[:, c, 0:TB], in0=xt.t[:, c, 0:TB],
                                                         scalar1=lp[l].t[:, LP_LN + g_col + c:LP_LN + g_col + c + 1],
                                                         scalar2=lp[l].t[:, LP_LN + b_col + c:LP_LN + b_col + c + 1],
                                                         op0=ALU.mult, op1=ALU.add), r=[xb[c], lp[l].b], w=[xb[c]])
            else:
                K.op(ACT, lambda e, c=c: e.activation(out=xt.t[:, c, 0:TB], in_=xt.t[:, c, 0:TB], func=AF.Identity,
                                                      scale=lp[l].t[:, LP_LN + g_col + c:LP_LN + g_col + c + 1],
                                                      bias=lp[l].t[:, LP_LN + b_col + c:LP_LN + b_col + c + 1]),
                     r=[xb[c], lp[l].b], w=[xb[c]])
            if after_chunk is not None:
                after_chunk(c)
        fcells.put(mean)
        fcells.put(var)
        fcells.put(rstd)

    def seq_ctx(l, sq_):
        kind, sidx, si, Tn, TB, t0 = sq_["kind"], sq_["idx"], sq_["mi"], sq_["T"], sq_["TB"], sq_["t0"]
        NT = TB // 128
        is_s = kind == "s"
        if l == 0:
            x_src = xp[sidx] if not is_s else xs[sidx]
            xsrc_b = None
        else:
            x_src = xmp[sidx] if not is_s else xms[sidx]
            xsrc_b = sq_["xmid_b"]
        if l == NL - 1:
            x_dst = yp[sidx] if not is_s else ys[sidx]
        else:
            x_dst = xmp[sidx] if not is_s else xms[sidx]
            sq_["xmid_b"] = [Buf("xmid%d" % c) for c in range(8)]
        o_k = (pk if not is_s else sk)[l, sidx]
        o_v = (pv if not is_s else sv)[l, sidx]
        o_lf = (plf if not is_s else slf)[l, sidx]
        o_C = (pC if not is_s else sCo)[l, sidx]
        o_M = (pM if not is_s else sMo)[l, sidx]
        lpt = lp[l].t
        return dict(kind=kind, sidx=sidx, si=si, Tn=Tn, TB=TB, t0=t0, NT=NT, is_s=is_s, x_src=x_src, xsrc_b=xsrc_b, x_dst=x_dst,
                    o_k=o_k, o_v=o_v, o_lf=o_lf, o_C=o_C, o_M=o_M, lpt=lpt)

    def seq_init(l, sq_):
        cx = sq_['ctx'][l]
        kind, sidx, si, Tn, TB, t0, NT, is_s = cx['kind'], cx['sidx'], cx['si'], cx['Tn'], cx['TB'], cx['t0'], cx['NT'], cx['is_s']
        x_src, xsrc_b, x_dst = cx['x_src'], cx['xsrc_b'], cx['x_dst']
        o_k, o_v, o_lf, o_C, o_M, lpt = cx['o_k'], cx['o_v'], cx['o_lf'], cx['o_C'], cx['o_M'], cx['lpt']
        K.op(POOL, lambda e: e.memset(carF.t[:, :], 0.0), w=[carF.b])
        K.op(POOL, lambda e: e.memset(carB.t[:, :], 0.0), w=[carB.b])
        if not is_s:
            K.op(POOL, lambda e: e.memset(carG.t[:, :], -1e30), w=[carG.b])
            for j in range(2):
                K.op(POOL, lambda e, j=j: e.memset(Ctil[j].t[:, :], 0.0), w=[Ctil[j].b])
        else:
            K.dma(SP, carG.t[:, :], sM[l, sidx, :, :], w=[carG.b], prim=carG.b)
            for j in range(2):
                K.dma(SP, Ctil[j].t[:, :], sC[l, sidx, j, :, :], w=[Ctil[j].b], prim=Ctil[j].b)
            for i in range(4):
                stg = [fcells.get(), fcells.get()]
                for hh_ in range(2):
                    K.dma(SP, stg[hh_].t[:, :], ck[l, sidx, i, :, hh_ * 512:(hh_ + 1) * 512], w=[stg[hh_].b], prim=stg[hh_].b)
                    copy_on(ev(hh_), kTc[i].t[:, hh_ * 512:(hh_ + 1) * 512], stg[hh_].t[:, :], [stg[hh_].b],
                            kTb[i][hh_ * 4:(hh_ + 1) * 4], scale=0.125)
                fcells.put(stg[0])
                fcells.put(stg[1])
            for j in range(PAST // 128):
                stg = fcells.get()
                K.dma(SP, stg.t[:, :], cv[l, sidx, j * 128:(j + 1) * 128, :], w=[stg.b], prim=stg.b)
                copy_on(ev(j), Vc.t[:, j, :, 0:64], stg.t[:, :].rearrange("p (h d) -> p h d", h=8), [stg.b], [Vb[j]])
                fcells.put(stg)
            for hh_ in range(PAST // 512):
                lfc = fcells.get()
                K.dma(SP, lfc.t[:, :], clf[l, sidx, :, hh_ * 512:(hh_ + 1) * 512], w=[lfc.b], prim=lfc.b)
                Ft = fcells.get()
                K.op(DVE, lambda e, lfc=lfc, Ft=Ft: e.tensor_tensor_scan(out=Ft.t[:, :], data0=ones5.t[:, 0:512], data1=lfc.t[:, :],
                                                                       initial=carF.t[:, 0:1], op0=ALU.mult, op1=ALU.add),
                     r=[lfc.b, carF.b, ones5.b], w=[Ft.b])
                K.op(DVE, lambda e, Ft=Ft: e.tensor_copy(out=carF.t[:, :], in_=Ft.t[:, 511:512]), r=[Ft.b], w=[carF.b])
                fcells.put(lfc)
                bias_rows(Ft, hh_ * 512, 512, None)
                fcells.put(Ft)


    def seq_final(l, sq_):
        cx = sq_['ctx'][l]
        kind, sidx, si, Tn, TB, t0, NT, is_s = cx['kind'], cx['sidx'], cx['si'], cx['Tn'], cx['TB'], cx['t0'], cx['NT'], cx['is_s']
        x_src, xsrc_b, x_dst = cx['x_src'], cx['xsrc_b'], cx['x_dst']
        o_k, o_v, o_lf, o_C, o_M, lpt = cx['o_k'], cx['o_v'], cx['o_lf'], cx['o_C'], cx['o_M'], cx['lpt']
        for j in range(2):
            K.dma(SP, o_C[j, :, :], Ctil[j].t[:, :], r=[Ctil[j].b], prim=Ctil[j].b)
        K.op(DVE, lambda e: e.tensor_tensor(out=mout.t[:, :], in0=carB.t[:, :], in1=carG.t[:, :], op=ALU.add), r=[carB.b, carG.b], w=[mout.b])
        K.dma(SP, o_M[:, :], mout.t[:, :], r=[mout.b], prim=mout.b)


    pre_stage = {}
    deferred = []

    def flush_deferred():
        while deferred:
            deferred.pop(0)()

    def prologue_pre(l, sq_, blk, g0):
        cx = sq_['ctx'][l]
        TB, x_src, xsrc_b = cx['TB'], cx['x_src'], cx['xsrc_b']
        tb0 = blk * TB
        for kc in range(2):
            tmpf = fcells.get()
            srcs = [] if xsrc_b is None else [xsrc_b[kc]]
            K.dma(SP, tmpf.t[:, 0:TB], x_src[kc, :, tb0:tb0 + TB], r=srcs, w=[tmpf.b], prim=tmpf.b)
            pre_stage[kc] = tmpf

    def prologue(l, sq_, blk, g0):
        cx = sq_['ctx'][l]
        kind, sidx, si, Tn, TB, t0, NT, is_s = cx['kind'], cx['sidx'], cx['si'], cx['Tn'], cx['TB'], cx['t0'], cx['NT'], cx['is_s']
        x_src, xsrc_b, x_dst = cx['x_src'], cx['xsrc_b'], cx['x_dst']
        o_k, o_v, o_lf, o_C, o_M, lpt = cx['o_k'], cx['o_v'], cx['o_lf'], cx['o_C'], cx['o_M'], cx['lpt']
        tb0 = blk * TB
        wt_hi = wget(g0 + PI_W1[0])
        wt_lo = wget(g0 + PI_W1LO, ahead=1)
        pq = [ps_st.get(), ps_st.get(), ps_gen.get(), ps_gen.get()]
        for kc in range(8):
            hlo = bcells.get()
            if kc in pre_stage:
                tmpf = pre_stage.pop(kc)
            else:
                tmpf = fcells.get()
                srcs = [] if xsrc_b is None else [xsrc_b[kc]]
                K.dma(SP, tmpf.t[:, 0:TB], x_src[kc, :, tb0:tb0 + TB], r=srcs, w=[tmpf.b], prim=tmpf.b)
            K.op(ACT, lambda e, kc=kc: e.activation(out=h8.t[:, kc, 0:TB], in_=tmpf.t[:, 0:TB], func=AF.Identity,
                                                   scale=dsc1[l].t[:, kc, si:si + 1], bias=modT[l].t[:, kc, si:si + 1]),
                 r=[tmpf.b, dsc1[l].b, modT[l].b], w=[h8b[kc]])
            K.op(DVE, lambda e, kc=kc: e.tensor_scalar(out=tmpf.t[:, 0:TB], in0=tmpf.t[:, 0:TB],
                                                       scalar1=dsc1[l].t[:, kc, si:si + 1], scalar2=modT[l].t[:, kc, si:si + 1],
                                                       op0=ALU.mult, op1=ALU.add), r=[tmpf.b, dsc1[l].b, modT[l].b], w=[tmpf.b])
            K.op(DVE, lambda e, kc=kc: e.tensor_tensor(out=hlo.t[:, 0:TB], in0=tmpf.t[:, 0:TB], in1=h8.t[:, kc, 0:TB], op=ALU.subtract),
                 r=[tmpf.b, h8b[kc]], w=[hlo.b])
            for oc in range(4):
                wsl = slice(oc * 1024 + kc * 128, oc * 1024 + (kc + 1) * 128)
                K.op(PE, lambda e, oc=oc, wsl=wsl, kc=kc: e.matmul(pq[oc].t[:, 0:TB], lhsT=wt_hi.t[:, wsl], rhs=h8.t[:, kc, 0:TB], start=(kc == 0), stop=False),
                     r=[wt_hi.b, h8b[kc]], w=[pq[oc].b])
                K.op(PE, lambda e, oc=oc, wsl=wsl, kc=kc: e.matmul(pq[oc].t[:, 0:TB], lhsT=wt_lo.t[:, wsl], rhs=h8.t[:, kc, 0:TB], start=False, stop=False),
                     r=[wt_lo.b, h8b[kc]], w=[pq[oc].b])
                K.op(PE, lambda e, oc=oc, wsl=wsl, kc=kc: e.matmul(pq[oc].t[:, 0:TB], lhsT=wt_hi.t[:, wsl], rhs=hlo.t[:, 0:TB], start=False, stop=(kc == 7)),
                     r=[wt_hi.b, hlo.b], w=[pq[oc].b])
            fcells.put(tmpf)
            bcells.put(hlo)
        for oc in range(4):
            pp = pq[oc]
            if oc < 2:
                copy_on(ACT, arena.t[:, oc, 0:TB], pp.t[:, 0:TB], [pp.b], [ab[oc]])
                K.op(DVE, lambda e, oc=oc, pp=pp: e.tensor_tensor(out=arena.t[:, 2 + oc, 0:TB], in0=pp.t[:, 0:TB], in1=arena.t[:, oc, 0:TB], op=ALU.subtract),
                     r=[pp.b, ab[oc]], w=[ab[2 + oc]])
            else:
                j = oc - 2
                copy_on(ACT, arena.t[:, 20 + j, 0:TB], pp.t[:, 0:TB], [pp.b], [ab[20 + j]], scale=0.125)
                K.op(DVE, lambda e, j=j, pp=pp: e.scalar_tensor_tensor(out=kTml[j].t[:, 0:TB], in0=pp.t[:, 0:TB], scalar=0.125, in1=arena.t[:, 20 + j, 0:TB],
                                                                      op0=ALU.mult, op1=ALU.subtract), r=[pp.b, ab[20 + j]], w=[kTml[j].b])
        ps_st.put(pq[0])
        ps_st.put(pq[1])
        ps_gen.put(pq[2])
        ps_gen.put(pq[3])


    def block_main(l, sq_, blk, g0, nxt):
        cx = sq_['ctx'][l]
        kind, sidx, si, Tn, TB, t0, NT, is_s = cx['kind'], cx['sidx'], cx['si'], cx['Tn'], cx['TB'], cx['t0'], cx['NT'], cx['is_s']
        x_src, xsrc_b, x_dst = cx['x_src'], cx['xsrc_b'], cx['x_dst']
        o_k, o_v, o_lf, o_C, o_M, lpt = cx['o_k'], cx['o_v'], cx['o_lf'], cx['o_C'], cx['o_M'], cx['lpt']
        tb0 = blk * TB
        pos0 = t0 + tb0
        jb0 = pos0 // 128
        gsp = fcells.get()
        gzi = fcells.get()
        gsp2 = fcells.get()

        def proj_tile(tl):
            wt = wget(g0 + PI_W1[tl // 4])
            wofs = (tl % 4) * 1024
            pp = rot(ps_gen)
            for kc in range(8):
                K.op(PE, lambda e, wt=wt, wofs=wofs, kc=kc, pp=pp: e.matmul(
                    pp.t[:, 0:TB], lhsT=wt.t[:, wofs + kc * 128: wofs + (kc + 1) * 128], rhs=h8.t[:, kc, 0:TB],
                    start=(kc == 0), stop=(kc == 7)), r=[wt.b, h8b[kc]], w=[pp.b])
            return pp

        pp = proj_tile(4)
        K.op(ACT, lambda e, pp=pp: e.activation(out=gsp.t[:, 0:TB], in_=pp.t[:, 0:TB], func=AF.Exp, scale=-1.0, bias=negb[l].t[:, 0:1]),
             r=[pp.b, negb[l].b], w=[gsp.b])
        K.op(ACT, lambda e: e.activation(out=gsp.t[:, 0:TB], in_=gsp.t[:, 0:TB], func=AF.Ln, bias=1.0), r=[gsp.b], w=[gsp.b])
        lfo = fcells.get()
        K.op(ACT, lambda e: e.activation(out=lfo.t[:, 0:TB], in_=gsp.t[:, 0:TB], func=AF.Copy, scale=-1.0), r=[gsp.b], w=[lfo.b])
        K.dma(ACT, o_lf[:, tb0:tb0 + TB], lfo.t[:, 0:TB], r=[lfo.b], prim=lfo.b)
        fcells.put(lfo)
        pp = proj_tile(5)
        K.op(ACT, lambda e, pp=pp: e.activation(out=gzi.t[:, 0:TB], in_=pp.t[:, 0:TB], func=AF.Identity, bias=lpt[:, LP_BI:LP_BI + 1]),
             r=[pp.b, lp[l].b], w=[gzi.b])
        pp = proj_tile(6)
        K.op(ACT, lambda e, pp=pp: e.activation(out=gsp2.t[:, 0:TB], in_=pp.t[:, 0:TB], func=AF.Exp, scale=-1.0, bias=negb[l].t[:, 1:2]),
             r=[pp.b, negb[l].b], w=[gsp2.b])
        K.op(ACT, lambda e: e.activation(out=gsp2.t[:, 0:TB], in_=gsp2.t[:, 0:TB], func=AF.Ln, bias=1.0), r=[gsp2.b], w=[gsp2.b])

        stop_at(2)
        Ft = fcells.get()
        K.op(DVE, lambda e: e.tensor_tensor_scan(out=Ft.t[:, 0:TB], data0=ones5.t[:, 0:TB], data1=gsp.t[:, 0:TB],
                                                 initial=carF.t[:, 0:1], op0=ALU.mult, op1=ALU.subtract),
             r=[gsp.b, carF.b, ones5.b], w=[Ft.b])
        K.op(DVE, lambda e: e.tensor_copy(out=carF.t[:, :], in_=Ft.t[:, TB - 1:TB]), r=[Ft.b], w=[carF.b])
        fcells.put(gsp)
        bias_rows(Ft, pos0, TB, [ab[12 + h] for h in range(8)])
        fcells.put(Ft)

        Bt = fcells.get()
        K.op(DVE, lambda e: e.tensor_tensor_scan(out=Bt.t[:, 0:TB], data0=ones5.t[:, 0:TB], data1=gsp2.t[:, 0:TB],
                                                 initial=carB.t[:, 0:1], op0=ALU.mult, op1=ALU.subtract),
             r=[gsp2.b, carB.b, ones5.b], w=[Bt.b])
        fcells.put(gsp2)
        at = gzi
        K.op(DVE, lambda e: e.tensor_tensor(out=at.t[0:4, 0:TB], in0=gzi.t[0:4, 0:TB], in1=Bt.t[0:4, 0:TB], op=ALU.subtract), r=[gzi.b, Bt.b], w=[at.b])
        Gt = fcells.get()
        K.op(DVE, lambda e: e.tensor_tensor_scan(out=Gt.t[0:4, 0:TB], data0=ones5.t[0:4, 0:TB], data1=at.t[0:4, 0:TB],
                                                 initial=carG.t[0:4, 0:1], op0=ALU.mult, op1=ALU.max),
             r=[at.b, carG.b, ones5.b], w=[Gt.b])
        nGt = fcells.get()
        K.op(DVE, lambda e: e.tensor_scalar(out=nGt.t[0:4, 0:TB], in0=Gt.t[0:4, 0:TB], scalar1=-1.0, scalar2=None, op0=ALU.mult), r=[Gt.b], w=[nGt.b])

        for tl in range(7, 15):
            pp = proj_tile(tl)
            if tl < 11:
                pr = tl - 7
                for half in range(2):
                    hidx = 4 + 2 * pr + half
                    lo, hi = half * 64, half * 64 + 64
                    olo, ohi = (1 - half) * 64, (1 - half) * 64 + 64
                    copy_on(ACT, arena.t[lo:hi, hidx, 0:TB], pp.t[lo:hi, 0:TB], [pp.b], [ab[hidx]])
                    K.op(POOL, lambda e, hidx=hidx, olo=olo, ohi=ohi: e.memset(arena.t[olo:ohi, hidx, 0:TB], 0.0), w=[ab[hidx]])
            else:
                i = tl - 11
                ko = fcells.get()
                copy_on(ACT, ko.t[:, 0:TB], pp.t[:, 0:TB], [pp.b], [ko.b])
                copy_on(DVE, kTc[i].t[:, pos0:pos0 + TB], ko.t[:, 0:TB], [ko.b], kTb[i][jb0:jb0 + NT], scale=0.125)
                K.dma(ACT, o_k[i, :, tb0:tb0 + TB], ko.t[:, 0:TB], r=[ko.b], prim=ko.b)
                fcells.put(ko)

        stop_at(4)
        wt = wget(g0 + PI_W2)
        for c in range(NT):
            cs = slice(c * 128, (c + 1) * 128)
            pB = rot(ps_st)
            for kc in range(8):
                K.op(PE, lambda e, kc=kc, wt=wt, pB=pB, cs=cs: e.matmul(pB.t[:, 0:256], lhsT=h8.t[:, kc, cs], rhs=wt.t[:, kc * 256:(kc + 1) * 256],
                                                                       start=(kc == 0), stop=(kc == 7)), r=[h8b[kc], wt.b], w=[pB.b])
            K.op(ACT, lambda e, pB=pB: e.activation(out=vnt.t[:, :], in_=pB.t[:, 0:256], func=AF.Sigmoid), r=[pB.b], w=[vnt.b])
            K.op(POOL, lambda e, c=c: e.tensor_tensor(out=sgt[c].t[:, :], in0=vnt.t[:, :], in1=lpt[:, LP_GM:LP_GM + 256], op=ALU.mult), r=[vnt.b, lp[l].b], w=[sgt[c].b])
        wt = wget(g0 + PI_W2 + 1)
        for c in range(NT):
            cs = slice(c * 128, (c + 1) * 128)
            pC_ = rot(ps_st)
            for kc in range(8):
                K.op(PE, lambda e, kc=kc, wt=wt, pC_=pC_, cs=cs: e.matmul(pC_.t[:, 0:512], lhsT=h8.t[:, kc, cs], rhs=wt.t[:, kc * 512:(kc + 1) * 512],
                                                                         start=(kc == 0), stop=(kc == 7)), r=[h8b[kc], wt.b], w=[pC_.b])
            copy_on(ACT, ut[c].t[:, :], pC_.t[:, 0:256], [pC_.b], [ut[c].b])
            K.op(DVE, lambda e, pC_=pC_: e.bn_stats(out=st6.t[:, :], in_=pC_.t[:, 256:512]), r=[pC_.b], w=[st6.b])
            K.op(DVE, lambda e: e.bn_aggr(out=st2.t[:, :], in_=st6.t[:, :]), r=[st6.b], w=[st2.b])
            K.op(DVE, lambda e: e.tensor_scalar(out=st2.t[:, 1:2], in0=st2.t[:, 1:2], scalar1=EPS_G, scalar2=None, op0=ALU.add), r=[st2.b], w=[st2.b])
            K.op(POOL, lambda e: e.tensor_tensor(out=st2.t[:, 1:2], in0=st2.t[:, 1:2], in1=mhalf.t[:, 0:1], op=ALU.pow), r=[st2.b, mhalf.b], w=[st2.b])
            K.op(DVE, lambda e, pC_=pC_: e.tensor_scalar(out=vnt.t[:, :], in0=pC_.t[:, 256:512], scalar1=st2.t[:, 0:1], scalar2=st2.t[:, 1:2],
                                                         op0=ALU.subtract, op1=ALU.mult), r=[pC_.b, st2.b], w=[vnt.b])
            K.op(POOL, lambda e: e.tensor_tensor(out=vnt.t[:, :], in0=vnt.t[:, :], in1=lpt[:, LP_GG:LP_GG + 256], op=ALU.mult), r=[vnt.b, lp[l].b], w=[vnt.b])
            K.op(POOL, lambda e: e.tensor_tensor(out=vnt.t[:, :], in0=vnt.t[:, :], in1=lpt[:, LP_GB:LP_GB + 256], op=ALU.add), r=[vnt.b, lp[l].b], w=[vnt.b])
            K.op(POOL, lambda e, c=c: e.tensor_copy(out=vnb[c].t[:, :], in_=vnt.t[:, :]), r=[vnt.b], w=[vnb[c].b])
            if is_s:
                K.dma(STQ, sgv[l, sidx, :, :], vnt.t[:, :], r=[vnt.b], prim=vnt.b)
        wt = wget(g0 + PI_W2 + 2)
        for c in range(NT):
            cs = slice(c * 128, (c + 1) * 128)
            jblk = jb0 + c
            pD = rot(ps_st)
            for kc in range(8):
                K.op(PE, lambda e, kc=kc, wt=wt, pD=pD, cs=cs: e.matmul(pD.t[:, 0:512], lhsT=h8.t[:, kc, cs], rhs=wt.t[:, kc * 512:(kc + 1) * 512],
                                                                       start=(kc == 0), stop=(kc == 7)), r=[h8b[kc], wt.b], w=[pD.b])
            vo = fcells.get()
            copy_on(ACT, vo.t[:, :], pD.t[:, 0:512], [pD.b], [vo.b])
            copy_on(DVE, Vc.t[:, jblk, :, 0:64], vo.t[:, 0:512].rearrange("p (h d) -> p h d", h=8), [vo.b], [Vb[jblk]])
            K.dma(ACT, o_v[tb0 + c * 128: tb0 + (c + 1) * 128, :], vo.t[:, :], r=[vo.b], prim=vo.b)
            fcells.put(vo)


        stop_at(3)
        wt = wget(g0 + PI_W2 + 3)
        for c in range(NT):
            cs = slice(c * 128, (c + 1) * 128)
            pA = rot(ps_st)
            for kc in range(8):
                K.op(PE, lambda e, kc=kc, wt=wt, pA=pA, cs=cs: e.matmul(pA.t[:, 0:512], lhsT=h8.t[:, kc, cs], rhs=wt.t[:, kc * 512:(kc + 1) * 512],
                                                                       start=(kc == 0), stop=(kc == 7)), r=[h8b[kc], wt.b], w=[pA.b])
            copy_on(ACT, ktok[c].t[:, :], pA.t[:, 0:256], [pA.b], [ktok[c].b], scale=0.125)
            K.op(DVE, lambda e, pA=pA, c=c: e.tensor_copy(out=vaug[c].t[:, :, 0:64], in_=pA.t[:, 256:512].rearrange("p (h d) -> p h d", h=4)),
                 r=[pA.b], w=[vaug[c].b])
            K.op(POOL, lambda e, c=c: e.memset(vaug[c].t[:, :, 64:65], 1.0), w=[vaug[c].b])

        stop_at(5)
        def pass3_gen():
            for c in range(NT):
                cs = slice(c * 128, (c + 1) * 128)
                ec = c * 128 + (DEC - 1 if is_s else 127)
                am = amt
                K.op(DVE, lambda e, cs=cs, ec=ec: e.tensor_scalar(out=am.t[0:4, 0:128], in0=at.t[0:4, cs], scalar1=nGt.t[0:4, ec:ec + 1], scalar2=0.0,
                                                                op0=ALU.add, op1=ALU.min), r=[at.b, nGt.b], w=[am.b])
                K.op(ACT, lambda e: e.activation(out=E1t.t[0:4, :], in_=am.t[0:4, 0:128], func=AF.Exp), r=[am.b], w=[E1t.b])
                if is_s:
                    K.op(DVE, lambda e: e.tensor_tensor(out=E1t.t[0:4, :], in0=E1t.t[0:4, :], in1=validF.t[0:4, :], op=ALU.mult), r=[E1t.b, validF.b], w=[E1t.b])
                K.op(ACT, lambda e, cs=cs, ec=ec: e.activation(out=CLt.t[0:4, :], in_=Bt.t[0:4, cs], func=AF.Exp, scale=-1.0, bias=nGt.t[0:4, ec:ec + 1]),
                     r=[Bt.b, nGt.b], w=[CLt.b])
                pt_ = rot(ps_gen)
                K.op(PE, lambda e, pt_=pt_: e.matmul(pt_.t[:, 0:4], lhsT=E1t.t[:, :], rhs=identF.t[:, 0:4], start=True, stop=True), r=[E1t.b, identF.b], w=[pt_.b])
                K.op(PE, lambda e, pt_=pt_: e.matmul(pt_.t[:, 4:8], lhsT=CLt.t[:, :], rhs=identF.t[:, 0:4], start=True, stop=True), r=[CLt.b, identF.b], w=[pt_.b])
                K.op(DVE, lambda e, pt_=pt_, c=c: e.tensor_copy(out=tokS[c].t[:, :], in_=pt_.t[:, 0:8]), r=[pt_.b], w=[tokS[c].b])
                yield

            for c in range(NT if 'pass3' not in os.environ.get('KSKIP', '') else 0):
                cs = slice(c * 128, (c + 1) * 128)
                ec = c * 128 + (DEC - 1 if is_s else 127)
                K.op(DVE, lambda e, c=c: e.tensor_tensor(out=vaug[c].t[:, :, :], in0=vaug[c].t[:, :, :],
                                                         in1=tokS[c].t[:, 0:4].rearrange("p (h o) -> p h o", o=1).to_broadcast([128, 4, 65]), op=ALU.mult),
                     r=[vaug[c].b, tokS[c].b], w=[vaug[c].b])
                K.op(DVE, lambda e: e.tensor_copy(out=gprev.t[:, :], in_=carG.t[:, :]), r=[carG.b], w=[gprev.b])
                K.op(ACT, lambda e, ec=ec: e.activation(out=lamc.t[0:4, :], in_=gprev.t[0:4, :], func=AF.Exp, bias=nGt.t[0:4, ec:ec + 1]),
                     r=[gprev.b, nGt.b], w=[lamc.b])
                K.op(DVE, lambda e, ec=ec: e.tensor_copy(out=carG.t[0:4, :], in_=Gt.t[0:4, ec:ec + 1]), r=[Gt.b, gprev.b], w=[carG.b])
                K.op(DVE, lambda e, ec=ec: e.tensor_copy(out=carB.t[0:4, :], in_=Bt.t[0:4, ec:ec + 1]), r=[Bt.b], w=[carB.b])
                K.op(DVE, lambda e: e.tensor_scalar(out=Dl.t[:, :], in0=identF.t[:, 0:4], scalar1=lamc.t[:, 0:1], scalar2=None, op0=ALU.mult), r=[identF.b, lamc.b], w=[Dl.b])
                pt_ = rot(ps_gen)
                K.op(PE, lambda e, pt_=pt_: e.matmul(pt_.t[:, 8:12], lhsT=onesF.t[:, :], rhs=Dl.t[:, :], start=True, stop=True), r=[onesF.b, Dl.b], w=[pt_.b])
                K.op(DVE, lambda e, pt_=pt_: e.tensor_copy(out=lamB.t[:, :], in_=pt_.t[:, 8:12]), r=[pt_.b], w=[lamB.b])
                for j in range(2):
                    for half in range(2):
                        lo, hi = half * 64, half * 64 + 64
                        hcol = 2 * j + half
                        K.op(DVE, lambda e, j=j, lo=lo, hi=hi, hcol=hcol: e.tensor_scalar(out=Cp[j].t[lo:hi, :], in0=Ctil[j].t[lo:hi, :],
                                                                                         scalar1=lamB.t[lo:hi, hcol:hcol + 1], scalar2=None, op0=ALU.mult),
                             r=[Ctil[j].b, lamB.b], w=[Cp[j].b])
                    K.op(POOL, lambda e, j=j: e.tensor_copy(out=Cb[j].t[:, :], in_=Cp[j].t[:, :]), r=[Cp[j].b], w=[Cb[j].b])

                pZ = rot(ps_gen)
                for g in range(4):
                    K.op(PE, lambda e, g=g, pZ=pZ, c=c: e.matmul(pZ.t[:, g * 64:(g + 1) * 64], lhsT=wsTb[l].t[:, g * 128:(g + 1) * 128], rhs=vnb[c].t[:, g * 64:(g + 1) * 64],
                                                                start=True, stop=True), r=[wsTb[l].b, vnb[c].b], w=[pZ.b])
                for g in range(4):
                    K.op(DVE, lambda e, g=g, pZ=pZ, c=c: e.scalar_tensor_tensor(out=hgb.t[:, g * 64:(g + 1) * 64], in0=pZ.t[:, g * 64:(g + 1) * 64],
                                                                               scalar=lpt[:, LP_BS + g:LP_BS + g + 1], in1=ut[c].t[:, g * 64:(g + 1) * 64],
                                                                               op0=ALU.add, op1=ALU.mult), r=[pZ.b, lp[l].b, ut[c].b], w=[hgb.b])
                yield
                pSx = [rot(ps_st), rot(ps_st)]
                for h in range(4):
                    j, half = h // 2, h % 2
                    lo, hi = half * 64, half * 64 + 64
                    pS = pSx[half]
                    osl = slice(j * 128, (j + 1) * 128)
                    K.op(PE, lambda e, j=j, lo=lo, hi=hi, osl=osl, pS=pS, cs=cs: e.matmul(pS.t[:, osl], lhsT=arena.t[lo:hi, 20 + j, cs], rhs=arena.t[lo:hi, j, cs],
                                                                                       start=True, stop=False), r=[ab[20 + j], ab[j]], w=[pS.b])
                    K.op(PE, lambda e, j=j, lo=lo, hi=hi, osl=osl, pS=pS, cs=cs: e.matmul(pS.t[:, osl], lhsT=arena.t[lo:hi, 20 + j, cs], rhs=arena.t[lo:hi, 2 + j, cs],
                                                                                       start=False, stop=False), r=[ab[20 + j], ab[2 + j]], w=[pS.b])
                    K.op(PE, lambda e, j=j, lo=lo, hi=hi, osl=osl, pS=pS, cs=cs: e.matmul(pS.t[:, osl], lhsT=kTml[j].t[lo:hi, cs], rhs=arena.t[lo:hi, j, cs],
                                                                                       start=False, stop=True), r=[kTml[j].b, ab[j]], w=[pS.b])
                smT = bcells.get()
                smT4 = smT.t[:, :].rearrange("p (j e t) -> p j e t", j=2, e=2)
                for half in range(2):
                    K.op(DVE, lambda e, half=half: e.tensor_tensor(out=smT4[:, :, half, :], in0=pSx[half].t[:, 0:256].rearrange("p (j t) -> p j t", j=2),
                                                                   in1=mask01.t[:, :].rearrange("p (o t) -> p o t", o=1).to_broadcast([128, 2, 128]), op=ALU.mult),
                         r=[pSx[half].b, mask01.b], w=[smT.b])
                yield
                pT = rot(ps_gen)
                for q in range(2):
                    K.op(PE, lambda e, q=q, pT=pT: e.matmul(pT.t[:, q * 128:(q + 1) * 128], lhsT=hgb.t[:, q * 128:(q + 1) * 128], rhs=identB.t[:, :],
                                                           start=True, stop=True), r=[hgb.b, identB.b], w=[pT.b])
                for q in range(2):
                    copy_on(ev(q), mix4.t[:, 2 + q, cs], pT.t[:, q * 128:(q + 1) * 128], [pT.b], [mix4b[2 + q]])
                yield
                pHx = [rot(ps_gen), rot(ps_gen)]
                for h in range(4):
                    j, half = h // 2, h % 2
                    lo, hi = half * 64, half * 64 + 64
                    pH = pHx[half]
                    K.op(PE, lambda e, h=h, j=j, pH=pH, c=c: e.matmul(pH.t[:, j * 65:(j + 1) * 65], lhsT=smT.t[:, h * 128:(h + 1) * 128], rhs=vaug[c].t[:, h, :],
                                                                    start=True, stop=False), r=[smT.b, vaug[c].b], w=[pH.b])
                    K.op(PE, lambda e, j=j, lo=lo, hi=hi, pH=pH, cs=cs: e.matmul(pH.t[:, j * 65:(j + 1) * 65], lhsT=arena.t[lo:hi, j, cs], rhs=Cb[j].t[lo:hi, :],
                                                                               start=False, stop=True), r=[ab[j], Cb[j].b], w=[pH.b])
                bcells.put(smT)
                dn, ad, rd, ssum, ssq, mean4, var4, rs4 = sm4
                v4 = lambda tt: tt.t[:, :].rearrange("p (j e o) -> p j e o", j=2, e=2)
                hh4 = hh.t[:, :].rearrange("p (j e d) -> p j e d", j=2, e=2)
                for half in range(2):
                    pH3 = pHx[half].t[:, 0:130].rearrange("p (j c) -> p j c", c=65)
                    K.op(DVE, lambda e, half=half, pH3=pH3: e.tensor_copy(out=v4(dn)[:, :, half, :], in_=pH3[:, :, 64:65]), r=[pHx[half].b], w=[dn.b])
                K.op(DVE, lambda e: e.scalar_tensor_tensor(out=ad.t[:, :], in0=dn.t[:, :], scalar=-1.0, in1=dn.t[:, :], op0=ALU.mult, op1=ALU.max), r=[dn.b], w=[ad.b])
                K.op(DVE, lambda e, c=c: e.tensor_tensor(out=ad.t[:, :], in0=ad.t[:, :], in1=tokS[c].t[:, 4:8], op=ALU.max), r=[ad.b, tokS[c].b], w=[ad.b])
                K.op(DVE, lambda e: e.reciprocal(out=rd.t[:, :], in_=ad.t[:, :]), r=[ad.b], w=[rd.b])
                for half in range(2):
                    pH3 = pHx[half].t[:, 0:130].rearrange("p (j c) -> p j c", c=65)
                    K.op(DVE, lambda e, half=half, pH3=pH3: e.tensor_tensor(out=hh4[:, :, half, :], in0=pH3[:, :, 0:64],
                                                                          in1=v4(rd)[:, :, half, :].to_broadcast([128, 2, 64]), op=ALU.mult),
                         r=[pHx[half].b, rd.b], w=[hh.b])
                hh3 = hh.t[:, :].rearrange("p (h d) -> p h d", h=4)
                sq3 = sq.t[:, :].rearrange("p (h d) -> p h d", h=4)
                bc4 = lambda tt: tt.t[:, :].rearrange("p (h o) -> p h o", o=1).to_broadcast([128, 4, 64])
                K.op(DVE, lambda e: e.tensor_reduce(out=ssum.t[:, :], in_=hh3, axis=AX.X, op=ALU.add), r=[hh.b], w=[ssum.b])
                K.op(POOL, lambda e: e.tensor_tensor(out=sq.t[:, :], in0=hh.t[:, :], in1=hh.t[:, :], op=ALU.mult), r=[hh.b], w=[sq.b])
                K.op(DVE, lambda e: e.tensor_reduce(out=ssq.t[:, :], in_=sq3, axis=AX.X, op=ALU.add), r=[sq.b], w=[ssq.b])
                K.op(DVE, lambda e: e.tensor_scalar(out=mean4.t[:, :], in0=ssum.t[:, :], scalar1=1.0 / 64.0, scalar2=None, op0=ALU.mult), r=[ssum.b], w=[mean4.b])
                K.op(DVE, lambda e: e.tensor_tensor(out=var4.t[:, :], in0=mean4.t[:, :], in1=mean4.t[:, :], op=ALU.mult), r=[mean4.b], w=[var4.b])
                K.op(DVE, lambda e: e.scalar_tensor_tensor(out=var4.t[:, :], in0=ssq.t[:, :], scalar=1.0 / 64.0, in1=var4.t[:, :], op0=ALU.mult, op1=ALU.subtract),
                     r=[ssq.b, var4.b], w=[var4.b])
                K.op(DVE, lambda e: e.tensor_scalar(out=var4.t[:, :], in0=var4.t[:, :], scalar1=0.0, scalar2=EPS_HN, op0=ALU.max, op1=ALU.add), r=[var4.b], w=[var4.b])
                K.op(POOL, lambda e: e.tensor_tensor(out=rs4.t[:, :], in0=var4.t[:, :], in1=mhalf.t[:, :], op=ALU.pow), r=[var4.b, mhalf.b], w=[rs4.b])
                K.op(DVE, lambda e: e.tensor_tensor(out=hh3, in0=hh3, in1=bc4(mean4), op=ALU.subtract), r=[hh.b, mean4.b], w=[hh.b])
                K.op(DVE, lambda e: e.tensor_tensor(out=hh3, in0=hh3, in1=bc4(rs4), op=ALU.mult), r=[hh.b, rs4.b], w=[hh.b])
                K.op(POOL, lambda e, c=c: e.tensor_tensor(out=hmb.t[:, :], in0=hh.t[:, :], in1=sgt[c].t[:, :], op=ALU.mult), r=[hh.b, sgt[c].b], w=[hmb.b])
                yield
                pU = ps_gen.get()
                for h in range(4):
                    K.op(PE, lambda e, h=h, pU=pU, c=c: e.matmul(pU.t[:, h * 65:(h + 1) * 65], lhsT=ktok[c].t[:, (h // 2) * 128:(h // 2 + 1) * 128], rhs=vaug[c].t[:, h, :],
                                                                start=True, stop=True), r=[ktok[c].b, vaug[c].b], w=[pU.b])
                yield
                pT = rot(ps_gen)
                for q in range(2):
                    K.op(PE, lambda e, q=q, pT=pT: e.matmul(pT.t[:, q * 128:(q + 1) * 128], lhsT=hmb.t[:, q * 128:(q + 1) * 128], rhs=identB.t[:, :],
                                                           start=True, stop=True), r=[hmb.b, identB.b], w=[pT.b])
                for q in range(2):
                    copy_on(ev(q), mix4.t[:, q, cs], pT.t[:, q * 128:(q + 1) * 128], [pT.b], [mix4b[q]])
                for h in range(4):
                    j, half = h // 2, h % 2
                    lo, hi = half * 64, half * 64 + 64
                    K.op(DVE, lambda e, h=h, j=j, lo=lo, hi=hi, pU=pU: e.tensor_tensor(out=Ctil[j].t[lo:hi, :], in0=Cp[j].t[lo:hi, :], in1=pU.t[lo:hi, h * 65:(h + 1) * 65], op=ALU.add),
                         r=[Cp[j].b, pU.b], w=[Ctil[j].b])
                ps_gen.put(pU)

            yield
        p3 = pass3_gen()

        def adv3():
            try:
                next(p3)
                return True
            except StopIteration:
                return False

        stop_at(6)
        flush_deferred()
        for c in range(8):
            srcs = [] if xsrc_b is None else [xsrc_b[c]]
            K.dma(SP, xt.t[:, c, 0:TB], x_src[c, :, tb0:tb0 + TB], r=srcs, w=[xb[c]], prim=xb[c])
        nj = jb0 + NT
        pend = [None]
        sp3 = [0, max(1, min(8, (8 * nj) // 26))]

        def finalize_copy(h, acc):
            nS = fcells.get()
            copy_on(ACT, nS.t[0:65, 0:TB], acc.t[0:65, 0:TB], [acc.b], [nS.b])
            ps_acc.put(acc)
            return nS

        def finalize(h, nS):
            pb = rot(ps_gen)
            K.op(PE, lambda e: e.matmul(pb.t[0:64, 0:TB], lhsT=onesF.t[64:65, 0:64], rhs=nS.t[64:65, 0:TB], start=True, stop=True),
                 r=[onesF.b, nS.b], w=[pb.b])
            RD = fcells.get()
            K.op(DVE, lambda e: e.reciprocal(out=RD.t[0:64, 0:TB], in_=pb.t[0:64, 0:TB]), r=[pb.b], w=[RD.b])
            K.op(DVE, lambda e: e.tensor_tensor(out=h8.t[0:64, h, 0:TB], in0=nS.t[0:64, 0:TB], in1=RD.t[0:64, 0:TB], op=ALU.mult),
                 r=[nS.b, RD.b], w=[h8b[h]])
            fcells.put(nS)
            fcells.put(RD)

        for h in range(8 if 'fox' not in os.environ.get('KSKIP', '') else 0):
            i = h // 2
            acc = ps_acc.get()
            pts = [None] * nj

            def s_step(j):
                pS = rot(ps_st)
                diag = j >= jb0
                c0 = 128 * (j - jb0) if (diag and NT > 1 and j < nj - 1) else 0
                K.op(PE, lambda e: e.matmul(pS.t[:, c0:TB], lhsT=kTc[i].t[:, j * 128:(j + 1) * 128], rhs=arena.t[:, 4 + h, c0:TB], start=True, stop=False),
                     r=[kTb[i][j], ab[4 + h]], w=[pS.b])
                K.op(PE, lambda e: e.matmul(pS.t[:, c0:TB], lhsT=KBt.t[:, j * 128:(j + 1) * 128], rhs=arena.t[:, 12 + h, c0:TB], start=False, stop=not diag),
                     r=[KBb[j], ab[12 + h]], w=[pS.b])
                if diag:
                    d = j - jb0
                    K.op(PE, lambda e: e.matmul(pS.t[:, c0:TB], lhsT=identB.t[:, :], rhs=mnegB.t[:, d * 512 + c0:d * 512 + TB], start=False, stop=True),
                         r=[identB.b, mnegB.b], w=[pS.b])
                pt_ = bcells.get()
                K.op(ACT, lambda e: e.activation(out=pt_.t[:, c0:TB], in_=pS.t[:, c0:TB], func=AF.Exp), r=[pS.b], w=[pt_.b])
                pts[j] = (pt_, c0)

            def pv_step(j):
                pt_, c0 = pts[j]
                K.op(PE, lambda e: e.matmul(acc.t[0:65, c0:TB], lhsT=Vc.t[:, j, h, :], rhs=pt_.t[:, c0:TB], start=(j == 0), stop=(j == nj - 1)),
                     r=[Vb[j], pt_.b], w=[acc.b])
                bcells.put(pt_)

            for j in range(nj):
                s_step(j)
                sp3[0] += 1
                if sp3[0] % sp3[1] == 0:
                    adv3()
                if j == min(3, nj - 1) and pend[0] is not None:
                    pend[0]()
                    pend[0] = None
                if j >= 1:
                    pv_step(j - 1)
            if pend[0] is not None:
                pend[0]()
                pend[0] = None
            pv_step(nj - 1)
            nS_ = finalize_copy(h, acc)
            pend[0] = (lambda h=h, nS_=nS_: finalize(h, nS_))
        if pend[0] is not None:
            pend[0]()
        while adv3():
            pass
        fcells.put(Bt)
        fcells.put(gzi)
        fcells.put(Gt)
        fcells.put(nGt)

        stop_at(7)
        lnm = ps_acc.get()
        lnq = ps_acc.get()
        for oc in range(8):
            wt = wget(g0 + PI_WO + oc // 2)
            wofs = (oc % 2) * 1536
            pp = rot(ps_gen)
            rhs_list = [(mix4.t[:, 0, 0:TB], mix4b[0]), (mix4.t[:, 1, 0:TB], mix4b[1])]
            rhs_list += [(h8.t[:, h, 0:TB], h8b[h]) for h in range(8)]
            rhs_list += [(mix4.t[:, 2, 0:TB], mix4b[2]), (mix4.t[:, 3, 0:TB], mix4b[3])]
            for kc, (rap, rbuf) in enumerate(rhs_list):
                K.op(PE, lambda e, kc=kc, rap=rap, wt=wt, wofs=wofs, pp=pp: e.matmul(pp.t[:, 0:TB], lhsT=wt.t[:, wofs + kc * 128: wofs + (kc + 1) * 128], rhs=rap,
                                                                                  start=(kc == 0), stop=(kc == 11)), r=[wt.b, rbuf], w=[pp.b])
            K.op(DVE, lambda e, oc=oc, pp=pp: e.scalar_tensor_tensor(out=xt.t[:, oc, 0:TB], in0=pp.t[:, 0:TB], scalar=dg1[l].t[:, oc, si:si + 1], in1=xt.t[:, oc, 0:TB],
                                                                    op0=ALU.mult, op1=ALU.add), r=[pp.b, dg1[l].b, xb[oc]], w=[xb[oc]])
            if oc >= 1:
                ln_chunk_stats(oc - 1, TB, lnm, lnq)
        ln_chunk_stats(7, TB, lnm, lnq)
        ln_finish(l, si, TB, 0, 8, False, lnm, lnq)

        stop_at(8)
        for pc in range(11):
            wt = wget(g0 + PI_GU + pc)
            for q in range(2):
                fc_ = pc * 2 + q
                pg = rot(ps_st)
                pu = rot(ps_gen)
                for kc in range(8):
                    K.op(PE, lambda e, kc=kc, q=q, wt=wt, pg=pg: e.matmul(pg.t[:, 0:TB], lhsT=wt.t[:, q * 1024 + kc * 128: q * 1024 + (kc + 1) * 128], rhs=h8.t[:, kc, 0:TB],
                                                                         start=(kc == 0), stop=(kc == 7)), r=[wt.b, h8b[kc]], w=[pg.b])
                for kc in range(8):
                    K.op(PE, lambda e, kc=kc, q=q, wt=wt, pu=pu: e.matmul(pu.t[:, 0:TB], lhsT=wt.t[:, 2048 + q * 1024 + kc * 128: 2048 + q * 1024 + (kc + 1) * 128], rhs=h8.t[:, kc, 0:TB],
                                                                         start=(kc == 0), stop=(kc == 7)), r=[wt.b, h8b[kc]], w=[pu.b])
                sg_ = fcells.get()
                K.op(ACT, lambda e, pg=pg, sg_=sg_: e.activation(out=sg_.t[:, 0:TB], in_=pg.t[:, 0:TB], func=AF.Silu), r=[pg.b], w=[sg_.b])
                K.op(DVE, lambda e, pu=pu, sg_=sg_, fc_=fc_: e.tensor_tensor(out=arena.t[:, fc_, 0:TB], in0=pu.t[:, 0:TB], in1=sg_.t[:, 0:TB], op=ALU.mult),
                     r=[pu.b, sg_.b], w=[ab[fc_]])
                fcells.put(sg_)
        if nxt is not None:
            prologue_pre(*nxt)
        lnm = ps_acc.get()
        lnq = ps_acc.get()
        for oc in range(8):
            wt = wget(g0 + PI_WD + oc)
            pp = rot(ps_gen)
            for kc in range(22):
                K.op(PE, lambda e, kc=kc, wt=wt, pp=pp: e.matmul(pp.t[:, 0:TB], lhsT=wt.t[:, kc * 128:(kc + 1) * 128], rhs=arena.t[:, kc, 0:TB],
                                                                start=(kc == 0), stop=(kc == 21)), r=[wt.b, ab[kc]], w=[pp.b])
            K.op(DVE, lambda e, oc=oc, pp=pp: e.scalar_tensor_tensor(out=xt.t[:, oc, 0:TB], in0=pp.t[:, 0:TB], scalar=dg2[l].t[:, oc, si:si + 1], in1=xt.t[:, oc, 0:TB],
                                                                    op0=ALU.mult, op1=ALU.add), r=[pp.b, dg2[l].b, xb[oc]], w=[xb[oc]])
            if oc >= 1:
                ln_chunk_stats(oc - 1, TB, lnm, lnq)
        ln_chunk_stats(7, TB, lnm, lnq)

        if nxt is not None:
            prologue(*nxt)

        def store_chunk(c):
            def emit():
                dsts = [] if l == NL - 1 else [sq_["xmid_b"][c]]
                K.dma(ACT, x_dst[c, :, tb0:tb0 + TB], xt.t[:, c, 0:TB], r=[xb[c]], w=dsts, prim=xb[c])
            deferred.append(emit)
        ln_finish(l, si, TB, 16, 24, True, lnm, lnq, after_chunk=store_chunk)
        if nxt is None:
            flush_deferred()
        stop_at(9)


    def bias_rows(Ft, pos, n, qb_bufs):
        HI = bcells.get()
        MID = bcells.get()
        LO = bcells.get()
        R1 = fcells.get()
        K.op(DVE, lambda e: e.tensor_copy(out=HI.t[:, 0:n], in_=Ft.t[:, 0:n]), r=[Ft.b], w=[HI.b])
        K.op(DVE, lambda e: e.tensor_tensor(out=R1.t[:, 0:n], in0=Ft.t[:, 0:n], in1=HI.t[:, 0:n], op=ALU.subtract), r=[Ft.b, HI.b], w=[R1.b])
        K.op(DVE, lambda e: e.tensor_copy(out=MID.t[:, 0:n], in_=R1.t[:, 0:n]), r=[R1.b], w=[MID.b])
        K.op(DVE, lambda e: e.tensor_tensor(out=R1.t[:, 0:n], in0=R1.t[:, 0:n], in1=MID.t[:, 0:n], op=ALU.subtract), r=[R1.b, MID.b], w=[R1.b])
        K.op(DVE, lambda e: e.tensor_copy(out=LO.t[:, 0:n], in_=R1.t[:, 0:n]), r=[R1.b], w=[LO.b])
        jb = pos // 128
        kbufs = KBb[jb: jb + (n + 127) // 128]
        T1 = R1
        sc = selc.t

        def combo(col0, out_ap, out_bufs):
            K.op(DVE, lambda e: e.tensor_scalar(out=T1.t[:, 0:n], in0=HI.t[:, 0:n], scalar1=sc[:, col0:col0 + 1], scalar2=sc[:, col0 + 3:col0 + 4], op0=ALU.mult, op1=ALU.add),
                 r=[HI.b, selc.b], w=[T1.b])
            K.op(DVE, lambda e: e.scalar_tensor_tensor(out=T1.t[:, 0:n], in0=MID.t[:, 0:n], scalar=sc[:, col0 + 1:col0 + 2], in1=T1.t[:, 0:n], op0=ALU.mult, op1=ALU.add),
                 r=[MID.b, selc.b, T1.b], w=[T1.b])
            K.op(DVE, lambda e: e.scalar_tensor_tensor(out=out_ap, in0=LO.t[:, 0:n], scalar=sc[:, col0 + 2:col0 + 3], in1=T1.t[:, 0:n], op0=ALU.mult, op1=ALU.add),
                 r=[LO.b, selc.b, T1.b], w=out_bufs)

        combo(0, KBt.t[:, pos:pos + n], kbufs)
        if qb_bufs is not None:
            QA = bcells.get()
            combo(4, QA.t[:, 0:n], [QA.b])
            for h in range(8):
                eng = DVE
                K.op(eng, lambda e, h=h: e.tensor_scalar(out=arena.t[:, 12 + h, 0:n], in0=QA.t[:, 0:n], scalar1=sc[:, 8 + h:9 + h], scalar2=None, op0=ALU.mult),
                     r=[QA.b, selc.b], w=[qb_bufs[h]])
            bcells.put(QA)
        bcells.put(HI)
        bcells.put(MID)
        bcells.put(LO)
        fcells.put(R1)

    blocks = []
    gi_ = 0
    for l in range(NL):
        for sq_ in seqs:
            for blk in range(sq_["T"] // sq_["TB"]):
                blocks.append((l, sq_, blk, gi_ * NPC))
                gi_ += 1
    for sq_ in seqs:
        sq_["ctx"] = {}
    for l in range(NL):
        for sq_ in seqs:
            sq_["ctx"][l] = seq_ctx(l, sq_)
    try:
        prologue(*blocks[0])
        for bi, (l, sq_, blk, g0) in enumerate(blocks):
            if blk == 0:
                seq_init(l, sq_)
            block_main(l, sq_, blk, g0, blocks[bi + 1] if bi + 1 < len(blocks) else None)
            if blk == sq_["T"] // sq_["TB"] - 1:
                seq_final(l, sq_)
    except StopEmit:
        pass
    K.finish()
    es.close()
    return nc


_IN_SIZES = (256, 256, 256, 256, 4, 4, 512, 512, 512, 8, 256, 256)


def _consts():
    c = np.zeros((128, NCON), np.float32)
    c[:, C_ID:C_ID + 128] = np.eye(128, dtype=np.float32)
    s = np.arange(128)[:, None]
    t = np.arange(128)[None, :]
    c[:, C_M01:C_M01 + 128] = (s <= t).astype(np.float32)
    for h in range(8):
        for r in range(6):
            p = 6 * h + r
            if r < 3:
                c[p, C_SEL + r] = -1.0
            else:
                c[p, C_SEL + 3] = 1.0
            if r < 3:
                c[p, C_SEL + 7] = 1.0
            else:
                c[p, C_SEL + 4 + (r - 3)] = 1.0
            c[p, C_SEL + 8 + h] = 1.0
    c[:, C_VAL:C_VAL + DEC] = 1.0
    tt = np.arange(512)[None, :]
    for d in range(4):
        c[:, C_MNEG + d * 512:C_MNEG + (d + 1) * 512] = np.where(tt >= 128 * d + s, 0.0, NEG).astype(np.float32)
    return c


def _layer_weights(w_in, w_o, w_gate, w_up, w_down):
    offs = np.cumsum((0,) + _IN_SIZES)
    col = {n: (offs[i], offs[i + 1]) for i, n in enumerate(["mq", "mk", "mv", "mo", "mi", "mf", "fq", "fk", "fv", "ff", "gu", "gv"])}
    W = np.zeros((128, FTOT_PAD), np.float32)

    def put_lhsT(off, mat):
        kk = mat.shape[0] // 128
        W[:, off:off + kk * 128] = mat.reshape(kk, 128, 128).transpose(1, 0, 2).reshape(128, kk * 128)

    def put_rhs(off, mat):
        n = mat.shape[1]
        W[:, off:off + 8 * n] = mat.reshape(8, 128, n).transpose(1, 0, 2).reshape(128, 8 * n)

    sl = lambda n: w_in[:, col[n][0]:col[n][1]]
    tiles = []
    for n in ("mq", "mk"):
        for j in range(2):
            tiles.append(sl(n)[:, j * 128:(j + 1) * 128])
    gff = np.zeros((1024, 128), np.float32)
    for h in range(8):
        for r in range(6):
            gff[:, 6 * h + r] = sl("ff")[:, h]
    gi = np.zeros((1024, 128), np.float32)
    gi[:, 0:4] = sl("mi")
    gf = np.zeros((1024, 128), np.float32)
    gf[:, 0:4] = sl("mf")
    tiles += [gff, gi, gf]
    for n in ("fq", "fk"):
        for j in range(4):
            tiles.append(sl(n)[:, j * 128:(j + 1) * 128])
    for i, tl in enumerate(tiles):
        put_lhsT(W1_OFF + i * 1024, tl)
    o = W2_OFF
    for grp in (np.concatenate([sl("mk"), sl("mv")], 1), sl("mo"), np.concatenate([sl("gu"), sl("gv")], 1), sl("fv")):
        put_rhs(o, grp)
        o += 8 * grp.shape[1]
    for oc in range(8):
        wo = w_o[:, oc * 128:(oc + 1) * 128]
        ch = [wo[0:128], wo[128:256]]
        for h in range(8):
            z = np.zeros((128, 128), np.float32)
            z[0:64] = wo[256 + 64 * h:256 + 64 * h + 64]
            ch.append(z)
        ch += [wo[768:896], wo[896:1024]]
        put_lhsT(WO_OFF + oc * 1536, np.concatenate(ch, 0))
    for pc in range(11):
        for q in range(2):
            fc_ = pc * 2 + q
            put_lhsT(GU_OFF + pc * 4096 + q * 1024, w_gate[:, fc_ * 128:(fc_ + 1) * 128])
            put_lhsT(GU_OFF + pc * 4096 + 2048 + q * 1024, w_up[:, fc_ * 128:(fc_ + 1) * 128])
    for oc in range(8):
        put_lhsT(WD_OFF + oc * 2816, w_down[:, oc * 128:(oc + 1) * 128])
    return W


def _layer_params(l, p):
    a = np.zeros((128, NLP), np.float32)
    a[:, LP_BADA:LP_BADA + 48] = p["b_ada"][l].reshape(48, 128).T
    for i, n in enumerate(("ln1_g", "ln1_b", "ln2_g", "ln2_b")):
        a[:, LP_LN + 8 * i:LP_LN + 8 * i + 8] = p[n][l].reshape(8, 128).T
    for h in range(8):
        a[6 * h:6 * h + 6, LP_BFF] = p["b_fox_f"][l][h]
    a[0:4, LP_BI] = p["b_mlstm_i"][l]
    a[0:4, LP_BF] = p["b_mlstm_f"][l]
    a[:, LP_BS:LP_BS + 4] = p["gmlp_bs"][l].T
    a[:, LP_GM:LP_GM + 256] = p["mlstm_norm_g"][l][None, :]
    a[:, LP_GG:LP_GG + 256] = p["gmlp_ln_g"][l][None, :]
    a[:, LP_GB:LP_GB + 256] = p["gmlp_ln_b"][l][None, :]
    a[:, LP_WS:LP_WS + 512] = p["gmlp_ws"][l].transpose(2, 0, 1).reshape(128, 512)
    return a


_NC_CACHE = {}


def run(inp, n_cores, NPS, NSS):
    f = lambda k: np.asarray(inp[k], np.float32)
    x_prompt = f("x_prompt")
    S = x_prompt.shape[1]
    key = (S, NPS, NSS)
    if key not in _NC_CACHE:
        _NC_CACHE[key] = build(S, NPS, NSS)
    nc = _NC_CACHE[key]
    p = {k: f(k) for k in inp}
    consts = _consts()
    lpar = np.stack([_layer_params(l, p) for l in range(NL)])
    wall = np.stack([_layer_weights(p["w_in"][l], p["w_o"][l], p["w_gate"][l], p["w_up"][l], p["w_down"][l]) for l in range(NL)])
    wada = np.ascontiguousarray(p["w_ada"].reshape(NL, 8, 128, 12, 512).transpose(0, 3, 2, 1, 4).reshape(NL, 12, 128, 4096))
    B = x_prompt.shape[0]
    xpT = np.ascontiguousarray(x_prompt.reshape(B, S, 8, 128).transpose(0, 2, 3, 1))
    xs_pad = np.zeros((p["x_sample"].shape[0], 128, 1024), np.float32)
    xs_pad[:, :DEC] = p["x_sample"]
    xsT = np.ascontiguousarray(xs_pad.reshape(-1, 128, 8, 128).transpose(0, 2, 3, 1))
    ckT = np.ascontiguousarray(p["cache_fox_k"].reshape(NL, -1, PAST, 4, 128).transpose(0, 1, 3, 4, 2))
    cvv = p["cache_fox_v"].reshape(NL, -1, PAST, 512)
    clfT = np.zeros((NL, cvv.shape[1], 128, PAST), np.float32)
    lfT = p["cache_fox_logf"].transpose(0, 1, 3, 2)
    for h in range(8):
        clfT[:, :, 6 * h:6 * h + 6, :] = lfT[:, :, h:h + 1, :]
    Bs = cvv.shape[1]
    sCn = np.concatenate([p["state_mlstm_C"], p["state_mlstm_n"][..., None]], -1).reshape(NL, Bs, 2, 128, 65)
    sMm = np.zeros((NL, Bs, 128, 1), np.float32)
    sMm[:, :, 0:4, 0] = p["state_mlstm_m"]
    in_maps = []
    for c in range(n_cores):
        ps_, ss_ = slice(c * NPS, (c + 1) * NPS), slice(c * NSS, (c + 1) * NSS)
        cc = np.concatenate([p["c_prompt"][ps_], p["c_sample"][ss_]], 0)
        cTt = np.ascontiguousarray(cc.reshape(-1, 8, 128).transpose(2, 1, 0).reshape(128, -1))
        in_maps.append(dict(
            xp=xpT[ps_], xs=xsT[ss_], cT=cTt,
            ck=np.ascontiguousarray(ckT[:, ss_]), cv=np.ascontiguousarray(cvv[:, ss_]), clf=np.ascontiguousarray(clfT[:, ss_]),
            sC=np.ascontiguousarray(sCn[:, ss_]), sM=np.ascontiguousarray(sMm[:, ss_]),
            consts=consts, lpar=lpar, wall=wall, wada=wada))
    res = run_bass_kernel_spmd(nc, in_maps, core_ids=list(range(n_cores)))
    R = res.results

    def cat(name, axis):
        return np.concatenate([r[name] for r in R], axis=axis)

    yp = cat("yp", 0)
    y_prompt = np.ascontiguousarray(yp.transpose(0, 3, 1, 2)).reshape(B, S, 1024)
    ys = cat("ys", 0)
    y_sample = np.ascontiguousarray(ys.transpose(0, 3, 1, 2)).reshape(Bs, 128, 1024)[:, :DEC]
    pk = cat("pk", 1)
    p_k = np.ascontiguousarray(pk.transpose(0, 1, 4, 2, 3)).reshape(NL, B, S, 8, 64)
    p_v = cat("pv", 1).reshape(NL, B, S, 8, 64)
    plf = cat("plf", 1)
    p_lf = np.ascontiguousarray(plf[:, :, 0:48:6, :].transpose(0, 1, 3, 2))
    pC = cat("pC", 1)
    pC4 = pC.reshape(NL, B, 4, 64, 65)
    p_C = np.ascontiguousarray(pC4[..., 0:64])
    p_n = np.ascontiguousarray(pC4[..., 64])
    p_m = np.ascontiguousarray(cat("pM", 1)[:, :, 0:4, 0])
    sk_ = cat("sk", 1)
    s_k = np.ascontiguousarray(sk_.transpose(0, 1, 4, 2, 3)).reshape(NL, Bs, 128, 8, 64)[:, :, :DEC]
    s_v = np.ascontiguousarray(cat("sv", 1).reshape(NL, Bs, 128, 8, 64)[:, :, :DEC])
    slf_ = cat("slf", 1)
    s_lf = np.ascontiguousarray(slf_[:, :, 0:48:6, :DEC].transpose(0, 1, 3, 2))
    sC_ = cat("sCo", 1).reshape(NL, Bs, 4, 64, 65)
    s_C = np.ascontiguousarray(sC_[..., 0:64])
    s_n = np.ascontiguousarray(sC_[..., 64])
    s_m = np.ascontiguousarray(cat("sMo", 1)[:, :, 0:4, 0])
    s_gv = np.ascontiguousarray(cat("sgv", 1)[:, :, :DEC, :])
    outs = (y_prompt, y_sample, p_k, p_v, p_lf, p_C, p_n, p_m, s_k, s_v, s_lf, s_C, s_n, s_m, s_gv)
    return tuple(np.ascontiguousarray(o, dtype=np.float32) for o in outs)


def kernel(**inputs):
    return run(inputs, 8, 2, 2)
```

```python
import numpy as np
import ml_dtypes
from contextlib import ExitStack
import concourse.bass as bass
import concourse.mybir as mybir
from concourse.bass_utils import run_bass_kernel_spmd

F32 = mybir.dt.float32
BF16 = mybir.dt.bfloat16
AF = mybir.ActivationFunctionType
ALU = mybir.AluOpType
AX = mybir.AxisListType

NL = 2
D = 1024
KC = 8
DFF = 2816
FC = 22
PAST = 1024
DEC = 32
ALPHA = (2 * NL) ** 0.25
EPS_LN = 1e-5 / (ALPHA * ALPHA)
EPS_G = 1e-5
EPS_HN = 1e-6
NEG = -30000.0

W1_OFF = 0
W2_OFF = W1_OFF + 15 * 1024
W2_SIZES = [4096, 2048, 4096, 4096]
WO_OFF = W2_OFF + sum(W2_SIZES)
GU_OFF = WO_OFF + 8 * 1536
WD_OFF = GU_OFF + 11 * 4096
FTOT = WD_OFF + 8 * 2816
CONV = 4096
FTOT_PAD = ((FTOT + CONV - 1) // CONV) * CONV


def weight_pieces():
    pcs = []
    for i in range(0, 15, 4):
        n = min(4, 15 - i)
        pcs.append((0, W1_OFF + i * 1024, n * 1024))
        if i == 0:
            pcs.append((1, 0, 4096))
    w2o = [W2_OFF + sum(W2_SIZES[:i]) for i in range(4)]
    for i in (1, 2, 3, 0):
        pcs.append((0, w2o[i], W2_SIZES[i]))
    for i in range(0, 8, 2):
        pcs.append((0, WO_OFF + i * 1536, 2 * 1536))
    for i in range(11):
        pcs.append((0, GU_OFF + i * 4096, 4096))
    for i in range(8):
        pcs.append((0, WD_OFF + i * 2816, 2816))
    return pcs


PI_W1 = [0, 2, 3, 4]
PI_W1LO = 1
PI_W2 = 5
PI_WO = 9
PI_GU = 13
PI_WD = 24


C_ID = 0
C_M01 = 128
C_SEL = 256
C_VAL = 272
C_MNEG = 400
NCON = C_MNEG + 4 * 512

LP_BADA = 0
LP_LN = 48
LP_BFF = 80
LP_BI = 81
LP_BF = 82
LP_BS = 83
LP_GM = 87
LP_GG = 343
LP_GB = 599
LP_WS = 855
NLP = LP_WS + 512


class Sem:
    def __init__(self, nc, es, name):
        self.sem = es.enter_context(nc.semaphore(name))
        self.n = 0


class Eng(Sem):
    def __init__(self, nc, es, name, h):
        super().__init__(nc, es, "e_" + name)
        self.h = h
        self.name = name
        self.seen = {}


class Buf:
    __slots__ = ("name", "w", "r", "dsem", "excl")

    def __init__(self, name):
        self.name = name
        self.w = None
        self.r = {}
        self.dsem = None
        self.excl = False


class KB:
    def __init__(self, nc, es):
        self.nc = nc
        self.es = es
        self.PE = Eng(nc, es, "pe", nc.tensor)
        self.ACT = Eng(nc, es, "act", nc.scalar)
        self.DVE = Eng(nc, es, "dve", nc.vector)
        self.POOL = Eng(nc, es, "pool", nc.gpsimd)
        self.SP = Eng(nc, es, "sp", nc.sync)
        self.engs = [self.PE, self.ACT, self.DVE, self.POOL, self.SP]
        self.dsems = []
        self.nsem = 5

    def _wait(self, eng, dep):
        so, v = dep
        if eng.seen.get(id(so), 0) >= v:
            return
        eng.h.wait_ge(so.sem, v)
        eng.seen[id(so)] = v

    def _deps(self, eng, r, w):
        for b in r:
            if b.w is not None:
                if not (b.w[0] is eng and eng is self.PE):
                    self._wait(eng, b.w)
            if b.excl:
                for d in b.r.values():
                    if d[0] is not eng:
                        self._wait(eng, d)
        for b in w:
            if b.w is not None:
                if not (b.w[0] is eng and eng is self.PE):
                    self._wait(eng, b.w)
            for d in b.r.values():
                if not (d[0] is eng and eng is self.PE):
                    self._wait(eng, d)

    def op(self, eng, ins_fn, r=(), w=()):
        self._deps(eng, r, w)
        ins = ins_fn(eng.h)
        eng.n += 1
        ins.then_inc(eng.sem, 1)
        tok = (eng, eng.n)
        eng.seen[id(eng)] = eng.n if eng is self.PE else eng.seen.get(id(eng), 0)
        for b in w:
            b.w = tok
            b.r = {}
        for b in r:
            b.r[id(eng)] = tok

    def dma(self, eng, out_ap, in_ap, r=(), w=(), prim=None):
        self._deps(eng, r, w)
        if prim.dsem is None:
            prim.dsem = Sem(self.nc, self.es, "d%d" % self.nsem)
            self.nsem += 1
            self.dsems.append(prim.dsem)
        so = prim.dsem
        ins = eng.h.dma_start(out=out_ap, in_=in_ap)
        so.n += 16
        ins.then_inc(so.sem, 16)
        tok = (so, so.n)
        for b in w:
            b.w = tok
            b.r = {}
        for b in r:
            b.r[id(so)] = tok

    def barrier(self):
        for e in self.engs:
            for o in self.engs:
                if o is not e and o.n > 0:
                    self._wait(e, (o, o.n))
            for so in self.dsems:
                if so.n > 0:
                    self._wait(e, (so, so.n))

    def finish(self):
        e = self.SP
        for so in self.dsems:
            if so.n > 0:
                self._wait(e, (so, so.n))
        for o in self.engs:
            if o is not e and o.n > 0:
                self._wait(e, (o, o.n))


import os
_STOP = float(os.environ.get("KSTOP", "999"))


class StopEmit(Exception):
    pass


def stop_at(n):
    if _STOP <= n:
        raise StopEmit()


class Pool:
    def __init__(self, cells):
        self.free = list(cells)

    def get(self):
        assert self.free, "pool exhausted"
        return self.free.pop(0)

    def put(self, c):
        self.free.append(c)


class T:
    def __init__(self, ap_tensor, name):
        self.t = ap_tensor
        self.b = Buf(name)


def build(S, NPS, NSS):
    NSEQ = NPS + NSS
    SC = max(S, PAST + 128)
    NB = SC // 128
    nc = bass.Bass("TRN2", target_bir_lowering=False)

    def din(name, shape, dt=F32):
        return nc.dram_tensor(name, list(shape), dt, kind="ExternalInput").ap()

    def dout(name, shape, dt=F32):
        return nc.dram_tensor(name, list(shape), dt, kind="ExternalOutput").ap()

    def dscr(name, shape, dt=F32):
        return nc.dram_tensor(name, list(shape), dt, kind="Internal").ap()

    xp = din("xp", [NPS, 8, 128, S])
    xs = din("xs", [NSS, 8, 128, 128])
    cT = din("cT", [128, 8 * NSEQ])
    ck = din("ck", [NL, NSS, 4, 128, PAST])
    cv = din("cv", [NL, NSS, PAST, 512])
    clf = din("clf", [NL, NSS, 128, PAST])
    sC = din("sC", [NL, NSS, 2, 128, 65])
    sM = din("sM", [NL, NSS, 128, 1])
    consts = din("consts", [128, NCON])
    lpar = din("lpar", [NL, 128, NLP])
    wall = din("wall", [NL, 128, FTOT_PAD])
    wada = din("wada", [NL, 12, 128, 4096])

    yp = dout("yp", [NPS, 8, 128, S])
    ys = dout("ys", [NSS, 8, 128, 128])
    pk = dout("pk", [NL, NPS, 4, 128, S])
    pv = dout("pv", [NL, NPS, S, 512])
    plf = dout("plf", [NL, NPS, 128, S])
    pC = dout("pC", [NL, NPS, 2, 128, 65])
    pM = dout("pM", [NL, NPS, 128, 1])
    sk = dout("sk", [NL, NSS, 4, 128, 128])
    sv = dout("sv", [NL, NSS, 128, 512])
    slf = dout("slf", [NL, NSS, 128, 128])
    sCo = dout("sCo", [NL, NSS, 2, 128, 65])
    sMo = dout("sMo", [NL, NSS, 128, 1])
    sgv = dout("sgv", [NL, NSS, 128, 256])

    wsc = dscr("wsc", [NL, 128, FTOT_PAD], BF16)
    wlo = dscr("wlo", [NL, 128, CONV], BF16)
    xmp = dscr("xmp", [NPS, 8, 128, S])
    xms = dscr("xms", [NSS, 8, 128, 128])

    es = ExitStack()
    K = KB(nc, es)
    PE, ACT, DVE, POOL, SP = K.PE, K.ACT, K.DVE, K.POOL, K.SP
    STQ = SP if os.environ.get('KSTQ', 'sp') == 'sp' else POOL
    cnt = [0]

    def sb(shape, dt=F32, name=None, stack=None):
        cnt[0] += 1
        nm = "%s_%d" % (name or "t", cnt[0])
        t = (stack or es).enter_context(nc.sbuf_tensor(nm, list(shape), dt))
        return T(t, nm)

    identF = sb([128, 128], F32, "identF")
    identB = sb([128, 128], BF16, "identB")
    mask01 = sb([128, 128], F32, "mask01")
    validF = sb([128, 128], F32, "validF")
    selc = sb([128, 16], F32, "selc")
    mnegB = sb([128, 4 * 512], BF16, "mnegB")
    onesM = sb([128, 128], BF16, "onesM")
    onesF = sb([128, 128], F32, "onesF")
    ones5 = sb([128, 512], F32, "ones5")
    lp = [sb([128, LP_WS], F32, "lp%d" % l) for l in range(NL)]
    wsTb = [sb([128, 512], BF16, "wsTb%d" % l) for l in range(NL)]
    modT = [sb([128, 48, NSEQ], F32, "modT%d" % l) for l in range(NL)]
    dsc1 = [sb([128, 8, NSEQ], F32, "dsc1") for l in range(NL)]
    dg1 = [sb([128, 8, NSEQ], F32, "dg1") for l in range(NL)]
    dA2 = [sb([128, 8, NSEQ], F32, "dA2") for l in range(NL)]
    dB2 = [sb([128, 8, NSEQ], F32, "dB2") for l in range(NL)]
    dg2 = [sb([128, 8, NSEQ], F32, "dg2") for l in range(NL)]
    negb = [sb([128, 2], F32, "negb") for l in range(NL)]

    psum = []
    for i in range(8):
        t = es.enter_context(nc.psum_tensor("ps%d" % i, [128, 512], F32))
        psum.append(T(t, "ps%d" % i))
        psum[-1].b.excl = True
    ps_acc = Pool(psum[0:2])
    ps_st = Pool(psum[2:5])
    ps_gen = Pool(psum[5:8])

    def rot(pool):
        c = pool.get()
        pool.put(c)
        return c

    rot8_state = [0]

    def rot8():
        pools = [ps_gen, ps_st, ps_acc]
        for _ in range(3):
            p = pools[rot8_state[0] % 3]
            rot8_state[0] += 1
            if p.free:
                return rot(p)
        raise AssertionError("no free PSUM bank")

    with ExitStack() as s0:
        cf = sb([128, NCON], F32, "cf", s0)
        K.dma(SP, cf.t[:, :], consts[:, :], w=[cf.b], prim=cf.b)
        for l in range(NL):
            K.dma(SP, lp[l].t[:, :], lpar[l, :, 0:LP_WS], w=[lp[l].b], prim=lp[l].b)
        wsF = [sb([128, 512], F32, "wsF", s0) for l in range(NL)]
        for l in range(NL):
            K.dma(SP, wsF[l].t[:, :], lpar[l, :, LP_WS:LP_WS + 512], w=[wsF[l].b], prim=wsF[l].b)
        ctile = sb([128, 8 * NSEQ], F32, "ctile", s0)
        K.dma(SP, ctile.t[:, :], cT[:, :], w=[ctile.b], prim=ctile.b)

        K.op(DVE, lambda e: e.tensor_copy(out=identF.t[:, :], in_=cf.t[:, C_ID:C_ID + 128]), r=[cf.b], w=[identF.b])
        K.op(DVE, lambda e: e.tensor_copy(out=identB.t[:, :], in_=cf.t[:, C_ID:C_ID + 128]), r=[cf.b], w=[identB.b])
        K.op(DVE, lambda e: e.tensor_copy(out=mask01.t[:, :], in_=cf.t[:, C_M01:C_M01 + 128]), r=[cf.b], w=[mask01.b])
        K.op(DVE, lambda e: e.tensor_copy(out=validF.t[:, :], in_=cf.t[:, C_VAL:C_VAL + 128]), r=[cf.b], w=[validF.b])
        K.op(DVE, lambda e: e.tensor_copy(out=selc.t[:, :], in_=cf.t[:, C_SEL:C_SEL + 16]), r=[cf.b], w=[selc.b])
        K.op(DVE, lambda e: e.tensor_copy(out=mnegB.t[:, :], in_=cf.t[:, C_MNEG:C_MNEG + 2048]), r=[cf.b], w=[mnegB.b])
        K.op(POOL, lambda e: e.memset(onesM.t[:, :], 1.0 / 1024.0), w=[onesM.b])
        K.op(POOL, lambda e: e.memset(onesF.t[:, :], 1.0), w=[onesF.b])
        K.op(POOL, lambda e: e.memset(ones5.t[:, :], 1.0), w=[ones5.b])
        for l in range(NL):
            K.op(DVE, lambda e, l=l: e.tensor_tensor(
                out=wsTb[l].t[:, :].rearrange("p (g t) -> p g t", g=4),
                in0=wsF[l].t[:, :].rearrange("p (g t) -> p g t", g=4),
                in1=cf.t[:, C_M01:C_M01 + 128].rearrange("p (o t) -> p o t", o=1).to_broadcast([128, 4, 128]),
                op=ALU.mult), r=[wsF[l].b, cf.b], w=[wsTb[l].b])
            K.op(DVE, lambda e, l=l: e.tensor_scalar(out=negb[l].t[:, 0:1], in0=lp[l].t[:, LP_BFF:LP_BFF + 1],
                                                     scalar1=-1.0, scalar2=None, op0=ALU.mult),
                 r=[lp[l].b], w=[negb[l].b])
            K.op(DVE, lambda e, l=l: e.tensor_scalar(out=negb[l].t[:, 1:2], in0=lp[l].t[:, LP_BF:LP_BF + 1],
                                                     scalar1=-1.0, scalar2=None, op0=ALU.mult),
                 r=[lp[l].b], w=[negb[l].b])

        scT = sb([128, 8 * NSEQ], F32, "scT", s0)
        K.op(ACT, lambda e: e.activation(out=scT.t[:, :], in_=ctile.t[:, :], func=AF.Silu), r=[ctile.b], w=[scT.b])
        wa = [sb([128, 4096], F32, "wa", s0) for i in range(2)]
        for l in range(NL):
            pm = rot(ps_gen)
            for j in range(12):
                wt = wa[(l * 12 + j) % 2]
                K.dma(SP, wt.t[:, :], wada[l, j, :, :], w=[wt.b], prim=wt.b)
                for q in range(4):
                    oc = 4 * j + q
                    for kc in range(8):
                        K.op(PE, lambda e, wt=wt, q=q, kc=kc, oc=oc, pm=pm: e.matmul(
                            pm.t[:, oc * NSEQ:(oc + 1) * NSEQ],
                            lhsT=wt.t[:, kc * 512 + q * 128: kc * 512 + (q + 1) * 128],
                            rhs=scT.t[:, kc * NSEQ:(kc + 1) * NSEQ],
                            start=(kc == 0), stop=(kc == 7)), r=[wt.b, scT.b], w=[pm.b])
            K.op(DVE, lambda e, l=l, pm=pm: e.tensor_tensor(
                out=modT[l].t[:, :, :],
                in0=pm.t[:, 0:48 * NSEQ].rearrange("p (c s) -> p c s", s=NSEQ),
                in1=lp[l].t[:, LP_BADA:LP_BADA + 48].rearrange("p (c o) -> p c o", o=1).to_broadcast([128, 48, NSEQ]),
                op=ALU.add), r=[pm.b, lp[l].b], w=[modT[l].b])
            m = modT[l].t
            lng = lambda off: lp[l].t[:, LP_LN + off:LP_LN + off + 8].rearrange("p (c o) -> p c o", o=1).to_broadcast([128, 8, NSEQ])
            K.op(DVE, lambda e, l=l, m=m: e.tensor_scalar(out=dsc1[l].t[:, :, :], in0=m[:, 8:16, :], scalar1=1.0, scalar2=None, op0=ALU.add),
                 r=[modT[l].b], w=[dsc1[l].b])
            K.op(DVE, lambda e, l=l, m=m: e.tensor_scalar(out=dg1[l].t[:, :, :], in0=m[:, 16:24, :], scalar1=1.0, scalar2=1.0 / ALPHA, op0=ALU.add, op1=ALU.mult),
                 r=[modT[l].b], w=[dg1[l].b])
            K.op(DVE, lambda e, l=l, m=m: e.tensor_scalar(out=dg2[l].t[:, :, :], in0=m[:, 40:48, :], scalar1=1.0, scalar2=1.0 / ALPHA, op0=ALU.add, op1=ALU.mult),
                 r=[modT[l].b], w=[dg2[l].b])
            K.op(DVE, lambda e, l=l, m=m: e.tensor_scalar(out=dA2[l].t[:, :, :], in0=m[:, 32:40, :], scalar1=1.0, scalar2=None, op0=ALU.add),
                 r=[modT[l].b], w=[dA2[l].b])
            K.op(DVE, lambda e, l=l: e.tensor_tensor(out=dB2[l].t[:, :, :], in0=dA2[l].t[:, :, :], in1=lng(8), op=ALU.mult),
                 r=[dA2[l].b, lp[l].b], w=[dB2[l].b])
            K.op(DVE, lambda e, l=l, m=m: e.tensor_tensor(out=dB2[l].t[:, :, :], in0=dB2[l].t[:, :, :], in1=m[:, 24:32, :], op=ALU.add),
                 r=[dB2[l].b, modT[l].b], w=[dB2[l].b])
            K.op(DVE, lambda e, l=l: e.tensor_tensor(out=dA2[l].t[:, :, :], in0=dA2[l].t[:, :, :], in1=lng(0), op=ALU.mult),
                 r=[dA2[l].b, lp[l].b], w=[dA2[l].b])

        cin = [sb([128, CONV], F32, "cin", s0) for i in range(4)]
        cout = [sb([128, CONV], BF16, "cout", s0) for i in range(4)]
        clo = sb([128, CONV], BF16, "clo", s0)
        wscb = [Buf("wsc%d" % l) for l in range(NL)]
        nconv = FTOT_PAD // CONV
        ci = 0
        cengs = [ACT, DVE, POOL]
        for l in range(NL):
            for j in range(nconv):
                a = cin[ci % 4]
                o = cout[ci % 4]
                K.dma(SP, a.t[:, :], wall[l, :, j * CONV:(j + 1) * CONV], w=[a.b], prim=a.b)
                eng = ACT if ci % 2 == 0 else DVE
                if eng is ACT:
                    K.op(ACT, lambda e, a=a, o=o: e.activation(out=o.t[:, :], in_=a.t[:, :], func=AF.Copy), r=[a.b], w=[o.b])
                else:
                    K.op(eng, lambda e, a=a, o=o: e.tensor_copy(out=o.t[:, :], in_=a.t[:, :]), r=[a.b], w=[o.b])
                K.dma(ACT, wsc[l, :, j * CONV:(j + 1) * CONV], o.t[:, :], r=[o.b], w=[], prim=o.b)
                if j == 0:
                    K.op(DVE, lambda e, a=a, o=o: e.tensor_tensor(out=clo.t[:, :], in0=a.t[:, :], in1=o.t[:, :], op=ALU.subtract), r=[a.b, o.b], w=[clo.b])
                    K.dma(ACT, wlo[l, :, :], clo.t[:, :], r=[clo.b], w=[], prim=clo.b)
                wscb[l].w = o.b.r[id(o.b.dsem)]
                ci += 1
        K.barrier()

    if _STOP <= 0:
        K.finish()
        es.close()
        return nc
    kTc = [sb([128, SC], BF16, "kTc%d" % i) for i in range(4)]
    KBt = sb([128, SC], BF16, "KBt")
    Vc = sb([128, NB, 8, 65], BF16, "Vc")
    kTb = [[Buf("kT%d_%d" % (i, j)) for j in range(NB)] for i in range(4)]
    KBb = [Buf("KB%d" % j) for j in range(NB)]
    Vb = [Buf("V%d" % j) for j in range(NB)]
    Ctil = [sb([128, 65], F32, "Ctil") for i in range(2)]
    Cp = [sb([128, 65], F32, "Cp") for i in range(2)]
    Cb = [sb([128, 65], BF16, "Cb") for i in range(2)]
    carF = sb([128, 1], F32, "carF")
    carB = sb([128, 1], F32, "carB")
    carG = sb([128, 1], F32, "carG")
    xt = sb([128, 8, 512], F32, "xt")
    xb = [Buf("x%d" % i) for i in range(8)]
    arena = sb([128, 22, 512], BF16, "arena")
    ab = [Buf("ar%d" % i) for i in range(22)]
    h8 = sb([128, 8, 512], BF16, "h8")
    h8b = [Buf("h8_%d" % i) for i in range(8)]
    mix4 = sb([128, 4, 512], BF16, "mix4")
    mix4b = [Buf("mx%d" % i) for i in range(4)]
    fcells = Pool([sb([128, 512], F32, "fc") for i in range(6)])
    kTml = [sb([128, 512], BF16, "kTml") for i in range(2)]
    bcells = Pool([sb([128, 512], BF16, "bc") for i in range(6)])
    wslot = [sb([128, 4096], BF16, "wslot") for i in range(3)]
    E1t = sb([128, 128], F32, "E1t")
    CLt = sb([128, 128], F32, "CLt")
    lamc = sb([128, 1], F32, "lamc")
    Dl = sb([128, 4], F32, "Dl")
    lamB = sb([128, 4], F32, "lamB")
    tokS = [sb([128, 8], F32, "tokS") for i in range(4)]
    ktok = [sb([128, 256], BF16, "ktok") for i in range(4)]
    vaug = [sb([128, 4, 65], BF16, "vaug") for i in range(4)]
    sgt = [sb([128, 256], BF16, "sgt") for i in range(4)]
    ut = [sb([128, 256], BF16, "ut") for i in range(4)]
    vnt = sb([128, 256], F32, "vnt")
    vnb = [sb([128, 256], BF16, "vnb") for i in range(4)]
    hh = sb([128, 256], F32, "hh")
    sq = sb([128, 256], F32, "sq")
    hmb = sb([128, 256], BF16, "hmb")
    hgb = sb([128, 256], BF16, "hgb")
    st6 = sb([128, 6], F32, "st6")
    st2 = sb([128, 2], F32, "st2")
    sm4 = [sb([128, 4], F32, "sm4_%d" % i) for i in range(8)]
    mout = sb([128, 1], F32, "mout")
    gprev = sb([128, 1], F32, "gprev")
    mhalf = sb([128, 4], F32, "mhalf")
    amt = sb([128, 128], F32, "amt")

    for tt in (E1t, CLt):
        K.op(POOL, lambda e, tt=tt: e.memset(tt.t[:, :], 0.0), w=[tt.b])
    K.op(POOL, lambda e: e.memset(lamc.t[:, :], 0.0), w=[lamc.b])
    K.op(POOL, lambda e: e.memset(mhalf.t[:, :], -0.5), w=[mhalf.b])
    K.op(POOL, lambda e: e.memset(Vc.t[:, :, :, :], 0.0), w=Vb)
    K.op(POOL, lambda e: e.memset(Vc.t[:, :, :, 64:65], 1.0), w=Vb)
    for i in range(4):
        K.op(POOL, lambda e, i=i: e.memset(kTc[i].t[:, :], 0.0), w=kTb[i])
    K.op(POOL, lambda e: e.memset(KBt.t[:, :], 0.0), w=KBb)
    K.op(POOL, lambda e: e.memset(h8.t[:, :, :], 0.0), w=h8b)

    if _STOP <= 1:
        K.finish()
        es.close()
        return nc
    pieces = weight_pieces()
    NPC = len(pieces)
    passes = []
    seqs = []
    for s in range(NPS):
        seqs.append(dict(kind="p", idx=s, mi=s, T=S, TB=512, t0=0))
    for s in range(NSS):
        seqs.append(dict(kind="s", idx=s, mi=NPS + s, T=128, TB=128, t0=PAST))
    for l in range(NL):
        for sq_ in seqs:
            for blk in range(sq_["T"] // sq_["TB"]):
                passes.append(l)
    wstate = dict(issued=0)

    def wissue(upto):
        while wstate["issued"] <= upto and wstate["issued"] < len(passes) * NPC:
            g = wstate["issued"]
            l = passes[g // NPC]
            src, off, sz = pieces[g % NPC]
            slot = wslot[g % 3]
            K.dma(SP, slot.t[:, 0:sz], (wsc if src == 0 else wlo)[l, :, off:off + sz], w=[slot.b], prim=slot.b)
            wstate["issued"] += 1

    def wget(g, ahead=2):
        wissue(g + ahead)
        return wslot[g % 3]

    gpass = [0]

    def ev(i):
        return ACT if (i % 2 == 0) else DVE

    def copy_on(eng, out_ap, in_ap, r, w, scale=None):
        if eng is ACT:
            if scale is None:
                K.op(ACT, lambda e: e.activation(out=out_ap, in_=in_ap, func=AF.Copy), r=r, w=w)
            else:
                K.op(ACT, lambda e: e.activation(out=out_ap, in_=in_ap, func=AF.Copy, scale=scale), r=r, w=w)
        else:
            if scale is None:
                K.op(eng, lambda e: e.tensor_copy(out=out_ap, in_=in_ap), r=r, w=w)
            else:
                K.op(eng, lambda e: e.tensor_scalar(out=out_ap, in0=in_ap, scalar1=scale, scalar2=None, op0=ALU.mult), r=r, w=w)

    def ln_chunk_stats(c, TB, pmean, pmsq):
        rb = bcells.get()
        rs = bcells.get()
        K.op(ACT, lambda e: e.activation(out=rb.t[:, 0:TB], in_=xt.t[:, c, 0:TB], func=AF.Copy), r=[xb[c]], w=[rb.b])
        K.op(ACT, lambda e: e.activation(out=rs.t[:, 0:TB], in_=xt.t[:, c, 0:TB], func=AF.Square), r=[xb[c]], w=[rs.b])
        K.op(PE, lambda e: e.matmul(pmean.t[:, 0:TB], lhsT=onesM.t[:, :], rhs=rb.t[:, 0:TB], start=(c == 0), stop=(c == 7)),
             r=[onesM.b, rb.b], w=[pmean.b])
        K.op(PE, lambda e: e.matmul(pmsq.t[:, 0:TB], lhsT=onesM.t[:, :], rhs=rs.t[:, 0:TB], start=(c == 0), stop=(c == 7)),
             r=[onesM.b, rs.b], w=[pmsq.b])
        bcells.put(rb)
        bcells.put(rs)

    def ln_finish(l, si, TB, g_col, b_col, second, pmean, pmsq, after_chunk=None):
        mean = fcells.get()
        var = fcells.get()
        rstd = fcells.get()
        if second:
            K.op(DVE, lambda e: e.tensor_copy(out=mean.t[:, 0:TB], in_=pmean.t[:, 0:TB]), r=[pmean.b], w=[mean.b])
        else:
            K.op(ACT, lambda e: e.activation(out=mean.t[:, 0:TB], in_=pmean.t[:, 0:TB], func=AF.Copy), r=[pmean.b], w=[mean.b])
        K.op(DVE, lambda e: e.tensor_tensor(out=var.t[:, 0:TB], in0=mean.t[:, 0:TB], in1=mean.t[:, 0:TB], op=ALU.mult), r=[mean.b], w=[var.b])
        K.op(DVE, lambda e: e.tensor_tensor(out=var.t[:, 0:TB], in0=pmsq.t[:, 0:TB], in1=var.t[:, 0:TB], op=ALU.subtract), r=[pmsq.b, var.b], w=[var.b])
        K.op(DVE, lambda e: e.tensor_scalar(out=var.t[:, 0:TB], in0=var.t[:, 0:TB], scalar1=0.0, scalar2=EPS_LN, op0=ALU.max, op1=ALU.add), r=[var.b], w=[var.b])
        K.op(ACT, lambda e: e.activation(out=rstd.t[:, 0:TB], in_=var.t[:, 0:TB], func=AF.Ln), r=[var.b], w=[rstd.b])
        K.op(ACT, lambda e: e.activation(out=rstd.t[:, 0:TB], in_=rstd.t[:, 0:TB], func=AF.Exp, scale=-0.5), r=[rstd.b], w=[rstd.b])
        K.op(DVE, lambda e: e.scalar_tensor_tensor(out=var.t[:, 0:TB], in0=mean.t[:, 0:TB], scalar=-1.0, in1=rstd.t[:, 0:TB], op0=ALU.mult, op1=ALU.mult),
             r=[mean.b, rstd.b], w=[var.b])
        ps_acc.put(pmean)
        ps_acc.put(pmsq)
        for c in range(8):
            eng = POOL if c in (3, 7) else DVE
            K.op(eng, lambda e, c=c: e.tensor_tensor(out=xt.t[:, c, 0:TB], in0=xt.t[:, c, 0:TB], in1=rstd.t[:, 0:TB], op=ALU.mult), r=[xb[c], rstd.b], w=[xb[c]])
            K.op(eng, lambda e, c=c: e.tensor_tensor(out=xt.t[:, c, 0:TB], in0=xt.t[:, c, 0:TB], in1=var.t[:, 0:TB], op=ALU.add), r=[xb[c], var.b], w=[xb[c]])
            if not second:
                K.op(DVE, lambda e, c=c: e.tensor_scalar(out=h8.t[:, c, 0:TB], in0=xt.t[:, c, 0:TB],
                                                         scalar1=dA2[l].t[:, c, si:si + 1], scalar2=dB2[l].t[:, c, si:si + 1],
                                                         op0=ALU.mult, op1=ALU.add), r=[xb[c], dA2[l].b, dB2[l].b], w=[h8b[c]])
            if second:
                K.op(DVE, lambda e, c=c: e.tensor_scalar(out=xt.t[:, c, 0:TB], in0=xt.t[:, c, 0:TB],
                                                         scalar1=lp[l].t[:, LP_LN + g_col + c:LP_LN + g_col + c + 1],
                                                         scalar2=lp[l].t[:, LP_LN + b_col + c:LP_LN + b_col + c + 1],
                                                         op0=ALU.mult, op1=ALU.add), r=[xb[c], lp[l].b], w=[xb[c]])
            else:
                K.op(ACT, lambda e, c=c: e.activation(out=xt.t[:, c, 0:TB], in_=xt.t[:, c, 0:TB], func=AF.Identity,
                                                      scale=lp[l].t[:, LP_LN + g_col + c:LP_LN + g_col + c + 1],
                                                      bias=lp[l].t[:, LP_LN + b_col + c:LP_LN + b_col + c + 1]),
                     r=[xb[c], lp[l].b], w=[xb[c]])
            if after_chunk is not None:
                after_chunk(c)
        fcells.put(mean)
        fcells.put(var)
        fcells.put(rstd)

    def seq_ctx(l, sq_):
        kind, sidx, si, Tn, TB, t0 = sq_["kind"], sq_["idx"], sq_["mi"], sq_["T"], sq_["TB"], sq_["t0"]
        NT = TB // 128
        is_s = kind == "s"
        if l == 0:
            x_src = xp[sidx] if not is_s else xs[sidx]
            xsrc_b = None
        else:
            x_src = xmp[sidx] if not is_s else xms[sidx]
            xsrc_b = sq_["xmid_b"]
        if l == NL - 1:
            x_dst = yp[sidx] if not is_s else ys[sidx]
        else:
            x_dst = xmp[sidx] if not is_s else xms[sidx]
            sq_["xmid_b"] = [Buf("xmid%d" % c) for c in range(8)]
        o_k = (pk if not is_s else sk)[l, sidx]
        o_v = (pv if not is_s else sv)[l, sidx]
        o_lf = (plf if not is_s else slf)[l, sidx]
        o_C = (pC if not is_s else sCo)[l, sidx]
        o_M = (pM if not is_s else sMo)[l, sidx]
        lpt = lp[l].t
        return dict(kind=kind, sidx=sidx, si=si, Tn=Tn, TB=TB, t0=t0, NT=NT, is_s=is_s, x_src=x_src, xsrc_b=xsrc_b, x_dst=x_dst,
                    o_k=o_k, o_v=o_v, o_lf=o_lf, o_C=o_C, o_M=o_M, lpt=lpt)

    def seq_init(l, sq_):
        cx = sq_['ctx'][l]
        kind, sidx, si, Tn, TB, t0, NT, is_s = cx['kind'], cx['sidx'], cx['si'], cx['Tn'], cx['TB'], cx['t0'], cx['NT'], cx['is_s']
        x_src, xsrc_b, x_dst = cx['x_src'], cx['xsrc_b'], cx['x_dst']
        o_k, o_v, o_lf, o_C, o_M, lpt = cx['o_k'], cx['o_v'], cx['o_lf'], cx['o_C'], cx['o_M'], cx['lpt']
        K.op(POOL, lambda e: e.memset(carF.t[:, :], 0.0), w=[carF.b])
        K.op(POOL, lambda e: e.memset(carB.t[:, :], 0.0), w=[carB.b])
        if not is_s:
            K.op(POOL, lambda e: e.memset(carG.t[:, :], -1e30), w=[carG.b])
            for j in range(2):
                K.op(POOL, lambda e, j=j: e.memset(Ctil[j].t[:, :], 0.0), w=[Ctil[j].b])
        else:
            K.dma(SP, carG.t[:, :], sM[l, sidx, :, :], w=[carG.b], prim=carG.b)
            for j in range(2):
                K.dma(SP, Ctil[j].t[:, :], sC[l, sidx, j, :, :], w=[Ctil[j].b], prim=Ctil[j].b)
            for i in range(4):
                stg = [fcells.get(), fcells.get()]
                for hh_ in range(2):
                    K.dma(SP, stg[hh_].t[:, :], ck[l, sidx, i, :, hh_ * 512:(hh_ + 1) * 512], w=[stg[hh_].b], prim=stg[hh_].b)
                    copy_on(ev(hh_), kTc[i].t[:, hh_ * 512:(hh_ + 1) * 512], stg[hh_].t[:, :], [stg[hh_].b],
                            kTb[i][hh_ * 4:(hh_ + 1) * 4], scale=0.125)
                fcells.put(stg[0])
                fcells.put(stg[1])
            for j in range(PAST // 128):
                stg = fcells.get()
                K.dma(SP, stg.t[:, :], cv[l, sidx, j * 128:(j + 1) * 128, :], w=[stg.b], prim=stg.b)
                copy_on(ev(j), Vc.t[:, j, :, 0:64], stg.t[:, :].rearrange("p (h d) -> p h d", h=8), [stg.b], [Vb[j]])
                fcells.put(stg)
            for hh_ in range(PAST // 512):
                lfc = fcells.get()
                K.dma(SP, lfc.t[:, :], clf[l, sidx, :, hh_ * 512:(hh_ + 1) * 512], w=[lfc.b], prim=lfc.b)
                Ft = fcells.get()
                K.op(DVE, lambda e, lfc=lfc, Ft=Ft: e.tensor_tensor_scan(out=Ft.t[:, :], data0=ones5.t[:, 0:512], data1=lfc.t[:, :],
                                                                       initial=carF.t[:, 0:1], op0=ALU.mult, op1=ALU.add),
                     r=[lfc.b, carF.b, ones5.b], w=[Ft.b])
                K.op(DVE, lambda e, Ft=Ft: e.tensor_copy(out=carF.t[:, :], in_=Ft.t[:, 511:512]), r=[Ft.b], w=[carF.b])
                fcells.put(lfc)
                bias_rows(Ft, hh_ * 512, 512, None)
                fcells.put(Ft)


    def seq_final(l, sq_):
        cx = sq_['ctx'][l]
        kind, sidx, si, Tn, TB, t0, NT, is_s = cx['kind'], cx['sidx'], cx['si'], cx['Tn'], cx['TB'], cx['t0'], cx['NT'], cx['is_s']
        x_src, xsrc_b, x_dst = cx['x_src'], cx['xsrc_b'], cx['x_dst']
        o_k, o_v, o_lf, o_C, o_M, lpt = cx['o_k'], cx['o_v'], cx['o_lf'], cx['o_C'], cx['o_M'], cx['lpt']
        for j in range(2):
            K.dma(SP, o_C[j, :, :], Ctil[j].t[:, :], r=[Ctil[j].b], prim=Ctil[j].b)
        K.op(DVE, lambda e: e.tensor_tensor(out=mout.t[:, :], in0=carB.t[:, :], in1=carG.t[:, :], op=ALU.add), r=[carB.b, carG.b], w=[mout.b])
        K.dma(SP, o_M[:, :], mout.t[:, :], r=[mout.b], prim=mout.b)


    pre_stage = {}
    deferred = []

    def flush_deferred():
        while deferred:
            deferred.pop(0)()

    def prologue_pre(l, sq_, blk, g0):
        cx = sq_['ctx'][l]
        TB, x_src, xsrc_b = cx['TB'], cx['x_src'], cx['xsrc_b']
        tb0 = blk * TB
        for kc in range(2):
            tmpf = fcells.get()
            srcs = [] if xsrc_b is None else [xsrc_b[kc]]
            K.dma(SP, tmpf.t[:, 0:TB], x_src[kc, :, tb0:tb0 + TB], r=srcs, w=[tmpf.b], prim=tmpf.b)
            pre_stage[kc] = tmpf

    def prologue(l, sq_, blk, g0):
        cx = sq_['ctx'][l]
        kind, sidx, si, Tn, TB, t0, NT, is_s = cx['kind'], cx['sidx'], cx['si'], cx['Tn'], cx['TB'], cx['t0'], cx['NT'], cx['is_s']
        x_src, xsrc_b, x_dst = cx['x_src'], cx['xsrc_b'], cx['x_dst']
        o_k, o_v, o_lf, o_C, o_M, lpt = cx['o_k'], cx['o_v'], cx['o_lf'], cx['o_C'], cx['o_M'], cx['lpt']
        tb0 = blk * TB
        wt_hi = wget(g0 + PI_W1[0])
        wt_lo = wget(g0 + PI_W1LO, ahead=1)
        pq = [ps_st.get(), ps_st.get(), ps_gen.get(), ps_gen.get()]
        for kc in range(8):
            hlo = bcells.get()
            if kc in pre_stage:
                tmpf = pre_stage.pop(kc)
            else:
                tmpf = fcells.get()
                srcs = [] if xsrc_b is None else [xsrc_b[kc]]
                K.dma(SP, tmpf.t[:, 0:TB], x_src[kc, :, tb0:tb0 + TB], r=srcs, w=[tmpf.b], prim=tmpf.b)
            K.op(ACT, lambda e, kc=kc: e.activation(out=h8.t[:, kc, 0:TB], in_=tmpf.t[:, 0:TB], func=AF.Identity,
                                                   scale=dsc1[l].t[:, kc, si:si + 1], bias=modT[l].t[:, kc, si:si + 1]),
                 r=[tmpf.b, dsc1[l].b, modT[l].b], w=[h8b[kc]])
            K.op(DVE, lambda e, kc=kc: e.tensor_scalar(out=tmpf.t[:, 0:TB], in0=tmpf.t[:, 0:TB],
                                                       scalar1=dsc1[l].t[:, kc, si:si + 1], scalar2=modT[l].t[:, kc, si:si + 1],
                                                       op0=ALU.mult, op1=ALU.add), r=[tmpf.b, dsc1[l].b, modT[l].b], w=[tmpf.b])
            K.op(DVE, lambda e, kc=kc: e.tensor_tensor(out=hlo.t[:, 0:TB], in0=tmpf.t[:, 0:TB], in1=h8.t[:, kc, 0:TB], op=ALU.subtract),
                 r=[tmpf.b, h8b[kc]], w=[hlo.b])
            for oc in range(4):
                wsl = slice(oc * 1024 + kc * 128, oc * 1024 + (kc + 1) * 128)
                K.op(PE, lambda e, oc=oc, wsl=wsl, kc=kc: e.matmul(pq[oc].t[:, 0:TB], lhsT=wt_hi.t[:, wsl], rhs=h8.t[:, kc, 0:TB], start=(kc == 0), stop=False),
                     r=[wt_hi.b, h8b[kc]], w=[pq[oc].b])
                K.op(PE, lambda e, oc=oc, wsl=wsl, kc=kc: e.matmul(pq[oc].t[:, 0:TB], lhsT=wt_lo.t[:, wsl], rhs=h8.t[:, kc, 0:TB], start=False, stop=False),
                     r=[wt_lo.b, h8b[kc]], w=[pq[oc].b])
                K.op(PE, lambda e, oc=oc, wsl=wsl, kc=kc: e.matmul(pq[oc].t[:, 0:TB], lhsT=wt_hi.t[:, wsl], rhs=hlo.t[:, 0:TB], start=False, stop=(kc == 7)),
                     r=[wt_hi.b, hlo.b], w=[pq[oc].b])
            fcells.put(tmpf)
            bcells.put(hlo)
        for oc in range(4):
            pp = pq[oc]
            if oc < 2:
                copy_on(ACT, arena.t[:, oc, 0:TB], pp.t[:, 0:TB], [pp.b], [ab[oc]])
                K.op(DVE, lambda e, oc=oc, pp=pp: e.tensor_tensor(out=arena.t[:, 2 + oc, 0:TB], in0=pp.t[:, 0:TB], in1=arena.t[:, oc, 0:TB], op=ALU.subtract),
                     r=[pp.b, ab[oc]], w=[ab[2 + oc]])
            else:
                j = oc - 2
                copy_on(ACT, arena.t[:, 20 + j, 0:TB], pp.t[:, 0:TB], [pp.b], [ab[20 + j]], scale=0.125)
                K.op(DVE, lambda e, j=j, pp=pp: e.scalar_tensor_tensor(out=kTml[j].t[:, 0:TB], in0=pp.t[:, 0:TB], scalar=0.125, in1=arena.t[:, 20 + j, 0:TB],
                                                                      op0=ALU.mult, op1=ALU.subtract), r=[pp.b, ab[20 + j]], w=[kTml[j].b])
        ps_st.put(pq[0])
        ps_st.put(pq[1])
        ps_gen.put(pq[2])
        ps_gen.put(pq[3])


    def block_main(l, sq_, blk, g0, nxt):
        cx = sq_['ctx'][l]
        kind, sidx, si, Tn, TB, t0, NT, is_s = cx['kind'], cx['sidx'], cx['si'], cx['Tn'], cx['TB'], cx['t0'], cx['NT'], cx['is_s']
        x_src, xsrc_b, x_dst = cx['x_src'], cx['xsrc_b'], cx['x_dst']
        o_k, o_v, o_lf, o_C, o_M, lpt = cx['o_k'], cx['o_v'], cx['o_lf'], cx['o_C'], cx['o_M'], cx['lpt']
        tb0 = blk * TB
        pos0 = t0 + tb0
        jb0 = pos0 // 128
        gsp = fcells.get()
        gzi = fcells.get()
        gsp2 = fcells.get()

        def proj_tile(tl):
            wt = wget(g0 + PI_W1[tl // 4])
            wofs = (tl % 4) * 1024
            pp = rot8()
            for kc in range(8):
                K.op(PE, lambda e, wt=wt, wofs=wofs, kc=kc, pp=pp: e.matmul(
                    pp.t[:, 0:TB], lhsT=wt.t[:, wofs + kc * 128: wofs + (kc + 1) * 128], rhs=h8.t[:, kc, 0:TB],
                    start=(kc == 0), stop=(kc == 7)), r=[wt.b, h8b[kc]], w=[pp.b])
            return pp

        pp = proj_tile(4)
        K.op(ACT, lambda e, pp=pp: e.activation(out=gsp.t[:, 0:TB], in_=pp.t[:, 0:TB], func=AF.Exp, scale=-1.0, bias=negb[l].t[:, 0:1]),
             r=[pp.b, negb[l].b], w=[gsp.b])
        K.op(ACT, lambda e: e.activation(out=gsp.t[:, 0:TB], in_=gsp.t[:, 0:TB], func=AF.Ln, bias=1.0), r=[gsp.b], w=[gsp.b])
        lfo = fcells.get()
        K.op(ACT, lambda e: e.activation(out=lfo.t[:, 0:TB], in_=gsp.t[:, 0:TB], func=AF.Copy, scale=-1.0), r=[gsp.b], w=[lfo.b])
        K.dma(ACT, o_lf[:, tb0:tb0 + TB], lfo.t[:, 0:TB], r=[lfo.b], prim=lfo.b)
        fcells.put(lfo)
        pp = proj_tile(5)
        K.op(ACT, lambda e, pp=pp: e.activation(out=gzi.t[:, 0:TB], in_=pp.t[:, 0:TB], func=AF.Identity, bias=lpt[:, LP_BI:LP_BI + 1]),
             r=[pp.b, lp[l].b], w=[gzi.b])
        pp = proj_tile(6)
        K.op(ACT, lambda e, pp=pp: e.activation(out=gsp2.t[:, 0:TB], in_=pp.t[:, 0:TB], func=AF.Exp, scale=-1.0, bias=negb[l].t[:, 1:2]),
             r=[pp.b, negb[l].b], w=[gsp2.b])
        K.op(ACT, lambda e: e.activation(out=gsp2.t[:, 0:TB], in_=gsp2.t[:, 0:TB], func=AF.Ln, bias=1.0), r=[gsp2.b], w=[gsp2.b])

        stop_at(2)
        Ft = fcells.get()
        K.op(DVE, lambda e: e.tensor_tensor_scan(out=Ft.t[:, 0:TB], data0=ones5.t[:, 0:TB], data1=gsp.t[:, 0:TB],
                                                 initial=carF.t[:, 0:1], op0=ALU.mult, op1=ALU.subtract),
             r=[gsp.b, carF.b, ones5.b], w=[Ft.b])
        K.op(DVE, lambda e: e.tensor_copy(out=carF.t[:, :], in_=Ft.t[:, TB - 1:TB]), r=[Ft.b], w=[carF.b])
        fcells.put(gsp)
        bias_rows(Ft, pos0, TB, [ab[12 + h] for h in range(8)])
        fcells.put(Ft)

        Bt = fcells.get()
        K.op(DVE, lambda e: e.tensor_tensor_scan(out=Bt.t[:, 0:TB], data0=ones5.t[:, 0:TB], data1=gsp2.t[:, 0:TB],
                                                 initial=carB.t[:, 0:1], op0=ALU.mult, op1=ALU.subtract),
             r=[gsp2.b, carB.b, ones5.b], w=[Bt.b])
        fcells.put(gsp2)
        at = gzi
        K.op(DVE, lambda e: e.tensor_tensor(out=at.t[0:4, 0:TB], in0=gzi.t[0:4, 0:TB], in1=Bt.t[0:4, 0:TB], op=ALU.subtract), r=[gzi.b, Bt.b], w=[at.b])
        Gt = fcells.get()
        K.op(DVE, lambda e: e.tensor_tensor_scan(out=Gt.t[0:4, 0:TB], data0=ones5.t[0:4, 0:TB], data1=at.t[0:4, 0:TB],
                                                 initial=carG.t[0:4, 0:1], op0=ALU.mult, op1=ALU.max),
             r=[at.b, carG.b, ones5.b], w=[Gt.b])
        nGt = fcells.get()
        K.op(DVE, lambda e: e.tensor_scalar(out=nGt.t[0:4, 0:TB], in0=Gt.t[0:4, 0:TB], scalar1=-1.0, scalar2=None, op0=ALU.mult), r=[Gt.b], w=[nGt.b])

        for tl in range(7, 15):
            pp = proj_tile(tl)
            if tl < 11:
                pr = tl - 7
                for half in range(2):
                    hidx = 4 + 2 * pr + half
                    lo, hi = half * 64, half * 64 + 64
                    olo, ohi = (1 - half) * 64, (1 - half) * 64 + 64
                    copy_on(ACT, arena.t[lo:hi, hidx, 0:TB], pp.t[lo:hi, 0:TB], [pp.b], [ab[hidx]])
                    K.op(POOL, lambda e, hidx=hidx, olo=olo, ohi=ohi: e.memset(arena.t[olo:ohi, hidx, 0:TB], 0.0), w=[ab[hidx]])
            else:
                i = tl - 11
                ko = fcells.get()
                copy_on(ACT, ko.t[:, 0:TB], pp.t[:, 0:TB], [pp.b], [ko.b])
                copy_on(DVE, kTc[i].t[:, pos0:pos0 + TB], ko.t[:, 0:TB], [ko.b], kTb[i][jb0:jb0 + NT], scale=0.125)
                K.dma(ACT, o_k[i, :, tb0:tb0 + TB], ko.t[:, 0:TB], r=[ko.b], prim=ko.b)
                fcells.put(ko)

        stop_at(4)
        wt = wget(g0 + PI_W2)
        for c in range(NT):
            cs = slice(c * 128, (c + 1) * 128)
            pB = rot8()
            for kc in range(8):
                K.op(PE, lambda e, kc=kc, wt=wt, pB=pB, cs=cs: e.matmul(pB.t[:, 0:256], lhsT=h8.t[:, kc, cs], rhs=wt.t[:, kc * 256:(kc + 1) * 256],
                                                                       start=(kc == 0), stop=(kc == 7)), r=[h8b[kc], wt.b], w=[pB.b])
            K.op(ACT, lambda e, pB=pB: e.activation(out=vnt.t[:, :], in_=pB.t[:, 0:256], func=AF.Sigmoid), r=[pB.b], w=[vnt.b])
            K.op(POOL, lambda e, c=c: e.tensor_tensor(out=sgt[c].t[:, :], in0=vnt.t[:, :], in1=lpt[:, LP_GM:LP_GM + 256], op=ALU.mult), r=[vnt.b, lp[l].b], w=[sgt[c].b])
        wt = wget(g0 + PI_W2 + 1)
        for c in range(NT):
            cs = slice(c * 128, (c + 1) * 128)
            pC_ = rot8()
            for kc in range(8):
                K.op(PE, lambda e, kc=kc, wt=wt, pC_=pC_, cs=cs: e.matmul(pC_.t[:, 0:512], lhsT=h8.t[:, kc, cs], rhs=wt.t[:, kc * 512:(kc + 1) * 512],
                                                                         start=(kc == 0), stop=(kc == 7)), r=[h8b[kc], wt.b], w=[pC_.b])
            copy_on(ACT, ut[c].t[:, :], pC_.t[:, 0:256], [pC_.b], [ut[c].b])
            K.op(DVE, lambda e, pC_=pC_: e.bn_stats(out=st6.t[:, :], in_=pC_.t[:, 256:512]), r=[pC_.b], w=[st6.b])
            K.op(DVE, lambda e: e.bn_aggr(out=st2.t[:, :], in_=st6.t[:, :]), r=[st6.b], w=[st2.b])
            K.op(DVE, lambda e: e.tensor_scalar(out=st2.t[:, 1:2], in0=st2.t[:, 1:2], scalar1=EPS_G, scalar2=None, op0=ALU.add), r=[st2.b], w=[st2.b])
            K.op(POOL, lambda e: e.tensor_tensor(out=st2.t[:, 1:2], in0=st2.t[:, 1:2], in1=mhalf.t[:, 0:1], op=ALU.pow), r=[st2.b, mhalf.b], w=[st2.b])
            K.op(DVE, lambda e, pC_=pC_: e.tensor_scalar(out=vnt.t[:, :], in0=pC_.t[:, 256:512], scalar1=st2.t[:, 0:1], scalar2=st2.t[:, 1:2],
                                                         op0=ALU.subtract, op1=ALU.mult), r=[pC_.b, st2.b], w=[vnt.b])
            K.op(POOL, lambda e: e.tensor_tensor(out=vnt.t[:, :], in0=vnt.t[:, :], in1=lpt[:, LP_GG:LP_GG + 256], op=ALU.mult), r=[vnt.b, lp[l].b], w=[vnt.b])
            K.op(POOL, lambda e: e.tensor_tensor(out=vnt.t[:, :], in0=vnt.t[:, :], in1=lpt[:, LP_GB:LP_GB + 256], op=ALU.add), r=[vnt.b, lp[l].b], w=[vnt.b])
            K.op(POOL, lambda e, c=c: e.tensor_copy(out=vnb[c].t[:, :], in_=vnt.t[:, :]), r=[vnt.b], w=[vnb[c].b])
            if is_s:
                K.dma(STQ, sgv[l, sidx, :, :], vnt.t[:, :], r=[vnt.b], prim=vnt.b)
        wt = wget(g0 + PI_W2 + 2)
        for c in range(NT):
            cs = slice(c * 128, (c + 1) * 128)
            jblk = jb0 + c
            pD = rot8()
            for kc in range(8):
                K.op(PE, lambda e, kc=kc, wt=wt, pD=pD, cs=cs: e.matmul(pD.t[:, 0:512], lhsT=h8.t[:, kc, cs], rhs=wt.t[:, kc * 512:(kc + 1) * 512],
                                                                       start=(kc == 0), stop=(kc == 7)), r=[h8b[kc], wt.b], w=[pD.b])
            vo = fcells.get()
            copy_on(ACT, vo.t[:, :], pD.t[:, 0:512], [pD.b], [vo.b])
            copy_on(DVE, Vc.t[:, jblk, :, 0:64], vo.t[:, 0:512].rearrange("p (h d) -> p h d", h=8), [vo.b], [Vb[jblk]])
            K.dma(ACT, o_v[tb0 + c * 128: tb0 + (c + 1) * 128, :], vo.t[:, :], r=[vo.b], prim=vo.b)
            fcells.put(vo)


        stop_at(3)
        wt = wget(g0 + PI_W2 + 3)
        for c in range(NT):
            cs = slice(c * 128, (c + 1) * 128)
            pA = rot8()
            for kc in range(8):
                K.op(PE, lambda e, kc=kc, wt=wt, pA=pA, cs=cs: e.matmul(pA.t[:, 0:512], lhsT=h8.t[:, kc, cs], rhs=wt.t[:, kc * 512:(kc + 1) * 512],
                                                                       start=(kc == 0), stop=(kc == 7)), r=[h8b[kc], wt.b], w=[pA.b])
            copy_on(ACT, ktok[c].t[:, :], pA.t[:, 0:256], [pA.b], [ktok[c].b], scale=0.125)
            K.op(DVE, lambda e, pA=pA, c=c: e.tensor_copy(out=vaug[c].t[:, :, 0:64], in_=pA.t[:, 256:512].rearrange("p (h d) -> p h d", h=4)),
                 r=[pA.b], w=[vaug[c].b])
            K.op(POOL, lambda e, c=c: e.memset(vaug[c].t[:, :, 64:65], 1.0), w=[vaug[c].b])

        stop_at(5)
        def pass3_gen():
            for c in range(NT):
                cs = slice(c * 128, (c + 1) * 128)
                ec = c * 128 + (DEC - 1 if is_s else 127)
                am = amt
                K.op(DVE, lambda e, cs=cs, ec=ec: e.tensor_scalar(out=am.t[0:4, 0:128], in0=at.t[0:4, cs], scalar1=nGt.t[0:4, ec:ec + 1], scalar2=0.0,
                                                                op0=ALU.add, op1=ALU.min), r=[at.b, nGt.b], w=[am.b])
                K.op(ACT, lambda e: e.activation(out=E1t.t[0:4, :], in_=am.t[0:4, 0:128], func=AF.Exp), r=[am.b], w=[E1t.b])
                if is_s:
                    K.op(DVE, lambda e: e.tensor_tensor(out=E1t.t[0:4, :], in0=E1t.t[0:4, :], in1=validF.t[0:4, :], op=ALU.mult), r=[E1t.b, validF.b], w=[E1t.b])
                K.op(ACT, lambda e, cs=cs, ec=ec: e.activation(out=CLt.t[0:4, :], in_=Bt.t[0:4, cs], func=AF.Exp, scale=-1.0, bias=nGt.t[0:4, ec:ec + 1]),
                     r=[Bt.b, nGt.b], w=[CLt.b])
                pt_ = rot(ps_gen)
                K.op(PE, lambda e, pt_=pt_: e.matmul(pt_.t[:, 0:4], lhsT=E1t.t[:, :], rhs=identF.t[:, 0:4], start=True, stop=True), r=[E1t.b, identF.b], w=[pt_.b])
                K.op(PE, lambda e, pt_=pt_: e.matmul(pt_.t[:, 4:8], lhsT=CLt.t[:, :], rhs=identF.t[:, 0:4], start=True, stop=True), r=[CLt.b, identF.b], w=[pt_.b])
                K.op(DVE, lambda e, pt_=pt_, c=c: e.tensor_copy(out=tokS[c].t[:, :], in_=pt_.t[:, 0:8]), r=[pt_.b], w=[tokS[c].b])
                yield

            for c in range(NT if 'pass3' not in os.environ.get('KSKIP', '') else 0):
                cs = slice(c * 128, (c + 1) * 128)
                ec = c * 128 + (DEC - 1 if is_s else 127)
                K.op(DVE, lambda e, c=c: e.tensor_tensor(out=vaug[c].t[:, :, :], in0=vaug[c].t[:, :, :],
                                                         in1=tokS[c].t[:, 0:4].rearrange("p (h o) -> p h o", o=1).to_broadcast([128, 4, 65]), op=ALU.mult),
                     r=[vaug[c].b, tokS[c].b], w=[vaug[c].b])
                K.op(DVE, lambda e: e.tensor_copy(out=gprev.t[:, :], in_=carG.t[:, :]), r=[carG.b], w=[gprev.b])
                K.op(ACT, lambda e, ec=ec: e.activation(out=lamc.t[0:4, :], in_=gprev.t[0:4, :], func=AF.Exp, bias=nGt.t[0:4, ec:ec + 1]),
                     r=[gprev.b, nGt.b], w=[lamc.b])
                K.op(DVE, lambda e, ec=ec: e.tensor_copy(out=carG.t[0:4, :], in_=Gt.t[0:4, ec:ec + 1]), r=[Gt.b, gprev.b], w=[carG.b])
                K.op(DVE, lambda e, ec=ec: e.tensor_copy(out=carB.t[0:4, :], in_=Bt.t[0:4, ec:ec + 1]), r=[Bt.b], w=[carB.b])
                K.op(DVE, lambda e: e.tensor_scalar(out=Dl.t[:, :], in0=identF.t[:, 0:4], scalar1=lamc.t[:, 0:1], scalar2=None, op0=ALU.mult), r=[identF.b, lamc.b], w=[Dl.b])
                pt_ = rot(ps_gen)
                K.op(PE, lambda e, pt_=pt_: e.matmul(pt_.t[:, 8:12], lhsT=onesF.t[:, :], rhs=Dl.t[:, :], start=True, stop=True), r=[onesF.b, Dl.b], w=[pt_.b])
                K.op(DVE, lambda e, pt_=pt_: e.tensor_copy(out=lamB.t[:, :], in_=pt_.t[:, 8:12]), r=[pt_.b], w=[lamB.b])
                for j in range(2):
                    for half in range(2):
                        lo, hi = half * 64, half * 64 + 64
                        hcol = 2 * j + half
                        K.op(DVE, lambda e, j=j, lo=lo, hi=hi, hcol=hcol: e.tensor_scalar(out=Cp[j].t[lo:hi, :], in0=Ctil[j].t[lo:hi, :],
                                                                                         scalar1=lamB.t[lo:hi, hcol:hcol + 1], scalar2=None, op0=ALU.mult),
                             r=[Ctil[j].b, lamB.b], w=[Cp[j].b])
                    K.op(POOL, lambda e, j=j: e.tensor_copy(out=Cb[j].t[:, :], in_=Cp[j].t[:, :]), r=[Cp[j].b], w=[Cb[j].b])

                pZ = rot(ps_gen)
                for g in range(4):
                    K.op(PE, lambda e, g=g, pZ=pZ, c=c: e.matmul(pZ.t[:, g * 64:(g + 1) * 64], lhsT=wsTb[l].t[:, g * 128:(g + 1) * 128], rhs=vnb[c].t[:, g * 64:(g + 1) * 64],
                                                                start=True, stop=True), r=[wsTb[l].b, vnb[c].b], w=[pZ.b])
                for g in range(4):
                    K.op(DVE, lambda e, g=g, pZ=pZ, c=c: e.scalar_tensor_tensor(out=hgb.t[:, g * 64:(g + 1) * 64], in0=pZ.t[:, g * 64:(g + 1) * 64],
                                                                               scalar=lpt[:, LP_BS + g:LP_BS + g + 1], in1=ut[c].t[:, g * 64:(g + 1) * 64],
                                                                               op0=ALU.add, op1=ALU.mult), r=[pZ.b, lp[l].b, ut[c].b], w=[hgb.b])
                yield
                pSx = [rot(ps_st), rot(ps_st)]
                for h in range(4):
                    j, half = h // 2, h % 2
                    lo, hi = half * 64, half * 64 + 64
                    pS = pSx[half]
                    osl = slice(j * 128, (j + 1) * 128)
                    K.op(PE, lambda e, j=j, lo=lo, hi=hi, osl=osl, pS=pS, cs=cs: e.matmul(pS.t[:, osl], lhsT=arena.t[lo:hi, 20 + j, cs], rhs=arena.t[lo:hi, j, cs],
                                                                                       start=True, stop=False), r=[ab[20 + j], ab[j]], w=[pS.b])
                    K.op(PE, lambda e, j=j, lo=lo, hi=hi, osl=osl, pS=pS, cs=cs: e.matmul(pS.t[:, osl], lhsT=arena.t[lo:hi, 20 + j, cs], rhs=arena.t[lo:hi, 2 + j, cs],
                                                                                       start=False, stop=False), r=[ab[20 + j], ab[2 + j]], w=[pS.b])
                    K.op(PE, lambda e, j=j, lo=lo, hi=hi, osl=osl, pS=pS, cs=cs: e.matmul(pS.t[:, osl], lhsT=kTml[j].t[lo:hi, cs], rhs=arena.t[lo:hi, j, cs],
                                                                                       start=False, stop=True), r=[kTml[j].b, ab[j]], w=[pS.b])
                smT = bcells.get()
                smT4 = smT.t[:, :].rearrange("p (j e t) -> p j e t", j=2, e=2)
                for half in range(2):
                    K.op(DVE, lambda e, half=half: e.tensor_tensor(out=smT4[:, :, half, :], in0=pSx[half].t[:, 0:256].rearrange("p (j t) -> p j t", j=2),
                                                                   in1=mask01.t[:, :].rearrange("p (o t) -> p o t", o=1).to_broadcast([128, 2, 128]), op=ALU.mult),
                         r=[pSx[half].b, mask01.b], w=[smT.b])
                yield
                pT = rot(ps_gen)
                for q in range(2):
                    K.op(PE, lambda e, q=q, pT=pT: e.matmul(pT.t[:, q * 128:(q + 1) * 128], lhsT=hgb.t[:, q * 128:(q + 1) * 128], rhs=identB.t[:, :],
                                                           start=True, stop=True), r=[hgb.b, identB.b], w=[pT.b])
                for q in range(2):
                    copy_on(ev(q), mix4.t[:, 2 + q, cs], pT.t[:, q * 128:(q + 1) * 128], [pT.b], [mix4b[2 + q]])
                yield
                pHx = [rot(ps_gen), rot(ps_gen)]
                for h in range(4):
                    j, half = h // 2, h % 2
                    lo, hi = half * 64, half * 64 + 64
                    pH = pHx[half]
                    K.op(PE, lambda e, h=h, j=j, pH=pH, c=c: e.matmul(pH.t[:, j * 65:(j + 1) * 65], lhsT=smT.t[:, h * 128:(h + 1) * 128], rhs=vaug[c].t[:, h, :],
                                                                    start=True, stop=False), r=[smT.b, vaug[c].b], w=[pH.b])
                    K.op(PE, lambda e, j=j, lo=lo, hi=hi, pH=pH, cs=cs: e.matmul(pH.t[:, j * 65:(j + 1) * 65], lhsT=arena.t[lo:hi, j, cs], rhs=Cb[j].t[lo:hi, :],
                                                                               start=False, stop=True), r=[ab[j], Cb[j].b], w=[pH.b])
                bcells.put(smT)
                dn, ad, rd, ssum, ssq, mean4, var4, rs4 = sm4
                v4 = lambda tt: tt.t[:, :].rearrange("p (j e o) -> p j e o", j=2, e=2)
                hh4 = hh.t[:, :].rearrange("p (j e d) -> p j e d", j=2, e=2)
                for half in range(2):
                    pH3 = pHx[half].t[:, 0:130].rearrange("p (j c) -> p j c", c=65)
                    K.op(DVE, lambda e, half=half, pH3=pH3: e.tensor_copy(out=v4(dn)[:, :, half, :], in_=pH3[:, :, 64:65]), r=[pHx[half].b], w=[dn.b])
                K.op(DVE, lambda e: e.scalar_tensor_tensor(out=ad.t[:, :], in0=dn.t[:, :], scalar=-1.0, in1=dn.t[:, :], op0=ALU.mult, op1=ALU.max), r=[dn.b], w=[ad.b])
                K.op(DVE, lambda e, c=c: e.tensor_tensor(out=ad.t[:, :], in0=ad.t[:, :], in1=tokS[c].t[:, 4:8], op=ALU.max), r=[ad.b, tokS[c].b], w=[ad.b])
                K.op(DVE, lambda e: e.reciprocal(out=rd.t[:, :], in_=ad.t[:, :]), r=[ad.b], w=[rd.b])
                for half in range(2):
                    pH3 = pHx[half].t[:, 0:130].rearrange("p (j c) -> p j c", c=65)
                    K.op(DVE, lambda e, half=half, pH3=pH3: e.tensor_tensor(out=hh4[:, :, half, :], in0=pH3[:, :, 0:64],
                                                                          in1=v4(rd)[:, :, half, :].to_broadcast([128, 2, 64]), op=ALU.mult),
                         r=[pHx[half].b, rd.b], w=[hh.b])
                hh3 = hh.t[:, :].rearrange("p (h d) -> p h d", h=4)
                sq3 = sq.t[:, :].rearrange("p (h d) -> p h d", h=4)
                bc4 = lambda tt: tt.t[:, :].rearrange("p (h o) -> p h o", o=1).to_broadcast([128, 4, 64])
                K.op(DVE, lambda e: e.tensor_reduce(out=ssum.t[:, :], in_=hh3, axis=AX.X, op=ALU.add), r=[hh.b], w=[ssum.b])
                K.op(POOL, lambda e: e.tensor_tensor(out=sq.t[:, :], in0=hh.t[:, :], in1=hh.t[:, :], op=ALU.mult), r=[hh.b], w=[sq.b])
                K.op(DVE, lambda e: e.tensor_reduce(out=ssq.t[:, :], in_=sq3, axis=AX.X, op=ALU.add), r=[sq.b], w=[ssq.b])
                K.op(DVE, lambda e: e.tensor_scalar(out=mean4.t[:, :], in0=ssum.t[:, :], scalar1=1.0 / 64.0, scalar2=None, op0=ALU.mult), r=[ssum.b], w=[mean4.b])
                K.op(DVE, lambda e: e.tensor_tensor(out=var4.t[:, :], in0=mean4.t[:, :], in1=mean4.t[:, :], op=ALU.mult), r=[mean4.b], w=[var4.b])
                K.op(DVE, lambda e: e.scalar_tensor_tensor(out=var4.t[:, :], in0=ssq.t[:, :], scalar=1.0 / 64.0, in1=var4.t[:, :], op0=ALU.mult, op1=ALU.subtract),
                     r=[ssq.b, var4.b], w=[var4.b])
                K.op(DVE, lambda e: e.tensor_scalar(out=var4.t[:, :], in0=var4.t[:, :], scalar1=0.0, scalar2=EPS_HN, op0=ALU.max, op1=ALU.add), r=[var4.b], w=[var4.b])
                K.op(POOL, lambda e: e.tensor_tensor(out=rs4.t[:, :], in0=var4.t[:, :], in1=mhalf.t[:, :], op=ALU.pow), r=[var4.b, mhalf.b], w=[rs4.b])
                K.op(DVE, lambda e: e.tensor_tensor(out=hh3, in0=hh3, in1=bc4(mean4), op=ALU.subtract), r=[hh.b, mean4.b], w=[hh.b])
                K.op(DVE, lambda e: e.tensor_tensor(out=hh3, in0=hh3, in1=bc4(rs4), op=ALU.mult), r=[hh.b, rs4.b], w=[hh.b])
                K.op(POOL, lambda e, c=c: e.tensor_tensor(out=hmb.t[:, :], in0=hh.t[:, :], in1=sgt[c].t[:, :], op=ALU.mult), r=[hh.b, sgt[c].b], w=[hmb.b])
                yield
                pU = ps_gen.get()
                for h in range(4):
                    K.op(PE, lambda e, h=h, pU=pU, c=c: e.matmul(pU.t[:, h * 65:(h + 1) * 65], lhsT=ktok[c].t[:, (h // 2) * 128:(h // 2 + 1) * 128], rhs=vaug[c].t[:, h, :],
                                                                start=True, stop=True), r=[ktok[c].b, vaug[c].b], w=[pU.b])
                yield
                pT = rot(ps_gen)
                for q in range(2):
                    K.op(PE, lambda e, q=q, pT=pT: e.matmul(pT.t[:, q * 128:(q + 1) * 128], lhsT=hmb.t[:, q * 128:(q + 1) * 128], rhs=identB.t[:, :],
                                                           start=True, stop=True), r=[hmb.b, identB.b], w=[pT.b])
                for q in range(2):
                    copy_on(ev(q), mix4.t[:, q, cs], pT.t[:, q * 128:(q + 1) * 128], [pT.b], [mix4b[q]])
                for h in range(4):
                    j, half = h // 2, h % 2
                    lo, hi = half * 64, half * 64 + 64
                    K.op(DVE, lambda e, h=h, j=j, lo=lo, hi=hi, pU=pU: e.tensor_tensor(out=Ctil[j].t[lo:hi, :], in0=Cp[j].t[lo:hi, :], in1=pU.t[lo:hi, h * 65:(h + 1) * 65], op=ALU.add),
                         r=[Cp[j].b, pU.b], w=[Ctil[j].b])
                ps_gen.put(pU)

            yield
        p3 = pass3_gen()

        def adv3():
            try:
                next(p3)
                return True
            except StopIteration:
                return False

        stop_at(6)
        flush_deferred()
        for c in range(8):
            srcs = [] if xsrc_b is None else [xsrc_b[c]]
            K.dma(SP, xt.t[:, c, 0:TB], x_src[c, :, tb0:tb0 + TB], r=srcs, w=[xb[c]], prim=xb[c])
        nj = jb0 + NT
        pend = [None]
        sp3 = [0, max(1, min(8, (8 * nj) // 26))]

        def finalize_copy(h, acc):
            nS = fcells.get()
            copy_on(ACT, nS.t[0:65, 0:TB], acc.t[0:65, 0:TB], [acc.b], [nS.b])
            ps_acc.put(acc)
            return nS

        def finalize(h, nS):
            pb = rot(ps_gen)
            K.op(PE, lambda e: e.matmul(pb.t[0:64, 0:TB], lhsT=onesF.t[64:65, 0:64], rhs=nS.t[64:65, 0:TB], start=True, stop=True),
                 r=[onesF.b, nS.b], w=[pb.b])
            RD = fcells.get()
            K.op(DVE, lambda e: e.reciprocal(out=RD.t[0:64, 0:TB], in_=pb.t[0:64, 0:TB]), r=[pb.b], w=[RD.b])
            K.op(DVE, lambda e: e.tensor_tensor(out=h8.t[0:64, h, 0:TB], in0=nS.t[0:64, 0:TB], in1=RD.t[0:64, 0:TB], op=ALU.mult),
                 r=[nS.b, RD.b], w=[h8b[h]])
            fcells.put(nS)
            fcells.put(RD)

        for h in range(8 if 'fox' not in os.environ.get('KSKIP', '') else 0):
            i = h // 2
            acc = ps_acc.get()
            pts = [None] * nj

            def s_step(j):
                pS = rot(ps_st)
                diag = j >= jb0
                c0 = 128 * (j - jb0) if (diag and NT > 1 and j < nj - 1) else 0
                K.op(PE, lambda e: e.matmul(pS.t[:, c0:TB], lhsT=kTc[i].t[:, j * 128:(j + 1) * 128], rhs=arena.t[:, 4 + h, c0:TB], start=True, stop=False),
                     r=[kTb[i][j], ab[4 + h]], w=[pS.b])
                K.op(PE, lambda e: e.matmul(pS.t[:, c0:TB], lhsT=KBt.t[:, j * 128:(j + 1) * 128], rhs=arena.t[:, 12 + h, c0:TB], start=False, stop=not diag),
                     r=[KBb[j], ab[12 + h]], w=[pS.b])
                if diag:
                    d = j - jb0
                    K.op(PE, lambda e: e.matmul(pS.t[:, c0:TB], lhsT=identB.t[:, :], rhs=mnegB.t[:, d * 512 + c0:d * 512 + TB], start=False, stop=True),
                         r=[identB.b, mnegB.b], w=[pS.b])
                pt_ = bcells.get()
                K.op(ACT, lambda e: e.activation(out=pt_.t[:, c0:TB], in_=pS.t[:, c0:TB], func=AF.Exp), r=[pS.b], w=[pt_.b])
                pts[j] = (pt_, c0)

            def pv_step(j):
                pt_, c0 = pts[j]
                K.op(PE, lambda e: e.matmul(acc.t[0:65, c0:TB], lhsT=Vc.t[:, j, h, :], rhs=pt_.t[:, c0:TB], start=(j == 0), stop=(j == nj - 1)),
                     r=[Vb[j], pt_.b], w=[acc.b])
                bcells.put(pt_)

            for j in range(nj):
                s_step(j)
                sp3[0] += 1
                if sp3[0] % sp3[1] == 0:
                    adv3()
                if j == min(3, nj - 1) and pend[0] is not None:
                    pend[0]()
                    pend[0] = None
                if j >= 1:
                    pv_step(j - 1)
            if pend[0] is not None:
                pend[0]()
                pend[0] = None
            pv_step(nj - 1)
            nS_ = finalize_copy(h, acc)
            pend[0] = (lambda h=h, nS_=nS_: finalize(h, nS_))
        if pend[0] is not None:
            pend[0]()
        while adv3():
            pass
        fcells.put(Bt)
        fcells.put(gzi)
        fcells.put(Gt)
        fcells.put(nGt)

        stop_at(7)
        lnm = ps_acc.get()
        lnq = ps_acc.get()
        for oc in range(8):
            wt = wget(g0 + PI_WO + oc // 2)
            wofs = (oc % 2) * 1536
            pp = rot(ps_gen)
            rhs_list = [(mix4.t[:, 0, 0:TB], mix4b[0]), (mix4.t[:, 1, 0:TB], mix4b[1])]
            rhs_list += [(h8.t[:, h, 0:TB], h8b[h]) for h in range(8)]
            rhs_list += [(mix4.t[:, 2, 0:TB], mix4b[2]), (mix4.t[:, 3, 0:TB], mix4b[3])]
            for kc, (rap, rbuf) in enumerate(rhs_list):
                K.op(PE, lambda e, kc=kc, rap=rap, wt=wt, wofs=wofs, pp=pp: e.matmul(pp.t[:, 0:TB], lhsT=wt.t[:, wofs + kc * 128: wofs + (kc + 1) * 128], rhs=rap,
                                                                                  start=(kc == 0), stop=(kc == 11)), r=[wt.b, rbuf], w=[pp.b])
            K.op(DVE, lambda e, oc=oc, pp=pp: e.scalar_tensor_tensor(out=xt.t[:, oc, 0:TB], in0=pp.t[:, 0:TB], scalar=dg1[l].t[:, oc, si:si + 1], in1=xt.t[:, oc, 0:TB],
                                                                    op0=ALU.mult, op1=ALU.add), r=[pp.b, dg1[l].b, xb[oc]], w=[xb[oc]])
            if oc >= 1:
                ln_chunk_stats(oc - 1, TB, lnm, lnq)
        ln_chunk_stats(7, TB, lnm, lnq)
        ln_finish(l, si, TB, 0, 8, False, lnm, lnq)

        stop_at(8)
        for pc in range(11):
            wt = wget(g0 + PI_GU + pc)
            for q in range(2):
                fc_ = pc * 2 + q
                pg = rot8()
                pu = rot8()
                for kc in range(8):
                    K.op(PE, lambda e, kc=kc, q=q, wt=wt, pg=pg: e.matmul(pg.t[:, 0:TB], lhsT=wt.t[:, q * 1024 + kc * 128: q * 1024 + (kc + 1) * 128], rhs=h8.t[:, kc, 0:TB],
                                                                         start=(kc == 0), stop=(kc == 7)), r=[wt.b, h8b[kc]], w=[pg.b])
                for kc in range(8):
                    K.op(PE, lambda e, kc=kc, q=q, wt=wt, pu=pu: e.matmul(pu.t[:, 0:TB], lhsT=wt.t[:, 2048 + q * 1024 + kc * 128: 2048 + q * 1024 + (kc + 1) * 128], rhs=h8.t[:, kc, 0:TB],
                                                                         start=(kc == 0), stop=(kc == 7)), r=[wt.b, h8b[kc]], w=[pu.b])
                sg_ = fcells.get()
                K.op(ACT, lambda e, pg=pg, sg_=sg_: e.activation(out=sg_.t[:, 0:TB], in_=pg.t[:, 0:TB], func=AF.Silu), r=[pg.b], w=[sg_.b])
                K.op(DVE, lambda e, pu=pu, sg_=sg_, fc_=fc_: e.tensor_tensor(out=arena.t[:, fc_, 0:TB], in0=pu.t[:, 0:TB], in1=sg_.t[:, 0:TB], op=ALU.mult),
                     r=[pu.b, sg_.b], w=[ab[fc_]])
                fcells.put(sg_)
        if nxt is not None:
            prologue_pre(*nxt)
        lnm = ps_acc.get()
        lnq = ps_acc.get()
        for oc in range(8):
            wt = wget(g0 + PI_WD + oc)
            pp = rot(ps_gen)
            for kc in range(22):
                K.op(PE, lambda e, kc=kc, wt=wt, pp=pp: e.matmul(pp.t[:, 0:TB], lhsT=wt.t[:, kc * 128:(kc + 1) * 128], rhs=arena.t[:, kc, 0:TB],
                                                                start=(kc == 0), stop=(kc == 21)), r=[wt.b, ab[kc]], w=[pp.b])
            K.op(DVE, lambda e, oc=oc, pp=pp: e.scalar_tensor_tensor(out=xt.t[:, oc, 0:TB], in0=pp.t[:, 0:TB], scalar=dg2[l].t[:, oc, si:si + 1], in1=xt.t[:, oc, 0:TB],
                                                                    op0=ALU.mult, op1=ALU.add), r=[pp.b, dg2[l].b, xb[oc]], w=[xb[oc]])
            if oc >= 1:
                ln_chunk_stats(oc - 1, TB, lnm, lnq)
        ln_chunk_stats(7, TB, lnm, lnq)

        if nxt is not None:
            prologue(*nxt)

        def store_chunk(c):
            def emit():
                dsts = [] if l == NL - 1 else [sq_["xmid_b"][c]]
                K.dma(ACT, x_dst[c, :, tb0:tb0 + TB], xt.t[:, c, 0:TB], r=[xb[c]], w=dsts, prim=xb[c])
            deferred.append(emit)
        ln_finish(l, si, TB, 16, 24, True, lnm, lnq, after_chunk=store_chunk)
        if nxt is None:
            flush_deferred()
        stop_at(9)


    def bias_rows(Ft, pos, n, qb_bufs):
        HI = bcells.get()
        MID = bcells.get()
        LO = bcells.get()
        R1 = fcells.get()
        K.op(DVE, lambda e: e.tensor_copy(out=HI.t[:, 0:n], in_=Ft.t[:, 0:n]), r=[Ft.b], w=[HI.b])
        K.op(DVE, lambda e: e.tensor_tensor(out=R1.t[:, 0:n], in0=Ft.t[:, 0:n], in1=HI.t[:, 0:n], op=ALU.subtract), r=[Ft.b, HI.b], w=[R1.b])
        K.op(DVE, lambda e: e.tensor_copy(out=MID.t[:, 0:n], in_=R1.t[:, 0:n]), r=[R1.b], w=[MID.b])
        K.op(DVE, lambda e: e.tensor_tensor(out=R1.t[:, 0:n], in0=R1.t[:, 0:n], in1=MID.t[:, 0:n], op=ALU.subtract), r=[R1.b, MID.b], w=[R1.b])
        K.op(DVE, lambda e: e.tensor_copy(out=LO.t[:, 0:n], in_=R1.t[:, 0:n]), r=[R1.b], w=[LO.b])
        jb = pos // 128
        kbufs = KBb[jb: jb + (n + 127) // 128]
        T1 = R1
        sc = selc.t

        def combo(col0, out_ap, out_bufs):
            K.op(DVE, lambda e: e.tensor_scalar(out=T1.t[:, 0:n], in0=HI.t[:, 0:n], scalar1=sc[:, col0:col0 + 1], scalar2=sc[:, col0 + 3:col0 + 4], op0=ALU.mult, op1=ALU.add),
                 r=[HI.b, selc.b], w=[T1.b])
            K.op(DVE, lambda e: e.scalar_tensor_tensor(out=T1.t[:, 0:n], in0=MID.t[:, 0:n], scalar=sc[:, col0 + 1:col0 + 2], in1=T1.t[:, 0:n], op0=ALU.mult, op1=ALU.add),
                 r=[MID.b, selc.b, T1.b], w=[T1.b])
            K.op(DVE, lambda e: e.scalar_tensor_tensor(out=out_ap, in0=LO.t[:, 0:n], scalar=sc[:, col0 + 2:col0 + 3], in1=T1.t[:, 0:n], op0=ALU.mult, op1=ALU.add),
                 r=[LO.b, selc.b, T1.b], w=out_bufs)

        combo(0, KBt.t[:, pos:pos + n], kbufs)
        if qb_bufs is not None:
            QA = bcells.get()
            combo(4, QA.t[:, 0:n], [QA.b])
            for h in range(8):
                eng = DVE
                K.op(eng, lambda e, h=h: e.tensor_scalar(out=arena.t[:, 12 + h, 0:n], in0=QA.t[:, 0:n], scalar1=sc[:, 8 + h:9 + h], scalar2=None, op0=ALU.mult),
                     r=[QA.b, selc.b], w=[qb_bufs[h]])
            bcells.put(QA)
        bcells.put(HI)
        bcells.put(MID)
        bcells.put(LO)
        fcells.put(R1)

    blocks = []
    gi_ = 0
    for l in range(NL):
        for sq_ in seqs:
            for blk in range(sq_["T"] // sq_["TB"]):
                blocks.append((l, sq_, blk, gi_ * NPC))
                gi_ += 1
    for sq_ in seqs:
        sq_["ctx"] = {}
    for l in range(NL):
        for sq_ in seqs:
            sq_["ctx"][l] = seq_ctx(l, sq_)
    try:
        prologue(*blocks[0])
        for bi, (l, sq_, blk, g0) in enumerate(blocks):
            if blk == 0:
                seq_init(l, sq_)
            block_main(l, sq_, blk, g0, blocks[bi + 1] if bi + 1 < len(blocks) else None)
            if blk == sq_["T"] // sq_["TB"] - 1:
                seq_final(l, sq_)
    except StopEmit:
        pass
    K.finish()
    es.close()
    return nc


_IN_SIZES = (256, 256, 256, 256, 4, 4, 512, 512, 512, 8, 256, 256)


def _consts():
    c = np.zeros((128, NCON), np.float32)
    c[:, C_ID:C_ID + 128] = np.eye(128, dtype=np.float32)
    s = np.arange(128)[:, None]
    t = np.arange(128)[None, :]
    c[:, C_M01:C_M01 + 128] = (s <= t).astype(np.float32)
    for h in range(8):
        for r in range(6):
            p = 6 * h + r
            if r < 3:
                c[p, C_SEL + r] = -1.0
            else:
                c[p, C_SEL + 3] = 1.0
            if r < 3:
                c[p, C_SEL + 7] = 1.0
            else:
                c[p, C_SEL + 4 + (r - 3)] = 1.0
            c[p, C_SEL + 8 + h] = 1.0
    c[:, C_VAL:C_VAL + DEC] = 1.0
    tt = np.arange(512)[None, :]
    for d in range(4):
        c[:, C_MNEG + d * 512:C_MNEG + (d + 1) * 512] = np.where(tt >= 128 * d + s, 0.0, NEG).astype(np.float32)
    return c


def _layer_weights(w_in, w_o, w_gate, w_up, w_down):
    offs = np.cumsum((0,) + _IN_SIZES)
    col = {n: (offs[i], offs[i + 1]) for i, n in enumerate(["mq", "mk", "mv", "mo", "mi", "mf", "fq", "fk", "fv", "ff", "gu", "gv"])}
    W = np.zeros((128, FTOT_PAD), np.float32)

    def put_lhsT(off, mat):
        kk = mat.shape[0] // 128
        W[:, off:off + kk * 128] = mat.reshape(kk, 128, 128).transpose(1, 0, 2).reshape(128, kk * 128)

    def put_rhs(off, mat):
        n = mat.shape[1]
        W[:, off:off + 8 * n] = mat.reshape(8, 128, n).transpose(1, 0, 2).reshape(128, 8 * n)

    sl = lambda n: w_in[:, col[n][0]:col[n][1]]
    tiles = []
    for n in ("mq", "mk"):
        for j in range(2):
            tiles.append(sl(n)[:, j * 128:(j + 1) * 128])
    gff = np.zeros((1024, 128), np.float32)
    for h in range(8):
        for r in range(6):
            gff[:, 6 * h + r] = sl("ff")[:, h]
    gi = np.zeros((1024, 128), np.float32)
    gi[:, 0:4] = sl("mi")
    gf = np.zeros((1024, 128), np.float32)
    gf[:, 0:4] = sl("mf")
    tiles += [gff, gi, gf]
    for n in ("fq", "fk"):
        for j in range(4):
            tiles.append(sl(n)[:, j * 128:(j + 1) * 128])
    for i, tl in enumerate(tiles):
        put_lhsT(W1_OFF + i * 1024, tl)
    o = W2_OFF
    for grp in (np.concatenate([sl("mk"), sl("mv")], 1), sl("mo"), np.concatenate([sl("gu"), sl("gv")], 1), sl("fv")):
        put_rhs(o, grp)
        o += 8 * grp.shape[1]
    for oc in range(8):
        wo = w_o[:, oc * 128:(oc + 1) * 128]
        ch = [wo[0:128], wo[128:256]]
        for h in range(8):
            z = np.zeros((128, 128), np.float32)
            z[0:64] = wo[256 + 64 * h:256 + 64 * h + 64]
            ch.append(z)
        ch += [wo[768:896], wo[896:1024]]
        put_lhsT(WO_OFF + oc * 1536, np.concatenate(ch, 0))
    for pc in range(11):
        for q in range(2):
            fc_ = pc * 2 + q
            put_lhsT(GU_OFF + pc * 4096 + q * 1024, w_gate[:, fc_ * 128:(fc_ + 1) * 128])
            put_lhsT(GU_OFF + pc * 4096 + 2048 + q * 1024, w_up[:, fc_ * 128:(fc_ + 1) * 128])
    for oc in range(8):
        put_lhsT(WD_OFF + oc * 2816, w_down[:, oc * 128:(oc + 1) * 128])
    return W


def _layer_params(l, p):
    a = np.zeros((128, NLP), np.float32)
    a[:, LP_BADA:LP_BADA + 48] = p["b_ada"][l].reshape(48, 128).T
    for i, n in enumerate(("ln1_g", "ln1_b", "ln2_g", "ln2_b")):
        a[:, LP_LN + 8 * i:LP_LN + 8 * i + 8] = p[n][l].reshape(8, 128).T
    for h in range(8):
        a[6 * h:6 * h + 6, LP_BFF] = p["b_fox_f"][l][h]
    a[0:4, LP_BI] = p["b_mlstm_i"][l]
    a[0:4, LP_BF] = p["b_mlstm_f"][l]
    a[:, LP_BS:LP_BS + 4] = p["gmlp_bs"][l].T
    a[:, LP_GM:LP_GM + 256] = p["mlstm_norm_g"][l][None, :]
    a[:, LP_GG:LP_GG + 256] = p["gmlp_ln_g"][l][None, :]
    a[:, LP_GB:LP_GB + 256] = p["gmlp_ln_b"][l][None, :]
    a[:, LP_WS:LP_WS + 512] = p["gmlp_ws"][l].transpose(2, 0, 1).reshape(128, 512)
    return a


_NC_CACHE = {}


def run(inp, n_cores, NPS, NSS):
    f = lambda k: np.asarray(inp[k], np.float32)
    x_prompt = f("x_prompt")
    S = x_prompt.shape[1]
    key = (S, NPS, NSS)
    if key not in _NC_CACHE:
        _NC_CACHE[key] = build(S, NPS, NSS)
    nc = _NC_CACHE[key]
    p = {k: f(k) for k in inp}
    consts = _consts()
    lpar = np.stack([_layer_params(l, p) for l in range(NL)])
    wall = np.stack([_layer_weights(p["w_in"][l], p["w_o"][l], p["w_gate"][l], p["w_up"][l], p["w_down"][l]) for l in range(NL)])
    wada = np.ascontiguousarray(p["w_ada"].reshape(NL, 8, 128, 12, 512).transpose(0, 3, 2, 1, 4).reshape(NL, 12, 128, 4096))
    B = x_prompt.shape[0]
    xpT = np.ascontiguousarray(x_prompt.reshape(B, S, 8, 128).transpose(0, 2, 3, 1))
    xs_pad = np.zeros((p["x_sample"].shape[0], 128, 1024), np.float32)
    xs_pad[:, :DEC] = p["x_sample"]
    xsT = np.ascontiguousarray(xs_pad.reshape(-1, 128, 8, 128).transpose(0, 2, 3, 1))
    ckT = np.ascontiguousarray(p["cache_fox_k"].reshape(NL, -1, PAST, 4, 128).transpose(0, 1, 3, 4, 2))
    cvv = p["cache_fox_v"].reshape(NL, -1, PAST, 512)
    clfT = np.zeros((NL, cvv.shape[1], 128, PAST), np.float32)
    lfT = p["cache_fox_logf"].transpose(0, 1, 3, 2)
    for h in range(8):
        clfT[:, :, 6 * h:6 * h + 6, :] = lfT[:, :, h:h + 1, :]
    Bs = cvv.shape[1]
    sCn = np.concatenate([p["state_mlstm_C"], p["state_mlstm_n"][..., None]], -1).reshape(NL, Bs, 2, 128, 65)
    sMm = np.zeros((NL, Bs, 128, 1), np.float32)
    sMm[:, :, 0:4, 0] = p["state_mlstm_m"]
    in_maps = []
    for c in range(n_cores):
        ps_, ss_ = slice(c * NPS, (c + 1) * NPS), slice(c * NSS, (c + 1) * NSS)
        cc = np.concatenate([p["c_prompt"][ps_], p["c_sample"][ss_]], 0)
        cTt = np.ascontiguousarray(cc.reshape(-1, 8, 128).transpose(2, 1, 0).reshape(128, -1))
        in_maps.append(dict(
            xp=xpT[ps_], xs=xsT[ss_], cT=cTt,
            ck=np.ascontiguousarray(ckT[:, ss_]), cv=np.ascontiguousarray(cvv[:, ss_]), clf=np.ascontiguousarray(clfT[:, ss_]),
            sC=np.ascontiguousarray(sCn[:, ss_]), sM=np.ascontiguousarray(sMm[:, ss_]),
            consts=consts, lpar=lpar, wall=wall, wada=wada))
    res = run_bass_kernel_spmd(nc, in_maps, core_ids=list(range(n_cores)))
    R = res.results

    def cat(name, axis):
        return np.concatenate([r[name] for r in R], axis=axis)

    yp = cat("yp", 0)
    y_prompt = np.ascontiguousarray(yp.transpose(0, 3, 1, 2)).reshape(B, S, 1024)
    ys = cat("ys", 0)
    y_sample = np.ascontiguousarray(ys.transpose(0, 3, 1, 2)).reshape(Bs, 128, 1024)[:, :DEC]
    pk = cat("pk", 1)
    p_k = np.ascontiguousarray(pk.transpose(0, 1, 4, 2, 3)).reshape(NL, B, S, 8, 64)
    p_v = cat("pv", 1).reshape(NL, B, S, 8, 64)
    plf = cat("plf", 1)
    p_lf = np.ascontiguousarray(plf[:, :, 0:48:6, :].transpose(0, 1, 3, 2))
    pC = cat("pC", 1)
    pC4 = pC.reshape(NL, B, 4, 64, 65)
    p_C = np.ascontiguousarray(pC4[..., 0:64])
    p_n = np.ascontiguousarray(pC4[..., 64])
    p_m = np.ascontiguousarray(cat("pM", 1)[:, :, 0:4, 0])
    sk_ = cat("sk", 1)
    s_k = np.ascontiguousarray(sk_.transpose(0, 1, 4, 2, 3)).reshape(NL, Bs, 128, 8, 64)[:, :, :DEC]
    s_v = np.ascontiguousarray(cat("sv", 1).reshape(NL, Bs, 128, 8, 64)[:, :, :DEC])
    slf_ = cat("slf", 1)
    s_lf = np.ascontiguousarray(slf_[:, :, 0:48:6, :DEC].transpose(0, 1, 3, 2))
    sC_ = cat("sCo", 1).reshape(NL, Bs, 4, 64, 65)
    s_C = np.ascontiguousarray(sC_[..., 0:64])
    s_n = np.ascontiguousarray(sC_[..., 64])
    s_m = np.ascontiguousarray(cat("sMo", 1)[:, :, 0:4, 0])
    s_gv = np.ascontiguousarray(cat("sgv", 1)[:, :, :DEC, :])
    outs = (y_prompt, y_sample, p_k, p_v, p_lf, p_C, p_n, p_m, s_k, s_v, s_lf, s_C, s_n, s_m, s_gv)
    return tuple(np.ascontiguousarray(o, dtype=np.float32) for o in outs)


def kernel(**inputs):
    return run(inputs, 8, 2, 2)
```
